# Optimizing a Trainium2 kernel written in Bass

```python
import jax, jax.numpy as jnp
from jax import lax
import numpy as np

D_MODEL = 2048
BATCH = 4
SEQ = 4096
DEPTH = 1

N_ATT_HEADS = 8
HEAD_DIM = 128
ATT_WIDTH = N_ATT_HEADS * HEAD_DIM
MOBA_BLOCK = 256
MOBA_TOPK = 3
Q_CHUNK = 128
N_SGU_GROUPS = 8
SGU_GROUP_DIM = 128
SGU_WIDTH = N_SGU_GROUPS * SGU_GROUP_DIM
SGU_CHUNK = 128
IN_SPLITS = (ATT_WIDTH, ATT_WIDTH, ATT_WIDTH, ATT_WIDTH,
             SGU_WIDTH, SGU_WIDTH, SGU_WIDTH,
             D_MODEL, D_MODEL)
IN_WIDTH = 4 * ATT_WIDTH + 3 * SGU_WIDTH + 2 * D_MODEL
EPS = 1e-6
NEG_INF = -1e30

kernel_name = "hybrid_moba_gmlp_gated_layer"


def rms_norm(x, g):
    xf = x.astype(jnp.float32)
    y = xf * lax.rsqrt(jnp.mean(xf * xf, axis=-1, keepdims=True) + EPS)
    return (y * g.astype(jnp.float32)).astype(x.dtype)


def moba_attention(q, k, v):
    B, S, H, Dh = q.shape
    nb = -(-S // MOBA_BLOCK)
    s_pad = nb * MOBA_BLOCK
    topk = min(MOBA_TOPK, nb)
    nc = S // Q_CHUNK
    scale = Dh ** -0.5

    qh = q.transpose(0, 2, 1, 3)
    pad = ((0, 0), (0, 0), (0, s_pad - S), (0, 0))
    kh = jnp.pad(k.transpose(0, 2, 1, 3), pad)
    vh = jnp.pad(v.transpose(0, 2, 1, 3), pad)
    k_blocks = kh.reshape(B, H, nb, MOBA_BLOCK, Dh)
    v_blocks = vh.reshape(B, H, nb, MOBA_BLOCK, Dh)
    k_mean = jnp.mean(k_blocks.astype(jnp.float32), axis=3)
    q_chunks = qh.reshape(B, H, nc, Q_CHUNK, Dh).transpose(2, 0, 1, 3, 4)

    bi = jnp.arange(B)[:, None, None, None]
    hi = jnp.arange(H)[None, :, None, None]
    blk_ids = jnp.arange(nb)

    def one_chunk(args):
        qc, c = args
        q_pos = c * Q_CHUNK + jnp.arange(Q_CHUNK)
        cur = (c * Q_CHUNK) // MOBA_BLOCK
        gate = jnp.einsum('bhqd,bhnd->bhqn', qc.astype(jnp.float32), k_mean)
        gate = jnp.where(blk_ids < cur, gate, NEG_INF)
        _, idx = lax.top_k(gate, topk)
        valid = idx < cur
        k_sel = k_blocks[bi, hi, idx]
        v_sel = v_blocks[bi, hi, idx]
        s_past = jnp.einsum('bhqd,bhqknd->bhqkn', qc, k_sel,
                            preferred_element_type=jnp.float32) * scale
        s_past = jnp.where(valid[..., None], s_past, NEG_INF)
        k_own = lax.dynamic_index_in_dim(k_blocks, cur, axis=2, keepdims=False)
        v_own = lax.dynamic_index_in_dim(v_blocks, cur, axis=2, keepdims=False)
        s_own = jnp.einsum('bhqd,bhnd->bhqn', qc, k_own,
                           preferred_element_type=jnp.float32) * scale
        key_pos = cur * MOBA_BLOCK + jnp.arange(MOBA_BLOCK)
        s_own = jnp.where(key_pos[None, :] <= q_pos[:, None], s_own, NEG_INF)
        logits = jnp.concatenate(
            [s_past.reshape(B, H, Q_CHUNK, topk * MOBA_BLOCK), s_own], axis=-1)
        p = jax.nn.softmax(logits, axis=-1)
        p_past = p[..., :topk * MOBA_BLOCK].reshape(B, H, Q_CHUNK, topk, MOBA_BLOCK)
        p_own = p[..., topk * MOBA_BLOCK:]
        o = (jnp.einsum('bhqkn,bhqknd->bhqd', p_past.astype(v_sel.dtype), v_sel)
             + jnp.einsum('bhqn,bhnd->bhqd', p_own.astype(v_own.dtype), v_own))
        return o.astype(qc.dtype)

    out = lax.map(one_chunk, (q_chunks, jnp.arange(nc, dtype=jnp.int32)))
    return out.transpose(1, 0, 3, 2, 4).reshape(B, S, H * Dh)


def spatial_gating(u, v, v_norm_g, w_s, b_s):
    B, S, _ = u.shape
    n = S // SGU_CHUNK
    vg = rms_norm(v.reshape(B, S, N_SGU_GROUPS, SGU_GROUP_DIM), v_norm_g)
    vg = vg.reshape(B, n, SGU_CHUNK, N_SGU_GROUPS, SGU_GROUP_DIM)
    causal = jnp.tril(jnp.ones((SGU_CHUNK, SGU_CHUNK), dtype=bool))
    w = jnp.where(causal[None], w_s, jnp.zeros((), w_s.dtype))
    mixed = jnp.einsum('gts,bnsgc->bntgc', w, vg) + b_s.T[:, :, None]
    return u * mixed.reshape(B, S, SGU_WIDTH)


def hybrid_layer(x, norm_g, w_in, q_norm_g, k_norm_g, sgu_norm_g, w_s, b_s,
                 w_proj_a, w_proj_b, w_out):
    B, S, _ = x.shape
    h = rms_norm(x, norm_g)
    proj = h @ w_in
    cuts = [int(c) for c in np.cumsum(IN_SPLITS)[:-1]]
    q, k, v, z_a, u_b, v_b, z_b, g_a, g_b = jnp.split(proj, cuts, axis=-1)
    q = rms_norm(q.reshape(B, S, N_ATT_HEADS, HEAD_DIM), q_norm_g)
    k = rms_norm(k.reshape(B, S, N_ATT_HEADS, HEAD_DIM), k_norm_g)
    v = v.reshape(B, S, N_ATT_HEADS, HEAD_DIM)
    att = moba_attention(q, k, v)
    y_a = (att * jax.nn.silu(z_a)) @ w_proj_a
    sgu = spatial_gating(jax.nn.gelu(u_b), jax.nn.gelu(v_b), sgu_norm_g, w_s, b_s)
    y_b = (sgu * jax.nn.silu(z_b)) @ w_proj_b
    merged = jax.nn.sigmoid(g_a) * y_a + jax.nn.sigmoid(g_b) * y_b
    return x + merged @ w_out


def setup_inputs(seed: int = 0) -> dict:
    key = jax.random.key(seed)
    ks = jax.random.split(key, 12)
    f32 = jnp.float32
    x = jax.random.normal(ks[0], (BATCH, SEQ, D_MODEL), f32)
    norm_g = 1.0 + 0.02 * jax.random.normal(ks[1], (DEPTH, D_MODEL), f32)
    w_in = jax.random.normal(ks[2], (DEPTH, D_MODEL, IN_WIDTH), f32) * D_MODEL ** -0.5
    q_norm_g = 1.0 + 0.02 * jax.random.normal(ks[3], (DEPTH, HEAD_DIM), f32)
    k_norm_g = 1.0 + 0.02 * jax.random.normal(ks[4], (DEPTH, HEAD_DIM), f32)
    sgu_norm_g = 1.0 + 0.02 * jax.random.normal(ks[5], (DEPTH, SGU_GROUP_DIM), f32)
    w_spatial = jax.random.normal(ks[6], (DEPTH, N_SGU_GROUPS, SGU_CHUNK, SGU_CHUNK), f32) * SGU_CHUNK ** -0.5
    b_spatial = 1.0 + 0.1 * jax.random.normal(ks[7], (DEPTH, N_SGU_GROUPS, SGU_CHUNK), f32)
    w_proj_a = jax.random.normal(ks[8], (DEPTH, ATT_WIDTH, D_MODEL), f32) * ATT_WIDTH ** -0.5
    w_proj_b = jax.random.normal(ks[9], (DEPTH, SGU_WIDTH, D_MODEL), f32) * SGU_WIDTH ** -0.5
    w_out = jax.random.normal(ks[10], (DEPTH, D_MODEL, D_MODEL), f32) * D_MODEL ** -0.5
    return {"x": x, "norm_g": norm_g, "w_in": w_in, "q_norm_g": q_norm_g,
            "k_norm_g": k_norm_g, "sgu_norm_g": sgu_norm_g, "w_spatial": w_spatial,
            "b_spatial": b_spatial, "w_proj_a": w_proj_a, "w_proj_b": w_proj_b,
            "w_out": w_out}


def reference(x, norm_g, w_in, q_norm_g, k_norm_g, sgu_norm_g, w_spatial, b_spatial,
              w_proj_a, w_proj_b, w_out):
    for l in range(DEPTH):
        x = hybrid_layer(x, norm_g[l], w_in[l], q_norm_g[l], k_norm_g[l], sgu_norm_g[l],
                         w_spatial[l], b_spatial[l], w_proj_a[l], w_proj_b[l], w_out[l])
    return x
```

```python
import numpy as np
from contextlib import ExitStack
import concourse.bass as bass
import concourse.mybir as mybir
from concourse.bass_utils import run_bass_kernel_spmd

F32 = mybir.dt.float32
BF16 = mybir.dt.bfloat16
U8 = mybir.dt.uint8
ALU = mybir.AluOpType
AF = mybir.ActivationFunctionType
AX = mybir.AxisListType

D = 2048
NH = 8
DH = 128
S_OWN = 2048
NT = 16
EPS = 1e-6
C1 = 0.7978845608028654
C2 = 0.044715
SQ128 = float(np.sqrt(128.0))


class Buf:
    def __init__(self, name, inherit=()):
        self.name = name
        self.writers = {}
        self.readers = {}
        self.dsem = None
        self.dcount = 0
        for b in inherit:
            for d in (b.writers, b.readers):
                for k, v in d.items():
                    if self.readers.get(k, (None, 0))[1] < v[1]:
                        self.readers[k] = v


class _Rec:
    def __init__(self):
        self.call = None

    def __getattr__(self, name):
        def f(*args, **kwargs):
            self.call = (name, args, kwargs)
        return f


class Eng:
    def __init__(self, name, sem):
        self.name = name
        self.sem = sem
        self.n = 0
        self.ops = []
        self.seen = {}


class Prog:
    def __init__(self, nc, st):
        self.nc = nc
        self.st = st
        self.E = {}
        for n in ("pe", "act", "dve", "pool", "sp"):
            self.E[n] = Eng(n, st.enter_context(nc.semaphore("s_" + n)))
        self.nsem = 5

    def _waits(self, eng, r, w, is_dma=False):
        E = self.E[eng]
        need = {}

        def add(d, raw, skip_dma=False):
            for k, (sem, val) in d.items():
                if skip_dma and k.startswith("dma_"):
                    continue
                if k == eng:
                    if eng == "pe" or not raw:
                        continue
                if need.get(k, (None, 0))[1] < val:
                    need[k] = (sem, val)
        for b in r:
            add(b.writers, True)
        for b in w:
            add(b.readers, False)
            add(b.writers, False, skip_dma=is_dma)
        out = []
        for k, (sem, val) in need.items():
            if E.seen.get(k, 0) >= val:
                continue
            E.seen[k] = val
            out.append((sem, val))
        return out

    def _record(self, key, tok, r, w):
        for b in r:
            b.readers[key] = tok
        for b in w:
            if b.readers:
                b.readers = {}
                b.writers = {}
            b.writers[key] = tok

    def op(self, eng, fn, r=(), w=()):
        E = self.E[eng]
        waits = self._waits(eng, r, w)
        E.n += 1
        tok = (E.sem, E.n)
        rec = _Rec()
        fn(rec)
        name, args, kwargs = rec.call
        E.ops.append((waits, lambda e: getattr(e, name)(*args, **kwargs), E.sem, 1))
        self._record(eng, tok, r, w)

    def dma(self, eng, out, in_, r=(), w=(), dbuf=None):
        E = self.E[eng]
        b = dbuf if dbuf is not None else (w[0] if w else r[0])
        if b.dsem is None:
            b.dsem = self.st.enter_context(self.nc.semaphore("d_" + b.name))
            self.nsem += 1
        waits = self._waits(eng, r, w, is_dma=True)
        b.dcount += 16
        tok = (b.dsem, b.dcount)
        E.ops.append((waits, lambda e: e.dma_start(out=out, in_=in_), b.dsem, 16))
        self._record("dma_" + b.name, tok, r, w)

    def final_wait(self, eng, bufs):
        E = self.E[eng]
        for b in bufs:
            E.ops.append(([(b.dsem, b.dcount)], None, None, 0))

    def emit(self, e, name):
        for waits, fn, sem, inc in self.E[name].ops:
            for s, v in waits:
                e.wait_ge(s, v)
            if fn is None:
                continue
            ins = fn(e)
            ins.then_inc(sem, inc)


def build_program(debug=False):
    nc = bass.Bass("TRN2", target_bir_lowering=False)

    def din(name, shape):
        return nc.dram_tensor(name, list(shape), F32, kind="ExternalInput").ap()

    xo = din("xo", (S_OWN, D))
    xp = din("xp", (S_OWN, D))
    wqkvz = din("wqkvz", (NH, 128, 8192))
    wkvpre = din("wkvpre", (128, 32768))
    wsgu = din("wsgu", (8, 128, 6144))
    wph3 = din("wph3", (16, 128, 6144))
    wout = din("wout", (8, 128, 4096))
    wph3_b = nc.dram_tensor("wph3_b", [16, 128, 6144], BF16, kind="Internal").ap()
    wout_b = nc.dram_tensor("wout_b", [8, 128, 4096], BF16, kind="Internal").ap()
    vecs = din("vecs", (128, 20))
    gbias_d = din("gbias", (128, 256))
    bsp_d = din("bsp", (128, 1024))
    wspT_d = din("wspT", (128, 8, 128))
    ident_d = din("ident", (128, 128))
    tri_d = din("tri", (128, 128))
    out_d = nc.dram_tensor("out", [S_OWN, D], F32, kind="ExternalOutput").ap()
    dbg = {}
    if debug:
        for nm, shp in (("d_hT", (128, 16 * 2048)), ("d_AT", (128, 8 * 2048)), ("d_ST", (128, 8 * 2048)),
                        ("d_KTpre", (128, 8 * 2048)), ("d_QK", (128, 2 * 2048)), ("d_sel", (128, 256)),
                        ("d_gm", (128, 256)), ("d_V", (128, 16 * 129)), ("d_za", (128, 2048))):
            dbg[nm] = nc.dram_tensor(nm, list(shp), BF16 if nm not in ("d_sel", "d_gm") else F32,
                                     kind="ExternalOutput").ap()

    with ExitStack() as st:
        ARENA_BYTES = 212480
        arena = st.enter_context(nc.sbuf_tensor("arena", [128, ARENA_BYTES], U8))
        psum_all = st.enter_context(nc.psum_tensor("ps", [128, 4096], F32))
        P = Prog(nc, st)

        def carve(off, nbytes, dt):
            assert off % 32 == 0 and off + nbytes <= ARENA_BYTES, (off, nbytes)
            return arena[:, off:off + nbytes].bitcast(dt)

        OFF_A = 0
        OFF_B = 65536
        OFF_C = 98304
        OFF_D = 131584
        OFF_E = 164352
        OFF_F = 200192
        hT = carve(OFF_A, 65536, BF16).rearrange("p (k t) -> p k t", k=16)
        Wkv = carve(OFF_A, 65536, BF16).rearrange("p (h k c) -> p h k c", h=8, k=16)
        KTpre = carve(OFF_B, 32768, BF16).rearrange("p (h t) -> p h t", h=8)
        AT = KTpre
        Vpre = carve(OFF_C, 33024, BF16).rearrange("p (t h c) -> p t h c", t=16, h=8)
        ST = carve(OFF_C, 32768, BF16).rearrange("p (g t) -> p g t", g=8)

        o = OFF_F
        ident = carve(o, 256, BF16); o += 256
        tri = carve(o, 256, BF16); o += 256
        WT = carve(o, 2048, BF16).rearrange("p (g t) -> p g t", g=8); o += 2048
        Bsp = carve(o, 4096, F32).rearrange("p (g t) -> p g t", g=8); o += 4096
        gbias = carve(o, 1024, F32); o += 1024
        vec = carve(o, 96, F32); o += 96
        m05 = carve(o, 64, F32); o += 64
        stat = carve(o, 512, F32); o += 512
        km = carve(o, 64, F32); o += 64
        kmb = carve(o, 32, BF16); o += 32
        gm = carve(o, 1024, F32).rearrange("p (c b) -> p c b", c=16); o += 1024
        sel = carve(o, 1024, F32).rearrange("p (c b) -> p c b", c=16); o += 1024
        m8 = carve(o, 512, F32).rearrange("p (c b) -> p c b", c=16); o += 512
        thr = carve(o, 64, F32); o += 64
        assert o <= ARENA_BYTES, o

        DBG_BUFS = []

        def dbuf(n):
            b = Buf(n)
            DBG_BUFS.append(b)
            return b

        B_const = Buf("const")
        B_A = Buf("A")
        banks = [Buf(f"bank{i}") for i in range(8)]

        def bankf(i):
            return psum_all[:, i * 512:(i + 1) * 512]

        def bankb(i):
            return psum_all[:, i * 512:(i + 1) * 512].bitcast(BF16)

        B_constp = Buf("constp")
        P.dma("sp", vec[:, 0:20], vecs, w=[B_const])
        P.dma("sp", gbias, gbias_d, w=[B_const])
        P.dma("sp", Bsp.rearrange("p g t -> p (g t)"), bsp_d, w=[B_const])
        P.dma("pool", ident, ident_d, w=[B_constp])
        P.dma("pool", tri, tri_d, w=[B_constp])
        P.dma("pool", WT.rearrange("p g t -> p (g t)"), wspT_d.rearrange("p g t -> p (g t)"), w=[B_constp])
        B_c2 = Buf("const2")
        P.op("dve", lambda e: e.tensor_scalar(out=vec[:, 20:23], in0=vec[:, 16:19], scalar1=SQ128, scalar2=None,
                                              op0=ALU.mult), r=[B_const], w=[B_c2])
        P.op("pool", lambda e: e.memset(m05, -0.5), w=[B_c2])
        P.op("dve", lambda e: e.tensor_tensor(out=WT, in0=WT, in1=tri.unsqueeze(1).to_broadcast([128, 8, 128]),
                                              op=ALU.mult), r=[B_constp], w=[B_c2])
        CONST = [B_const, B_constp, B_c2]

        od = OFF_E
        xt = [carve(od + i * 8192, 8192, F32) for i in range(2)]; od += 16384
        xn = [carve(od + i * 4096, 4096, BF16) for i in range(2)]; od += 8192
        hTt = [carve(od + i * 4096, 4096, BF16).rearrange("p (k t) -> p k t", k=16) for i in range(2)]; od += 8192
        junk = carve(od, 256, BF16); od += 256
        assert od <= OFF_F
        kn = [carve(OFF_D + 28672 + i * 2048, 2048, BF16).rearrange("p (h c) -> p h c", h=8) for i in range(2)]
        B_xt = [Buf("xt0"), Buf("xt1")]
        B_xn = [Buf("xn0"), Buf("xn1")]; B_hTt = [Buf("hTt0"), Buf("hTt1")]; B_kn = [Buf("kn0"), Buf("kn1")]
        B_junk = Buf("junk")
        B_st0 = [Buf("st0a"), Buf("st0b")]; B_stk = [Buf(f"stk{i}") for i in range(8)]
        B_KTpre = [Buf(f"KTpre{h}") for h in range(8)]; B_Vpre = Buf("Vpre")

        P.op("pool", lambda e: e.memset(Vpre[:, :, :, 128:129], 1.0), w=[B_Vpre])
        for q4 in range(4):
            P.dma("pool", carve(OFF_A + q4 * 16384, 16384, BF16), wkvpre[:, q4 * 8192:(q4 + 1) * 8192], w=[B_A])
        B_hT = None

        def st_A(src_dram, t):
            xb = xt[t % 2]; bx = B_xt[t % 2]; xnb = xn[t % 2]; bxn = B_xn[t % 2]
            P.dma("sp", xb, src_dram[(t % 16) * 128:(t % 16 + 1) * 128, :], w=[bx])
            ss = stat[:, 2 * (t % 2):2 * (t % 2) + 1]; rs = stat[:, 2 * (t % 2) + 1:2 * (t % 2) + 2]
            bst = B_st0[t % 2]
            P.op("act", lambda e: e.activation(out=xnb, in_=xb, func=AF.Square, accum_out=ss), r=[bx], w=[bxn, bst])
            P.op("dve", lambda e: e.tensor_scalar(out=ss, in0=ss, scalar1=1.0 / D, scalar2=EPS, op0=ALU.mult,
                                                  op1=ALU.add), r=[bst], w=[bst])
            P.op("pool", lambda e: e.tensor_tensor(out=rs, in0=ss, in1=m05[:, 0:1], op=ALU.pow),
                 r=[bst] + CONST, w=[bst])
            P.op("dve", lambda e: e.tensor_scalar(out=xnb, in0=xb, scalar1=rs, scalar2=None, op0=ALU.mult),
                 r=[bx, bst], w=[bxn])

        def st_B(t, dst_ap, dst_buf):
            xnb = xn[t % 2]; bxn = B_xn[t % 2]
            for half in range(2):
                bk = half
                for j in range(8):
                    kt = half * 8 + j
                    P.op("pe", lambda e: e.transpose(out=bankb(bk)[:, j * 128:(j + 1) * 128],
                                                     in_=xnb[:, kt * 128:(kt + 1) * 128], identity=ident),
                         r=[bxn] + CONST, w=[banks[bk]])
                P.op("dve", lambda e: e.tensor_tensor(
                    out=dst_ap[:, half * 8:(half + 1) * 8, :], in0=bankb(bk).rearrange("p (k t) -> p k t", k=8),
                    in1=vec[:, half * 8:(half + 1) * 8].unsqueeze(2).to_broadcast([128, 8, 128]), op=ALU.mult),
                    r=[banks[bk]] + CONST, w=[dst_buf])

        def st_CD(t):
            hb = hTt[t % 2]; bhb = B_hTt[t % 2]; knb = kn[t % 2]; bkn = B_kn[t % 2]
            for b2 in range(4):
                bk = 2 + b2
                for hh in range(2):
                    h = 2 * b2 + hh
                    for kt in range(16):
                        P.op("pe", lambda e: e.matmul(bankf(bk)[:, hh * 256:hh * 256 + 256], lhsT=hb[:, kt, :],
                                                      rhs=Wkv[:, h, kt, :], start=(kt == 0), stop=(kt == 15)),
                             r=[bhb, B_A], w=[banks[bk]])
                so = 8 + ((t % 2) * 4 + b2) * 4
                ssk = stat[:, so:so + 2]; rsk = stat[:, so + 2:so + 4]; bst = B_stk[(t % 2) * 4 + b2]
                pv = bankf(bk).rearrange("p (h c) -> p h c", h=2)
                for hh in range(2):
                    P.op("act", lambda e: e.activation(out=junk, in_=pv[:, hh, 0:128], func=AF.Square,
                                                       accum_out=ssk[:, hh:hh + 1]), r=[banks[bk]], w=[B_junk, bst])
                P.op("dve", lambda e: e.tensor_scalar(out=ssk, in0=ssk, scalar1=128.0 * EPS, scalar2=None,
                                                      op0=ALU.add), r=[bst], w=[bst])
                P.op("pool", lambda e: e.tensor_tensor(out=rsk, in0=ssk, in1=m05[:, 0:2], op=ALU.pow),
                     r=[bst] + CONST, w=[bst])
                P.op("dve", lambda e: e.tensor_tensor(
                    out=knb[:, 2 * b2:2 * b2 + 2, :], in0=pv[:, :, 0:128],
                    in1=rsk.unsqueeze(2).to_broadcast([128, 2, 128]), op=ALU.mult), r=[banks[bk], bst], w=[bkn])
                P.op("act", lambda e: e.copy(out=Vpre[:, t, 2 * b2:2 * b2 + 2, 0:128], in_=pv[:, :, 128:256]),
                     r=[banks[bk]], w=[B_Vpre])

        def st_E(t):
            knb = kn[t % 2]; bkn = B_kn[t % 2]
            for h in range(8):
                P.op("pe", lambda e: e.transpose(out=bankb(6)[:, h * 128:(h + 1) * 128], in_=knb[:, h, :],
                                                 identity=ident), r=[bkn] + CONST, w=[banks[6]])
            P.op("dve", lambda e: e.tensor_scalar(
                out=KTpre[:, :, t * 128:(t + 1) * 128], in0=bankb(6).rearrange("p (h t) -> p h t", h=8),
                scalar1=vec[:, 21:22], scalar2=None, op0=ALU.mult), r=[banks[6]] + CONST, w=B_KTpre)

        st_A(xp, 0)
        st_B(0, hTt[0], B_hTt[0])
        st_A(xp, 1)
        for t in range(NT):
            if t + 1 < NT:
                st_B(t + 1, hTt[(t + 1) % 2], B_hTt[(t + 1) % 2])
            if t + 2 < NT:
                st_A(xp, t + 2)
            elif t + 2 == NT:
                st_A(xo, 16)
            st_CD(t)
            if t >= 1:
                st_E(t - 1)
        st_E(NT - 1)
        B_hT = Buf("hT", inherit=[B_A])
        for t in range(16, 32):
            if t + 1 < 32:
                st_A(xo, t + 1)
            st_B(t, hT[:, :, (t - 16) * 128:(t - 15) * 128], B_hT)

        B_pre3 = [Buf(f"pre3_{i}") for i in range(16)]; B_preo = [Buf(f"preo_{i}") for i in range(8)]
        wslot = [carve(OFF_D + i * 16384, 16384, BF16).rearrange("p (k f c) -> p k f c", k=16, f=4) for i in range(2)]
        B_ws = [Buf("ws0"), Buf("ws1", inherit=B_kn)]
        oe = OFF_E
        B_ph0 = B_xt + B_xn + B_hTt + B_kn + [B_junk]
        QK = carve(oe, 8192, BF16).rearrange("p (f t) -> p f t", f=2); oe += 8192
        Vown = carve(oe, 4160, BF16)[:, 0:16 * 129].rearrange("p (t c) -> p t c", t=16); oe += 4160
        zaT = carve(oe, 4096, BF16); oe += 4096
        PT = [carve(oe + i * 2048, 2048, BF16).rearrange("p (j q) -> p j q", j=2) for i in range(3)]; oe += 6144
        acc = [carve(oe + i * 2080, 2064, F32).rearrange("p (c d) -> p c d", c=4) for i in range(2)]; oe += 4160
        qkn = [carve(oe + i * 512, 512, BF16).rearrange("p (f c) -> p f c", f=2) for i in range(3)]; oe += 1536
        attn = carve(oe, 1024, BF16).rearrange("p (c d) -> p c d", c=4); oe += 1024
        th = carve(oe, 2048, F32); oe += 2048
        rinv = carve(oe, 32, F32); oe += 32
        junk1 = carve(oe, 256, BF16); oe += 256
        assert oe <= OFF_E + 35840, oe
        B_QK = Buf("QK", inherit=B_ph0); B_Vown = Buf("Vown", inherit=B_ph0); B_zaT = Buf("zaT", inherit=B_ph0)
        B_PT = [Buf(f"PT{i}", inherit=B_ph0) for i in range(3)]
        B_acc = [Buf(f"acc{i}", inherit=B_ph0) for i in range(2)]
        B_qkn = [Buf(f"qkn{i}", inherit=B_ph0) for i in range(3)]
        B_st2 = [Buf(f"st2_{i}") for i in range(3)]
        B_attn = Buf("attn", inherit=B_ph0); B_th = Buf("th", inherit=B_ph0); B_rinv = Buf("rinv", inherit=B_ph0)
        B_junk1 = Buf("junk1", inherit=B_ph0)
        B_gate = Buf("gate")
        B_AT = B_KTpre

        P.op("pool", lambda e: e.memset(Vown[:, :, 128:129], 1.0), w=[B_Vown])

        def load_head_w(h):
            s = h % 2
            P.dma("pool", carve(OFF_D + s * 16384, 16384, BF16), wqkvz[h], w=[B_ws[s]])
            for ct in (2 * h, 2 * h + 1):
                P.dma("pool", wph3_b[ct], wph3[ct], w=[B_pre3[ct]])
            P.dma("pool", wout_b[h], wout[h], w=[B_preo[h]])

        load_head_w(0)
        pt_rr = 0
        for h in range(NH):
            if h + 1 < NH:
                load_head_w(h + 1)
            ws = wslot[h % 2]; bws = B_ws[h % 2]
            def st_C(t):
                bk = t % 3
                for kt in range(16):
                    P.op("pe", lambda e: e.matmul(
                        bankf(bk)[:, 0:384], lhsT=hT[:, kt, t * 128:(t + 1) * 128],
                        rhs=ws[:, kt, 0:3, :].rearrange("p f c -> p (f c)"), start=(kt == 0), stop=(kt == 15)),
                        r=[B_hT, bws], w=[banks[bk]])
                so = 72 + (t % 3) * 4
                ss2 = stat[:, so:so + 2]; rs2 = stat[:, so + 2:so + 4]; bst = B_st2[t % 3]
                for f in range(2):
                    P.op("act", lambda e: e.activation(out=junk1, in_=bankf(bk)[:, f * 128:(f + 1) * 128],
                                                       func=AF.Square, accum_out=ss2[:, f:f + 1]),
                         r=[banks[bk]], w=[B_junk1, bst])
                P.op("dve", lambda e: e.tensor_scalar(out=ss2, in0=ss2, scalar1=128.0 * EPS, scalar2=None,
                                                      op0=ALU.add), r=[bst], w=[bst])
                P.op("pool", lambda e: e.tensor_tensor(out=rs2, in0=ss2, in1=m05[:, 0:2], op=ALU.pow),
                     r=[bst] + CONST, w=[bst])
                qn = qkn[t % 3]; bqn = B_qkn[t % 3]
                P.op("dve", lambda e: e.tensor_tensor(
                    out=qn, in0=bankf(bk)[:, 0:256].rearrange("p (f c) -> p f c", f=2),
                    in1=rs2.unsqueeze(2).to_broadcast([128, 2, 128]), op=ALU.mult), r=[banks[bk], bst], w=[bqn])
                P.op("act", lambda e: e.copy(out=Vown[:, t, 0:128], in_=bankf(bk)[:, 256:384]),
                     r=[banks[bk]], w=[B_Vown])

            def st_Eq(t):
                qn = qkn[t % 3]; bqn = B_qkn[t % 3]
                for f in range(2):
                    P.op("pe", lambda e: e.transpose(out=bankb(3)[:, f * 128:(f + 1) * 128], in_=qn[:, f, :],
                                                     identity=ident), r=[bqn] + CONST, w=[banks[3]])
                P.op("dve", lambda e: e.tensor_tensor(
                    out=QK[:, :, t * 128:(t + 1) * 128], in0=bankb(3)[:, 0:256].rearrange("p (f c) -> p f c", f=2),
                    in1=vec[:, 20:22].unsqueeze(2).to_broadcast([128, 2, 128]), op=ALU.mult),
                    r=[banks[3]] + CONST, w=[B_QK])

            def st_za(G):
                bk = 4 + G % 2
                for kt in range(16):
                    P.op("pe", lambda e: e.matmul(bankf(bk), lhsT=ws[:, kt, 3, :], rhs=hT[:, kt, G * 512:(G + 1) * 512],
                                                  start=(kt == 0), stop=(kt == 15)), r=[B_hT, bws], w=[banks[bk]])
                P.op("act", lambda e: e.activation(out=th, in_=bankf(bk), func=AF.Tanh, scale=0.5),
                     r=[banks[bk]], w=[B_th])
                P.op("dve", lambda e: e.scalar_tensor_tensor(
                    out=zaT[:, G * 512:(G + 1) * 512], in0=th, scalar=1.0, in1=bankf(bk), op0=ALU.add, op1=ALU.mult),
                    r=[B_th, banks[bk]], w=[B_zaT])

            for t in range(NT):
                st_C(t)
                if t >= 2:
                    st_Eq(t - 2)
                if t % 4 == 3:
                    st_za(t // 4)
            st_Eq(NT - 2)
            st_Eq(NT - 1)
            P.op("dve", lambda e, h=h: e.tensor_reduce(out=km[:, 0:8], in_=KTpre[:, h, :].rearrange("p (b k) -> p b k", b=8),
                                                       axis=AX.X, op=ALU.add), r=[B_KTpre[h]], w=[B_gate])
            P.op("dve", lambda e: e.tensor_reduce(out=km[:, 8:16], in_=QK[:, 1, :].rearrange("p (b k) -> p b k", b=8),
                                                  axis=AX.X, op=ALU.add), r=[B_QK], w=[B_gate])
            P.op("dve", lambda e: e.tensor_scalar(out=kmb, in0=km, scalar1=1.0 / 256.0, scalar2=None, op0=ALU.mult),
                 r=[B_gate], w=[B_gate])
            for c in range(16):
                P.op("pe", lambda e, c=c: e.matmul(bankf(6)[:, c * 16:(c + 1) * 16], lhsT=QK[:, 0, c * 128:(c + 1) * 128],
                                                   rhs=kmb, start=True, stop=True), r=[B_QK, B_gate], w=[banks[6]])
            P.op("dve", lambda e: e.tensor_tensor(out=gm.rearrange("p c b -> p (c b)"), in0=bankf(6)[:, 0:256],
                                                  in1=gbias, op=ALU.add), r=[banks[6]] + CONST, w=[B_gate])
            for c in range(16):
                P.op("dve", lambda e, c=c: e.max(out=m8[:, c, :], in_=gm[:, c, :]), r=[B_gate], w=[B_gate])
            P.op("dve", lambda e: e.tensor_scalar(out=thr, in0=m8[:, :, 2], scalar1=-1e29, scalar2=None, op0=ALU.max),
                 r=[B_gate], w=[B_gate])
            P.op("dve", lambda e: e.tensor_tensor(out=sel, in0=gm, in1=thr.unsqueeze(2).to_broadcast([128, 16, 16]),
                                                  op=ALU.is_ge), r=[B_gate], w=[B_gate])
            if debug and h == 0:
                P.dma("sp", dbg["d_QK"], QK.rearrange("p f t -> p (f t)"), r=[B_QK], dbuf=dbuf("dq"))
                P.dma("sp", dbg["d_sel"], sel.rearrange("p c b -> p (c b)"), r=[B_gate], dbuf=dbuf("ds"))
                P.dma("sp", dbg["d_gm"], gm.rearrange("p c b -> p (c b)"), r=[B_gate], dbuf=dbuf("dg"))
                P.dma("sp", dbg["d_V"], Vown.rearrange("p t c -> p (t c)"), r=[B_Vown], dbuf=dbuf("dv"))

            def ktile(j):
                return KTpre[:, h, j * 128:(j + 1) * 128] if j < 16 else QK[:, 1, (j - 16) * 128:(j - 15) * 128]

            def vtile(j):
                return Vpre[:, j, h, :] if j < 16 else Vown[:, j - 16, :]

            for m in range(4):
                ac = acc[m % 2]; bac = B_acc[m % 2]
                steps = []
                for b in range(8 + 2 * m):
                    steps.append(dict(tiles=[(2 * b, 0), (2 * b + 1, 0)],
                                      chunks={c: [0, 1] for c in range(4)}, blk={c: b for c in range(4)}, diag=[]))
                r0 = 16 + 4 * m
                steps.append(dict(tiles=[(r0, 0), (r0 + 1, 1)],
                                  chunks={0: [0], 1: [0, 1], 2: [0, 1], 3: [0, 1]},
                                  blk={0: None, 1: None, 2: 8 + 2 * m, 3: 8 + 2 * m}, diag=[(0, 0), (1, 1)]))
                steps.append(dict(tiles=[(r0 + 2, 2), (r0 + 3, 3)],
                                  chunks={2: [0], 3: [0, 1]}, blk={2: None, 3: None}, diag=[(0, 2), (1, 3)]))
                first_for_chunk = {}

                def emit_qk(si, sp_):
                    pt = PT[sp_["pt"]]; bpt = B_PT[sp_["pt"]]
                    sb = sp_["sb"]
                    for idx, (j, c0) in enumerate(sp_["tiles"]):
                        bk = sb + idx
                        n = (4 - c0) * 128
                        P.op("pe", lambda e: e.matmul(
                            bankf(bk)[:, 0:n], lhsT=ktile(j), rhs=QK[:, 0, m * 512 + c0 * 128:(m + 1) * 512],
                            start=True, stop=True), r=[B_KTpre[h], B_QK], w=[banks[bk]])
                    if all(c0 == 0 for (_, c0) in sp_["tiles"]):
                        P.op("act", lambda e: e.activation(
                            out=pt, in_=psum_all[:, sb * 512:(sb + 2) * 512].rearrange("p (j q) -> p j q", j=2),
                            func=AF.Exp, scale=float(DH ** -0.5)), r=[banks[sb], banks[sb + 1]], w=[bpt])
                    else:
                        for idx, (j, c0) in enumerate(sp_["tiles"]):
                            bk = sb + idx
                            n = (4 - c0) * 128
                            P.op("act", lambda e: e.activation(
                                out=pt[:, idx, c0 * 128:512], in_=bankf(bk)[:, 0:n], func=AF.Exp,
                                scale=float(DH ** -0.5)), r=[banks[bk]], w=[bpt])
                    for (idx, c) in sp_["diag"]:
                        P.op("pool", lambda e: e.tensor_tensor(
                            out=pt[:, idx, c * 128:(c + 1) * 128], in0=pt[:, idx, c * 128:(c + 1) * 128], in1=tri,
                            op=ALU.mult), r=[bpt] + CONST, w=[bpt])

                def emit_pv(si, sp_):
                    pt = PT[sp_["pt"]]; bpt = B_PT[sp_["pt"]]
                    for c, tl in sp_["chunks"].items():
                        bk = sp_["ob"] + c // 2
                        po = bankf(bk)[:, (c % 2) * 129:(c % 2) * 129 + 129]
                        for ii, idx in enumerate(tl):
                            j = sp_["tiles"][idx][0]
                            P.op("pe", lambda e, c=c, idx=idx, j=j, po=po, ii=ii, tl=tl, pt=pt: e.matmul(
                                po, lhsT=pt[:, idx, c * 128:(c + 1) * 128], rhs=vtile(j),
                                start=(ii == 0), stop=(ii == len(tl) - 1)),
                                r=[bpt, B_Vpre, B_Vown], w=[banks[bk]])
                    for c, tl in sp_["chunks"].items():
                        bk = sp_["ob"] + c // 2
                        po = bankf(bk)[:, (c % 2) * 129:(c % 2) * 129 + 129]
                        b = sp_["blk"][c]
                        sc = 1.0 if b is None else sel[:, 4 * m + c, b:b + 1]
                        if c not in first_for_chunk:
                            first_for_chunk[c] = True
                            P.op("dve", lambda e, c=c, po=po, sc=sc: e.tensor_scalar(
                                out=ac[:, c, :], in0=po, scalar1=sc, scalar2=None, op0=ALU.mult),
                                r=[banks[bk], B_gate], w=[bac])
                        else:
                            P.op("dve", lambda e, c=c, po=po, sc=sc: e.scalar_tensor_tensor(
                                out=ac[:, c, :], in0=po, scalar=sc, in1=ac[:, c, :], op0=ALU.mult, op1=ALU.add),
                                r=[banks[bk], B_gate, bac], w=[bac])

                for si, sp_ in enumerate(steps):
                    sp_["pt"] = (pt_rr + si) % 3
                    sp_["sb"] = 0 + 2 * (si % 2)
                    sp_["ob"] = 4 + 2 * (si % 2)
                pt_rr += len(steps)
                emit_qk(0, steps[0])
                for si, sp_ in enumerate(steps):
                    if si + 1 < len(steps):
                        emit_qk(si + 1, steps[si + 1])
                    emit_pv(si, sp_)
                P.op("dve", lambda e: e.tensor_scalar(out=rinv[:, 0:4], in0=ac[:, :, 128], scalar1=2.0, scalar2=None,
                                                      op0=ALU.mult), r=[bac], w=[B_rinv])
                P.op("dve", lambda e: e.reciprocal(out=rinv[:, 4:8], in_=rinv[:, 0:4]), r=[B_rinv], w=[B_rinv])
                P.op("dve", lambda e: e.tensor_tensor(out=attn, in0=ac[:, :, 0:128],
                                                      in1=rinv[:, 4:8].unsqueeze(2).to_broadcast([128, 4, 128]),
                                                      op=ALU.mult), r=[bac, B_rinv], w=[B_attn])
                for c in range(4):
                    P.op("pe", lambda e, c=c: e.transpose(out=bankb(7)[:, c * 128:(c + 1) * 128], in_=attn[:, c, :],
                                                          identity=ident), r=[B_attn] + CONST, w=[banks[7]])
                P.op("dve", lambda e, m=m: e.tensor_tensor(out=zaT[:, m * 512:(m + 1) * 512], in0=bankb(7)[:, 0:512],
                                                           in1=zaT[:, m * 512:(m + 1) * 512], op=ALU.mult),
                     r=[banks[7], B_zaT], w=[B_zaT])
            P.op("pool", lambda e, h=h: e.tensor_copy(out=AT[:, h, :], in_=zaT), r=[B_zaT], w=[B_KTpre[h]])

        if debug:
            P.dma("sp", dbg["d_hT"], hT.rearrange("p k t -> p (k t)"), r=[B_hT], dbuf=dbuf("dh"))
            P.dma("sp", dbg["d_AT"], AT.rearrange("p h t -> p (h t)"), r=B_AT, dbuf=dbuf("da"))

        B_ph1 = [B_QK, B_Vown, B_zaT] + B_PT + B_acc + B_qkn + [B_attn, B_th, B_rinv, B_junk1]
        wsl2 = [carve(OFF_D + i * 16384, 12288, BF16).rearrange("p (k f c) -> p k f c", k=16, f=3) for i in range(2)]
        B_ws2 = [Buf("ws2_0", inherit=[B_ws[0]]), Buf("ws2_1", inherit=[B_ws[1]])]
        oe = OFF_E
        a1 = [carve(oe + i * 2048, 2048, F32) for i in range(2)]; oe += 4096
        a2 = [carve(oe + i * 2048, 2048, F32) for i in range(2)]; oe += 4096
        a3 = carve(oe, 2048, F32); oe += 2048
        ug2 = carve(oe, 1024, BF16); oe += 1024
        zs2 = carve(oe, 1024, BF16); oe += 1024
        uz = [carve(oe + i * 1024, 1024, BF16) for i in range(2)]; oe += 2048
        vg2 = carve(oe, 2048, F32); oe += 2048
        vn = [carve(oe + i * 1024, 1024, BF16) for i in range(2)]; oe += 2048
        s1 = carve(oe, 2048, F32); oe += 2048
        junk2 = carve(oe, 256, BF16); oe += 256
        assert oe <= OFF_E + 35840
        B_a1 = [Buf(f"a1_{i}", inherit=B_ph1) for i in range(2)]
        B_a2 = [Buf(f"a2_{i}", inherit=B_ph1) for i in range(2)]
        B_a3 = Buf("a3", inherit=B_ph1); B_ug2 = Buf("ug2", inherit=B_ph1); B_zs2 = Buf("zs2", inherit=B_ph1)
        B_uz = [Buf(f"uz{i}", inherit=B_ph1) for i in range(2)]; B_vg2 = Buf("vg2", inherit=B_ph1)
        B_vn = [Buf(f"vn{i}", inherit=B_ph1) for i in range(2)]
        B_s1 = Buf("s1", inherit=B_ph1); B_junk2 = Buf("junk2", inherit=B_ph1)
        B_ST = Buf("ST", inherit=[B_Vpre])
        B_stv = [Buf("stva"), Buf("stvb")]

        def gelu2(bk, i, dst, dst_buf):
            P.op("act", lambda e: e.activation(out=a1[i], in_=bankf(bk), func=AF.Square, scale=float(np.sqrt(C2))),
                 r=[banks[bk]], w=[B_a1[i]])
            P.op("dve", lambda e: e.scalar_tensor_tensor(out=a2[i], in0=a1[i], scalar=1.0, in1=bankf(bk),
                                                         op0=ALU.add, op1=ALU.mult),
                 r=[B_a1[i], banks[bk]], w=[B_a2[i]])
            P.op("act", lambda e: e.activation(out=a1[i], in_=a2[i], func=AF.Tanh, scale=C1),
                 r=[B_a2[i]], w=[B_a1[i]])
            P.op("dve", lambda e: e.scalar_tensor_tensor(out=dst, in0=a1[i], scalar=1.0, in1=bankf(bk),
                                                         op0=ALU.add, op1=ALU.mult),
                 r=[B_a1[i], banks[bk]], w=[dst_buf])

        def load_sgu_w(g):
            P.dma("pool", carve(OFF_D + (g % 2) * 16384, 12288, BF16), wsgu[g], w=[B_ws2[g % 2]])

        def st2_C(g, G):
            ws = wsl2[g % 2]; bws = B_ws2[g % 2]
            par = G % 2
            bu, bz, bv = 0 + par, 2 + par, 4 + par
            for kt in range(16):
                P.op("pe", lambda e: e.matmul(bankf(bu), lhsT=ws[:, kt, 0, :], rhs=hT[:, kt, G * 512:(G + 1) * 512],
                                              start=(kt == 0), stop=(kt == 15)), r=[B_hT, bws], w=[banks[bu]])
            for kt in range(16):
                P.op("pe", lambda e: e.matmul(bankf(bz), lhsT=ws[:, kt, 2, :], rhs=hT[:, kt, G * 512:(G + 1) * 512],
                                              start=(kt == 0), stop=(kt == 15)), r=[B_hT, bws], w=[banks[bz]])
            for tt in range(4):
                t = G * 4 + tt
                for kt in range(16):
                    P.op("pe", lambda e: e.matmul(bankf(bv)[:, tt * 128:(tt + 1) * 128],
                                                  lhsT=hT[:, kt, t * 128:(t + 1) * 128], rhs=ws[:, kt, 1, :],
                                                  start=(kt == 0), stop=(kt == 15)), r=[B_hT, bws], w=[banks[bv]])

        def st2_D(g, G):
            par = G % 2
            bu, bz, bv = 0 + par, 2 + par, 4 + par
            gelu2(bu, 0, ug2, B_ug2)
            P.op("act", lambda e: e.activation(out=a3, in_=bankf(bz), func=AF.Tanh, scale=0.5),
                 r=[banks[bz]], w=[B_a3])
            P.op("dve", lambda e: e.scalar_tensor_tensor(out=zs2, in0=a3, scalar=1.0, in1=bankf(bz),
                                                         op0=ALU.add, op1=ALU.mult), r=[B_a3, banks[bz]], w=[B_zs2])
            P.op("dve", lambda e: e.scalar_tensor_tensor(out=uz[par], in0=ug2, scalar=0.25, in1=zs2,
                                                         op0=ALU.mult, op1=ALU.mult), r=[B_ug2, B_zs2], w=[B_uz[par]])
            gelu2(bv, 1, vg2, B_vg2)
            ssv = stat[:, 48 + 8 * par:52 + 8 * par]; rsv = stat[:, 52 + 8 * par:56 + 8 * par]
            bst = B_stv[par]
            for tt in range(4):
                P.op("act", lambda e: e.activation(out=junk2, in_=vg2[:, tt * 128:(tt + 1) * 128],
                                                   func=AF.Square, accum_out=ssv[:, tt:tt + 1]),
                     r=[B_vg2], w=[B_junk2, bst])
            P.op("dve", lambda e: e.tensor_scalar(out=ssv, in0=ssv, scalar1=512.0 * EPS, scalar2=None, op0=ALU.add),
                 r=[bst], w=[bst])
            P.op("pool", lambda e: e.tensor_tensor(out=rsv, in0=ssv, in1=m05[:, 0:4], op=ALU.pow),
                 r=[bst] + CONST, w=[bst])
            P.op("dve", lambda e: e.tensor_tensor(out=vn[par].rearrange("p (t c) -> p t c", t=4),
                                                  in0=vg2.rearrange("p (t c) -> p t c", t=4),
                                                  in1=rsv.unsqueeze(2).to_broadcast([128, 4, 128]), op=ALU.mult),
                 r=[B_vg2, bst], w=[B_vn[par]])

        def st2_M(g, G):
            par = G % 2
            bm = 6 + par
            for tt in range(4):
                P.op("pe", lambda e: e.matmul(bankf(bm)[:, tt * 128:(tt + 1) * 128],
                                              lhsT=vn[par][:, tt * 128:(tt + 1) * 128], rhs=WT[:, g, :],
                                              start=True, stop=True), r=[B_vn[par]] + CONST, w=[banks[bm]])
            P.op("dve", lambda e: e.scalar_tensor_tensor(
                out=s1.rearrange("p (t c) -> p t c", t=4), in0=bankf(bm).rearrange("p (t c) -> p t c", t=4),
                scalar=vec[:, 22:23], in1=Bsp[:, g, :].unsqueeze(1).to_broadcast([128, 4, 128]),
                op0=ALU.mult, op1=ALU.add), r=[banks[bm]] + CONST, w=[B_s1])
            P.op("dve", lambda e: e.tensor_tensor(out=ST[:, g, G * 512:(G + 1) * 512], in0=s1, in1=uz[par],
                                                  op=ALU.mult), r=[B_s1, B_uz[par]], w=[B_ST])

        load_sgu_w(0)
        items = [(g, G) for g in range(8) for G in range(4)]
        for i, (g, G) in enumerate(items):
            if G == 0 and g + 1 < 8:
                load_sgu_w(g + 1)
            st2_C(g, G)
            st2_D(g, G)
            if i >= 1:
                st2_M(*items[i - 1])
        st2_M(*items[-1])
        if debug:
            P.dma("sp", dbg["d_ST"], ST.rearrange("p g t -> p (g t)"), r=[B_ST], dbuf=dbuf("dst"))

        B_ph2 = B_a1 + B_a2 + [B_a3, B_ug2, B_zs2, B_vg2, B_s1, B_junk2] + B_uz + B_vn
        od = OFF_D
        gsl = []
        psl = []
        sl3 = []
        for i in range(2):
            sl3.append(carve(od, 12288, BF16))
            gsl.append(carve(od, 8192, BF16).rearrange("p (k f c) -> p k f c", k=16, f=2)); od += 8192
            psl.append(carve(od, 4096, BF16).rearrange("p (k f c) -> p k f c", k=8, f=2)); od += 4096
        wosl = []
        for i in range(2):
            wosl.append(carve(od, 8192, BF16).rearrange("p (k c) -> p k c", k=16)); od += 8192
        mT = carve(od, 16384, BF16).rearrange("p (k t) -> p k t", k=16); od += 16384
        xr = []
        for i in range(2):
            xr.append(carve(od, 2048, F32)); od += 2048
        tha = carve(od, 2048, F32); od += 2048
        thb = carve(od, 2048, F32); od += 2048
        t1 = carve(od, 1024, BF16); od += 1024
        t2 = carve(od, 1024, BF16); od += 1024
        assert od <= OFF_F, od
        inh = B_ph2 + B_ws2 + B_ph1 + B_ws + B_ph0
        B_gsl = [Buf(f"gsl{i}", inherit=inh) for i in range(2)]
        B_wo = [Buf(f"wo{i}", inherit=inh) for i in range(2)]
        B_mT = Buf("mT", inherit=inh)
        B_xr = [Buf(f"xr{i}", inherit=inh) for i in range(4)]
        B_tha = Buf("tha", inherit=inh); B_thb = Buf("thb", inherit=inh)
        B_t1 = Buf("t1", inherit=inh); B_t2 = Buf("t2", inherit=inh)

        def load_ct(i):
            ct = i % 16
            P.dma("pool", sl3[i % 2], wph3_b[ct], r=[B_pre3[ct]], w=[B_gsl[i % 2]])

        def load_wo(i):
            P.dma("pool", wosl[i % 2].rearrange("p k c -> p (k c)"), wout_b[i % 8], r=[B_preo[i % 8]], w=[B_wo[i % 2]])

        load_ct(0)
        load_wo(0)
        ict = 0
        iwo = 0
        ixr = 0
        for G in range(4):
            gs = slice(G * 512, (G + 1) * 512)
            for ct in range(16):
                if ict + 1 < 64:
                    load_ct(ict + 1)
                gw = gsl[ict % 2]; pw = psl[ict % 2]; bw = B_gsl[ict % 2]
                par = ct % 2
                bya, byb, bga, bgb = 4 * par, 4 * par + 1, 4 * par + 2, 4 * par + 3
                for kt in range(8):
                    P.op("pe", lambda e, kt=kt, bya=bya, pw=pw: e.matmul(
                        bankf(bya), lhsT=pw[:, kt, 0, :], rhs=AT[:, kt, gs], start=(kt == 0), stop=(kt == 7)),
                        r=B_AT + [bw], w=[banks[bya]])
                for kt in range(8):
                    P.op("pe", lambda e, kt=kt, byb=byb, pw=pw: e.matmul(
                        bankf(byb), lhsT=pw[:, kt, 1, :], rhs=ST[:, kt, gs], start=(kt == 0), stop=(kt == 7)),
                        r=[B_ST, bw], w=[banks[byb]])
                for kt in range(16):
                    P.op("pe", lambda e, kt=kt, bga=bga, gw=gw: e.matmul(
                        bankf(bga), lhsT=gw[:, kt, 0, :], rhs=hT[:, kt, gs], start=(kt == 0), stop=(kt == 15)),
                        r=[B_hT, bw], w=[banks[bga]])
                for kt in range(16):
                    P.op("pe", lambda e, kt=kt, bgb=bgb, gw=gw: e.matmul(
                        bankf(bgb), lhsT=gw[:, kt, 1, :], rhs=hT[:, kt, gs], start=(kt == 0), stop=(kt == 15)),
                        r=[B_hT, bw], w=[banks[bgb]])
                P.op("act", lambda e, bga=bga: e.activation(out=tha, in_=bankf(bga), func=AF.Tanh, scale=0.5),
                     r=[banks[bga]], w=[B_tha])
                P.op("act", lambda e, bgb=bgb: e.activation(out=thb, in_=bankf(bgb), func=AF.Tanh, scale=0.5),
                     r=[banks[bgb]], w=[B_thb])
                P.op("dve", lambda e, bya=bya: e.scalar_tensor_tensor(out=t1, in0=tha, scalar=1.0, in1=bankf(bya),
                                                                      op0=ALU.add, op1=ALU.mult),
                     r=[B_tha, banks[bya]], w=[B_t1])
                P.op("dve", lambda e, byb=byb: e.scalar_tensor_tensor(out=t2, in0=thb, scalar=1.0, in1=bankf(byb),
                                                                      op0=ALU.add, op1=ALU.mult),
                     r=[B_thb, banks[byb]], w=[B_t2])
                P.op("dve", lambda e, ct=ct: e.tensor_tensor(out=mT[:, ct, :], in0=t1, in1=t2, op=ALU.add),
                     r=[B_t1, B_t2], w=[B_mT])
                ict += 1
            for cb in range(8):
                if iwo + 1 < 32:
                    load_wo(iwo + 1)
                wo = wosl[iwo % 2]; bwo = B_wo[iwo % 2]
                for tt in range(4):
                    row0 = G * 512 + tt * 128
                    xi = ixr % 4
                    xb = xr[xi // 2][:, (xi % 2) * 256:(xi % 2) * 256 + 256]
                    bxb = B_xr[xi]
                    bk = (ixr % 8)
                    P.dma("sp", xb, xo[row0:row0 + 128, cb * 256:(cb + 1) * 256], w=[bxb])
                    for kt in range(16):
                        P.op("pe", lambda e, kt=kt, tt=tt, bk=bk, wo=wo: e.matmul(
                            bankf(bk)[:, 0:256], lhsT=mT[:, kt, tt * 128:(tt + 1) * 128], rhs=wo[:, kt, :],
                            start=(kt == 0), stop=(kt == 15)), r=[B_mT, bwo], w=[banks[bk]])
                    P.op("dve", lambda e, bk=bk, xb=xb: e.scalar_tensor_tensor(
                        out=xb, in0=bankf(bk)[:, 0:256], scalar=0.5, in1=xb, op0=ALU.mult, op1=ALU.add),
                        r=[banks[bk], bxb], w=[bxb])
                    P.dma("sp", out_d[row0:row0 + 128, cb * 256:(cb + 1) * 256], xb, r=[bxb], dbuf=bxb)
                    ixr += 1
                iwo += 1
        fin = list(B_xr) + DBG_BUFS
        P.final_wait("sp", fin)

        block = st.enter_context(nc.Block())

        @block.tensor
        def _(e):
            P.emit(e, "pe")

        @block.scalar
        def _(e):
            P.emit(e, "act")

        @block.vector
        def _(e):
            P.emit(e, "dve")

        @block.gpsimd
        def _(e):
            P.emit(e, "pool")

        @block.sync
        def _(e):
            P.emit(e, "sp")
    return nc


def _host_layouts(x, norm_g, w_in, q_norm_g, k_norm_g, sgu_norm_g, w_spatial, b_spatial,
                  w_proj_a, w_proj_b, w_out):
    f32 = np.float32
    w_in = np.asarray(w_in, f32)[0]
    wi = w_in.reshape(16, 128, 11264)

    def fam(c0, n):
        blk = wi[:, :, c0:c0 + n].reshape(16, 128, n // 128, 128)
        return np.transpose(blk, (2, 1, 0, 3))
    q, k, v, za = fam(0, 1024), fam(1024, 1024), fam(2048, 1024), fam(3072, 1024)
    ub, vb, zb = fam(4096, 1024), fam(5120, 1024), fam(6144, 1024)
    ga, gb = fam(7168, 2048), fam(9216, 2048)
    wqkvz = np.ascontiguousarray(np.stack([q, k, v, za], axis=3)).reshape(8, 128, 8192)
    wkvpre = np.ascontiguousarray(np.transpose(np.stack([k, v], axis=3), (1, 0, 2, 3, 4))).reshape(128, 32768)
    wsgu = np.ascontiguousarray(np.stack([ub, vb, zb], axis=3)).reshape(8, 128, 6144)
    wgate = np.ascontiguousarray(np.stack([ga, gb], axis=3)).reshape(16, 128, 4096)
    pa = np.asarray(w_proj_a, f32)[0].reshape(8, 128, 16, 128)
    pb = np.asarray(w_proj_b, f32)[0].reshape(8, 128, 16, 128)
    wproj = np.ascontiguousarray(np.stack([np.transpose(pa, (2, 1, 0, 3)), np.transpose(pb, (2, 1, 0, 3))],
                                          axis=3)).reshape(16, 128, 2048)
    wph3 = np.ascontiguousarray(np.concatenate([wgate, wproj], axis=2))
    wo = np.asarray(w_out, f32)[0].reshape(16, 128, 8, 256)
    wout = np.ascontiguousarray(np.transpose(wo, (2, 1, 0, 3))).reshape(8, 128, 4096)
    vecs = np.zeros((128, 20), f32)
    vecs[:, 0:16] = np.asarray(norm_g, f32)[0].reshape(16, 128).T
    vecs[:, 16] = np.asarray(q_norm_g, f32)[0]
    vecs[:, 17] = np.asarray(k_norm_g, f32)[0]
    vecs[:, 18] = np.asarray(sgu_norm_g, f32)[0]
    bsp = np.ascontiguousarray(np.broadcast_to(np.asarray(b_spatial, f32)[0].reshape(1, 1024), (128, 1024)))
    wspT = np.ascontiguousarray(np.transpose(np.asarray(w_spatial, f32)[0], (2, 0, 1)))
    ident = np.eye(128, dtype=f32)
    tri = np.triu(np.ones((128, 128), f32))
    shared = dict(wqkvz=wqkvz, wkvpre=wkvpre, wsgu=wsgu, wph3=wph3, wout=wout, vecs=vecs, bsp=bsp,
                  wspT=wspT, ident=ident, tri=tri)
    x = np.asarray(x, f32)
    in_maps = []
    for c in range(8):
        b, j = c // 2, c % 2
        gb_ = np.full((16, 16), -1e30, f32)
        for i in range(16):
            lo = 0 if j == 1 else 8
            gb_[i, lo:8 + i // 2] = 0.0
        m = dict(shared)
        m["xo"] = np.ascontiguousarray(x[b, j * 2048:(j + 1) * 2048])
        m["xp"] = np.ascontiguousarray(x[b, 0:2048])
        m["gbias"] = np.ascontiguousarray(np.broadcast_to(gb_.reshape(1, 256), (128, 256)))
        in_maps.append(m)
    return in_maps


_NC_CACHE = {}


def kernel(x, norm_g, w_in, q_norm_g, k_norm_g, sgu_norm_g, w_spatial, b_spatial, w_proj_a, w_proj_b, w_out,
           _debug=False):
    in_maps = _host_layouts(x, norm_g, w_in, q_norm_g, k_norm_g, sgu_norm_g, w_spatial, b_spatial,
                            w_proj_a, w_proj_b, w_out)
    nc = build_program(debug=_debug)
    res = run_bass_kernel_spmd(nc, in_maps, core_ids=list(range(8)))
    out = np.empty((4, 4096, 2048), np.float32)
    for c in range(8):
        b, j = c // 2, c % 2
        out[b, j * 2048:(j + 1) * 2048] = res.results[c]["out"]
    if _debug:
        return out, res
    return out
```

```python
import numpy as np
from contextlib import ExitStack
import concourse.bass as bass
import concourse.mybir as mybir
from concourse.bass_utils import run_bass_kernel_spmd

F32 = mybir.dt.float32
BF16 = mybir.dt.bfloat16
U8 = mybir.dt.uint8
ALU = mybir.AluOpType
AF = mybir.ActivationFunctionType
AX = mybir.AxisListType

D = 2048
NH = 8
DH = 128
S_OWN = 2048
NT = 16
EPS = 1e-6
C1 = 0.7978845608028654
C2 = 0.044715
SQ128 = float(np.sqrt(128.0))


class Buf:
    def __init__(self, name, inherit=()):
        self.name = name
        self.writers = {}
        self.readers = {}
        self.dsem = None
        self.dcount = 0
        for b in inherit:
            for d in (b.writers, b.readers):
                for k, v in d.items():
                    if self.readers.get(k, (None, 0))[1] < v[1]:
                        self.readers[k] = v


class _Rec:
    def __init__(self):
        self.call = None

    def __getattr__(self, name):
        def f(*args, **kwargs):
            self.call = (name, args, kwargs)
        return f


class Eng:
    def __init__(self, name, sem):
        self.name = name
        self.sem = sem
        self.n = 0
        self.ops = []
        self.seen = {}


class Prog:
    def __init__(self, nc, st):
        self.nc = nc
        self.st = st
        self.E = {}
        for n in ("pe", "act", "dve", "pool", "sp"):
            self.E[n] = Eng(n, st.enter_context(nc.semaphore("s_" + n)))
        self.nsem = 5

    def _waits(self, eng, r, w, is_dma=False):
        E = self.E[eng]
        need = {}

        def add(d, raw, skip_dma=False):
            for k, (sem, val) in d.items():
                if skip_dma and k.startswith("dma_"):
                    continue
                if k == eng:
                    if eng == "pe" or not raw:
                        continue
                if need.get(k, (None, 0))[1] < val:
                    need[k] = (sem, val)
        for b in r:
            add(b.writers, True)
        for b in w:
            add(b.readers, False)
            add(b.writers, False, skip_dma=is_dma)
        out = []
        for k, (sem, val) in need.items():
            if E.seen.get(k, 0) >= val:
                continue
            E.seen[k] = val
            out.append((sem, val))
        return out

    def _record(self, key, tok, r, w):
        for b in r:
            b.readers[key] = tok
        for b in w:
            if b.readers:
                b.readers = {}
                b.writers = {}
            b.writers[key] = tok

    def op(self, eng, fn, r=(), w=()):
        E = self.E[eng]
        waits = self._waits(eng, r, w)
        E.n += 1
        tok = (E.sem, E.n)
        rec = _Rec()
        fn(rec)
        name, args, kwargs = rec.call
        E.ops.append((waits, lambda e: getattr(e, name)(*args, **kwargs), E.sem, 1))
        self._record(eng, tok, r, w)

    def dma(self, eng, out, in_, r=(), w=(), dbuf=None):
        E = self.E[eng]
        b = dbuf if dbuf is not None else (w[0] if w else r[0])
        if b.dsem is None:
            b.dsem = self.st.enter_context(self.nc.semaphore("d_" + b.name))
            self.nsem += 1
        waits = self._waits(eng, r, w, is_dma=True)
        b.dcount += 16
        tok = (b.dsem, b.dcount)
        E.ops.append((waits, lambda e: e.dma_start(out=out, in_=in_), b.dsem, 16))
        self._record("dma_" + b.name, tok, r, w)

    def final_wait(self, eng, bufs):
        E = self.E[eng]
        for b in bufs:
            E.ops.append(([(b.dsem, b.dcount)], None, None, 0))

    def emit(self, e, name):
        for waits, fn, sem, inc in self.E[name].ops:
            for s, v in waits:
                e.wait_ge(s, v)
            if fn is None:
                continue
            ins = fn(e)
            ins.then_inc(sem, inc)


def build_program(debug=False):
    nc = bass.Bass("TRN2", target_bir_lowering=False)

    def din(name, shape):
        return nc.dram_tensor(name, list(shape), F32, kind="ExternalInput").ap()

    xo = din("xo", (S_OWN, D))
    xp = din("xp", (S_OWN, D))
    wqkvz = din("wqkvz", (NH, 128, 8192))
    wkvpre = din("wkvpre", (128, 32768))
    wsgu = din("wsgu", (8, 128, 6144))
    wph3 = din("wph3", (16, 128, 6144))
    wout = din("wout", (8, 128, 4096))
    wph3_b = nc.dram_tensor("wph3_b", [16, 128, 6144], BF16, kind="Internal").ap()
    wout_b = nc.dram_tensor("wout_b", [8, 128, 4096], BF16, kind="Internal").ap()
    vecs = din("vecs", (128, 20))
    gbias_d = din("gbias", (128, 256))
    bsp_d = din("bsp", (128, 1024))
    wspT_d = din("wspT", (128, 8, 128))
    ident_d = din("ident", (128, 128))
    tri_d = din("tri", (128, 128))
    out_d = nc.dram_tensor("out", [S_OWN, D], F32, kind="ExternalOutput").ap()
    dbg = {}
    if debug:
        for nm, shp in (("d_hT", (128, 16 * 2048)), ("d_AT", (128, 8 * 2048)), ("d_ST", (128, 8 * 2048)),
                        ("d_KTpre", (128, 8 * 2048)), ("d_QK", (128, 2 * 2048)), ("d_sel", (128, 256)),
                        ("d_gm", (128, 256)), ("d_V", (128, 16 * 129)), ("d_za", (128, 2048))):
            dbg[nm] = nc.dram_tensor(nm, list(shp), BF16 if nm not in ("d_sel", "d_gm") else F32,
                                     kind="ExternalOutput").ap()

    with ExitStack() as st:
        ARENA_BYTES = 212480
        arena = st.enter_context(nc.sbuf_tensor("arena", [128, ARENA_BYTES], U8))
        psum_all = st.enter_context(nc.psum_tensor("ps", [128, 4096], F32))
        P = Prog(nc, st)

        def carve(off, nbytes, dt):
            assert off % 32 == 0 and off + nbytes <= ARENA_BYTES, (off, nbytes)
            return arena[:, off:off + nbytes].bitcast(dt)

        OFF_A = 0
        OFF_B = 65536
        OFF_C = 98304
        OFF_D = 131584
        OFF_E = 164352
        OFF_F = 200192
        hT = carve(OFF_A, 65536, BF16).rearrange("p (k t) -> p k t", k=16)
        Wkv = carve(OFF_A, 65536, BF16).rearrange("p (h k c) -> p h k c", h=8, k=16)
        KTpre = carve(OFF_B, 32768, BF16).rearrange("p (h t) -> p h t", h=8)
        AT = KTpre
        Vpre = carve(OFF_C, 33024, BF16).rearrange("p (t h c) -> p t h c", t=16, h=8)
        ST = carve(OFF_C, 32768, BF16).rearrange("p (g t) -> p g t", g=8)

        o = OFF_F
        ident = carve(o, 256, BF16); o += 256
        tri = carve(o, 256, BF16); o += 256
        WT = carve(o, 2048, BF16).rearrange("p (g t) -> p g t", g=8); o += 2048
        Bsp = carve(o, 4096, F32).rearrange("p (g t) -> p g t", g=8); o += 4096
        gbias = carve(o, 1024, F32); o += 1024
        vec = carve(o, 96, F32); o += 96
        m05 = carve(o, 64, F32); o += 64
        stat = carve(o, 512, F32); o += 512
        km = carve(o, 64, F32); o += 64
        kmb = carve(o, 32, BF16); o += 32
        gm = carve(o, 1024, F32).rearrange("p (c b) -> p c b", c=16); o += 1024
        sel = carve(o, 1024, F32).rearrange("p (c b) -> p c b", c=16); o += 1024
        m8 = carve(o, 512, F32).rearrange("p (c b) -> p c b", c=16); o += 512
        thr = carve(o, 64, F32); o += 64
        assert o <= ARENA_BYTES, o

        DBG_BUFS = []

        def dbuf(n):
            b = Buf(n)
            DBG_BUFS.append(b)
            return b

        B_const = Buf("const")
        B_Aq = [Buf(f"A{i}") for i in range(4)]
        banks = [Buf(f"bank{i}") for i in range(8)]

        def bankf(i):
            return psum_all[:, i * 512:(i + 1) * 512]

        def bankb(i):
            return psum_all[:, i * 512:(i + 1) * 512].bitcast(BF16)

        B_constp = Buf("constp")
        P.dma("sp", vec[:, 0:20], vecs, w=[B_const])
        P.dma("sp", gbias, gbias_d, w=[B_const])
        P.dma("sp", Bsp.rearrange("p g t -> p (g t)"), bsp_d, w=[B_const])
        P.dma("pool", ident, ident_d, w=[B_constp])
        P.dma("pool", tri, tri_d, w=[B_constp])
        P.dma("pool", WT.rearrange("p g t -> p (g t)"), wspT_d.rearrange("p g t -> p (g t)"), w=[B_constp])
        B_c2 = Buf("const2")
        P.op("dve", lambda e: e.tensor_scalar(out=vec[:, 20:23], in0=vec[:, 16:19], scalar1=SQ128, scalar2=None,
                                              op0=ALU.mult), r=[B_const], w=[B_c2])
        P.op("pool", lambda e: e.memset(m05, -0.5), w=[B_c2])
        P.op("dve", lambda e: e.tensor_tensor(out=WT, in0=WT, in1=tri.unsqueeze(1).to_broadcast([128, 8, 128]),
                                              op=ALU.mult), r=[B_constp], w=[B_c2])
        CONST = [B_const, B_constp, B_c2]

        od = OFF_E
        xt = [carve(od + i * 8192, 8192, F32) for i in range(2)]; od += 16384
        xn = [carve(od + i * 4096, 4096, BF16) for i in range(2)]; od += 8192
        hTt = [carve(od + i * 4096, 4096, BF16).rearrange("p (k t) -> p k t", k=16) for i in range(2)]; od += 8192
        junk = carve(od, 256, BF16); od += 256
        assert od <= OFF_F
        kn = [carve(OFF_D + 28672 + i * 2048, 2048, BF16).rearrange("p (h c) -> p h c", h=8) for i in range(2)]
        B_xt = [Buf("xt0"), Buf("xt1")]
        B_xn = [Buf("xn0"), Buf("xn1")]; B_hTt = [Buf("hTt0"), Buf("hTt1")]; B_kn = [Buf("kn0"), Buf("kn1")]
        B_junk = Buf("junk")
        B_st0 = [Buf("st0a"), Buf("st0b")]; B_stk = [Buf(f"stk{i}") for i in range(8)]
        B_KTpre = [Buf(f"KTpre{h}") for h in range(8)]; B_Vpre = Buf("Vpre")

        P.op("pool", lambda e: e.memset(Vpre[:, :, :, 128:129], 1.0), w=[B_Vpre])
        for q4 in range(4):
            P.dma("pool", carve(OFF_A + q4 * 16384, 16384, BF16), wkvpre[:, q4 * 8192:(q4 + 1) * 8192], w=[B_Aq[q4]])
        B_hT = None

        def st_A(src_dram, t):
            xb = xt[t % 2]; bx = B_xt[t % 2]; xnb = xn[t % 2]; bxn = B_xn[t % 2]
            P.dma("sp", xb, src_dram[(t % 16) * 128:(t % 16 + 1) * 128, :], w=[bx])
            ss = stat[:, 2 * (t % 2):2 * (t % 2) + 1]; rs = stat[:, 2 * (t % 2) + 1:2 * (t % 2) + 2]
            bst = B_st0[t % 2]
            P.op("act", lambda e: e.activation(out=xnb, in_=xb, func=AF.Square, accum_out=ss), r=[bx], w=[bxn, bst])
            P.op("dve", lambda e: e.tensor_scalar(out=ss, in0=ss, scalar1=1.0 / D, scalar2=EPS, op0=ALU.mult,
                                                  op1=ALU.add), r=[bst], w=[bst])
            P.op("pool", lambda e: e.tensor_tensor(out=rs, in0=ss, in1=m05[:, 0:1], op=ALU.pow),
                 r=[bst] + CONST, w=[bst])
            P.op("dve", lambda e: e.tensor_scalar(out=xnb, in0=xb, scalar1=rs, scalar2=None, op0=ALU.mult),
                 r=[bx, bst], w=[bxn])

        def st_B(t, dst_ap, dst_buf):
            xnb = xn[t % 2]; bxn = B_xn[t % 2]
            for half in range(2):
                bk = half
                for j in range(8):
                    kt = half * 8 + j
                    P.op("pe", lambda e: e.transpose(out=bankb(bk)[:, j * 128:(j + 1) * 128],
                                                     in_=xnb[:, kt * 128:(kt + 1) * 128], identity=ident),
                         r=[bxn] + CONST, w=[banks[bk]])
                P.op("dve", lambda e: e.tensor_tensor(
                    out=dst_ap[:, half * 8:(half + 1) * 8, :], in0=bankb(bk).rearrange("p (k t) -> p k t", k=8),
                    in1=vec[:, half * 8:(half + 1) * 8].unsqueeze(2).to_broadcast([128, 8, 128]), op=ALU.mult),
                    r=[banks[bk]] + CONST, w=[dst_buf])

        def st_CD(t):
            hb = hTt[t % 2]; bhb = B_hTt[t % 2]; knb = kn[t % 2]; bkn = B_kn[t % 2]
            for b2 in range(4):
                bk = 2 + b2
                for hh in range(2):
                    h = 2 * b2 + hh
                    for kt in range(16):
                        P.op("pe", lambda e: e.matmul(bankf(bk)[:, hh * 256:hh * 256 + 256], lhsT=hb[:, kt, :],
                                                      rhs=Wkv[:, h, kt, :], start=(kt == 0), stop=(kt == 15)),
                             r=[bhb, B_Aq[b2]], w=[banks[bk]])
                so = 8 + ((t % 2) * 4 + b2) * 4
                ssk = stat[:, so:so + 2]; rsk = stat[:, so + 2:so + 4]; bst = B_stk[(t % 2) * 4 + b2]
                pv = bankf(bk).rearrange("p (h c) -> p h c", h=2)
                for hh in range(2):
                    P.op("act", lambda e: e.activation(out=junk, in_=pv[:, hh, 0:128], func=AF.Square,
                                                       accum_out=ssk[:, hh:hh + 1]), r=[banks[bk]], w=[B_junk, bst])
                P.op("dve", lambda e: e.tensor_scalar(out=ssk, in0=ssk, scalar1=128.0 * EPS, scalar2=None,
                                                      op0=ALU.add), r=[bst], w=[bst])
                P.op("pool", lambda e: e.tensor_tensor(out=rsk, in0=ssk, in1=m05[:, 0:2], op=ALU.pow),
                     r=[bst] + CONST, w=[bst])
                P.op("dve", lambda e: e.tensor_tensor(
                    out=knb[:, 2 * b2:2 * b2 + 2, :], in0=pv[:, :, 0:128],
                    in1=rsk.unsqueeze(2).to_broadcast([128, 2, 128]), op=ALU.mult), r=[banks[bk], bst], w=[bkn])
                P.op("act", lambda e: e.copy(out=Vpre[:, t, 2 * b2:2 * b2 + 2, 0:128], in_=pv[:, :, 128:256]),
                     r=[banks[bk]], w=[B_Vpre])

        def st_E(t):
            knb = kn[t % 2]; bkn = B_kn[t % 2]
            for h in range(8):
                P.op("pe", lambda e: e.transpose(out=bankb(6)[:, h * 128:(h + 1) * 128], in_=knb[:, h, :],
                                                 identity=ident), r=[bkn] + CONST, w=[banks[6]])
            P.op("dve", lambda e: e.tensor_scalar(
                out=KTpre[:, :, t * 128:(t + 1) * 128], in0=bankb(6).rearrange("p (h t) -> p h t", h=8),
                scalar1=vec[:, 21:22], scalar2=None, op0=ALU.mult), r=[banks[6]] + CONST, w=B_KTpre)

        st_A(xp, 0)
        st_B(0, hTt[0], B_hTt[0])
        st_A(xp, 1)
        for t in range(NT):
            if t + 1 < NT:
                st_B(t + 1, hTt[(t + 1) % 2], B_hTt[(t + 1) % 2])
            if t + 2 < NT:
                st_A(xp, t + 2)
            elif t + 2 == NT:
                st_A(xo, 16)
            st_CD(t)
            if t >= 1:
                st_E(t - 1)
        st_E(NT - 1)
        B_hT = Buf("hT", inherit=B_Aq)
        for t in range(16, 32):
            if t + 1 < 32:
                st_A(xo, t + 1)
            st_B(t, hT[:, :, (t - 16) * 128:(t - 15) * 128], B_hT)

        B_pre3 = [Buf(f"pre3_{i}") for i in range(16)]; B_preo = [Buf(f"preo_{i}") for i in range(8)]
        wslot = [carve(OFF_D + i * 16384, 16384, BF16).rearrange("p (k f c) -> p k f c", k=16, f=4) for i in range(2)]
        B_ws = [Buf("ws0"), Buf("ws1", inherit=B_kn)]
        oe = OFF_E
        B_ph0 = B_xt + B_xn + B_hTt + B_kn + [B_junk]
        QK = carve(oe, 8192, BF16).rearrange("p (f t) -> p f t", f=2); oe += 8192
        Vown = carve(oe, 4160, BF16)[:, 0:16 * 129].rearrange("p (t c) -> p t c", t=16); oe += 4160
        zaT = carve(oe, 4096, BF16); oe += 4096
        PT = [carve(oe + i * 2048, 2048, BF16).rearrange("p (j q) -> p j q", j=2) for i in range(3)]; oe += 6144
        acc = [carve(oe + i * 2080, 2064, F32).rearrange("p (c d) -> p c d", c=4) for i in range(2)]; oe += 4160
        qkn = [carve(oe + i * 512, 512, BF16).rearrange("p (f c) -> p f c", f=2) for i in range(3)]; oe += 1536
        attn = carve(oe, 1024, BF16).rearrange("p (c d) -> p c d", c=4); oe += 1024
        th = carve(oe, 2048, F32); oe += 2048
        rinv = carve(oe, 32, F32); oe += 32
        junk1 = carve(oe, 256, BF16); oe += 256
        assert oe <= OFF_E + 35840, oe
        B_QK = Buf("QK", inherit=B_ph0); B_Vown = Buf("Vown", inherit=B_ph0); B_zaT = Buf("zaT", inherit=B_ph0)
        B_PT = [Buf(f"PT{i}", inherit=B_ph0) for i in range(3)]
        B_acc = [Buf(f"acc{i}", inherit=B_ph0) for i in range(2)]
        B_qkn = [Buf(f"qkn{i}", inherit=B_ph0) for i in range(3)]
        B_st2 = [Buf(f"st2_{i}") for i in range(3)]
        B_attn = Buf("attn", inherit=B_ph0); B_th = Buf("th", inherit=B_ph0); B_rinv = Buf("rinv", inherit=B_ph0)
        B_junk1 = Buf("junk1", inherit=B_ph0)
        B_gate = Buf("gate")
        B_AT = B_KTpre

        P.op("pool", lambda e: e.memset(Vown[:, :, 128:129], 1.0), w=[B_Vown])

        def load_head_w(h):
            s = h % 2
            P.dma("pool", carve(OFF_D + s * 16384, 16384, BF16), wqkvz[h], w=[B_ws[s]])

        load_head_w(0)
        pt_rr = 0
        for h in range(NH):
            if h + 1 < NH:
                load_head_w(h + 1)
            ws = wslot[h % 2]; bws = B_ws[h % 2]
            def st_C(t):
                bk = t % 3
                for kt in range(16):
                    P.op("pe", lambda e: e.matmul(
                        bankf(bk)[:, 0:384], lhsT=hT[:, kt, t * 128:(t + 1) * 128],
                        rhs=ws[:, kt, 0:3, :].rearrange("p f c -> p (f c)"), start=(kt == 0), stop=(kt == 15)),
                        r=[B_hT, bws], w=[banks[bk]])
                so = 72 + (t % 3) * 4
                ss2 = stat[:, so:so + 2]; rs2 = stat[:, so + 2:so + 4]; bst = B_st2[t % 3]
                for f in range(2):
                    P.op("act", lambda e: e.activation(out=junk1, in_=bankf(bk)[:, f * 128:(f + 1) * 128],
                                                       func=AF.Square, accum_out=ss2[:, f:f + 1]),
                         r=[banks[bk]], w=[B_junk1, bst])
                P.op("dve", lambda e: e.tensor_scalar(out=ss2, in0=ss2, scalar1=128.0 * EPS, scalar2=None,
                                                      op0=ALU.add), r=[bst], w=[bst])
                P.op("pool", lambda e: e.tensor_tensor(out=rs2, in0=ss2, in1=m05[:, 0:2], op=ALU.pow),
                     r=[bst] + CONST, w=[bst])
                qn = qkn[t % 3]; bqn = B_qkn[t % 3]
                P.op("dve", lambda e: e.tensor_tensor(
                    out=qn, in0=bankf(bk)[:, 0:256].rearrange("p (f c) -> p f c", f=2),
                    in1=rs2.unsqueeze(2).to_broadcast([128, 2, 128]), op=ALU.mult), r=[banks[bk], bst], w=[bqn])
                P.op("act", lambda e: e.copy(out=Vown[:, t, 0:128], in_=bankf(bk)[:, 256:384]),
                     r=[banks[bk]], w=[B_Vown])

            def st_Eq(t):
                qn = qkn[t % 3]; bqn = B_qkn[t % 3]
                for f in range(2):
                    P.op("pe", lambda e: e.transpose(out=bankb(3)[:, f * 128:(f + 1) * 128], in_=qn[:, f, :],
                                                     identity=ident), r=[bqn] + CONST, w=[banks[3]])
                P.op("dve", lambda e: e.tensor_tensor(
                    out=QK[:, :, t * 128:(t + 1) * 128], in0=bankb(3)[:, 0:256].rearrange("p (f c) -> p f c", f=2),
                    in1=vec[:, 20:22].unsqueeze(2).to_broadcast([128, 2, 128]), op=ALU.mult),
                    r=[banks[3]] + CONST, w=[B_QK])

            def st_za(G):
                bk = 4 + G % 2
                for kt in range(16):
                    P.op("pe", lambda e: e.matmul(bankf(bk), lhsT=ws[:, kt, 3, :], rhs=hT[:, kt, G * 512:(G + 1) * 512],
                                                  start=(kt == 0), stop=(kt == 15)), r=[B_hT, bws], w=[banks[bk]])
                P.op("act", lambda e: e.activation(out=th, in_=bankf(bk), func=AF.Tanh, scale=0.5),
                     r=[banks[bk]], w=[B_th])
                P.op("dve", lambda e: e.scalar_tensor_tensor(
                    out=zaT[:, G * 512:(G + 1) * 512], in0=th, scalar=1.0, in1=bankf(bk), op0=ALU.add, op1=ALU.mult),
                    r=[B_th, banks[bk]], w=[B_zaT])

            for t in range(NT):
                st_C(t)
                if t >= 2:
                    st_Eq(t - 2)
                if t % 4 == 3:
                    st_za(t // 4)
            st_Eq(NT - 2)
            st_Eq(NT - 1)
            def ktile(j):
                return KTpre[:, h, j * 128:(j + 1) * 128] if j < 16 else QK[:, 1, (j - 16) * 128:(j - 15) * 128]

            def vtile(j):
                return Vpre[:, j, h, :] if j < 16 else Vown[:, j - 16, :]

            allsteps = []
            for m in range(4):
                steps = []
                for b in range(8 + 2 * m):
                    steps.append(dict(tiles=[(2 * b, 0), (2 * b + 1, 0)],
                                      chunks={c: [0, 1] for c in range(4)}, blk={c: b for c in range(4)}, diag=[]))
                r0 = 16 + 4 * m
                steps.append(dict(tiles=[(r0, 0), (r0 + 1, 1)],
                                  chunks={0: [0], 1: [0, 1], 2: [0, 1], 3: [0, 1]},
                                  blk={0: None, 1: None, 2: 8 + 2 * m, 3: 8 + 2 * m}, diag=[(0, 0), (1, 1)]))
                steps.append(dict(tiles=[(r0 + 2, 2), (r0 + 3, 3)],
                                  chunks={2: [0], 3: [0, 1]}, blk={2: None, 3: None}, diag=[(0, 2), (1, 3)]))
                for k_, sp_ in enumerate(steps):
                    sp_["m"] = m
                    sp_["first"] = (k_ == 0)
                    sp_["last"] = (k_ == len(steps) - 1)
                allsteps += steps
            for si, sp_ in enumerate(allsteps):
                sp_["pt"] = (pt_rr + si) % 3
                sp_["sb"] = 0 + 2 * (si % 2)
                sp_["ob"] = 4 + 2 * (si % 2)
            pt_rr += len(allsteps)

            def emit_qk(sp_):
                m = sp_["m"]
                pt = PT[sp_["pt"]]; bpt = B_PT[sp_["pt"]]
                sb = sp_["sb"]
                for idx, (j, c0) in enumerate(sp_["tiles"]):
                    bk = sb + idx
                    n = (4 - c0) * 128
                    P.op("pe", lambda e: e.matmul(
                        bankf(bk)[:, 0:n], lhsT=ktile(j), rhs=QK[:, 0, m * 512 + c0 * 128:(m + 1) * 512],
                        start=True, stop=True), r=[B_KTpre[h], B_QK], w=[banks[bk]])
                if all(c0 == 0 for (_, c0) in sp_["tiles"]):
                    P.op("act", lambda e: e.activation(
                        out=pt, in_=psum_all[:, sb * 512:(sb + 2) * 512].rearrange("p (j q) -> p j q", j=2),
                        func=AF.Exp, scale=float(DH ** -0.5)), r=[banks[sb], banks[sb + 1]], w=[bpt])
                else:
                    for idx, (j, c0) in enumerate(sp_["tiles"]):
                        bk = sb + idx
                        n = (4 - c0) * 128
                        P.op("act", lambda e: e.activation(
                            out=pt[:, idx, c0 * 128:512], in_=bankf(bk)[:, 0:n], func=AF.Exp,
                            scale=float(DH ** -0.5)), r=[banks[bk]], w=[bpt])
                for (idx, c) in sp_["diag"]:
                    P.op("pool", lambda e: e.tensor_tensor(
                        out=pt[:, idx, c * 128:(c + 1) * 128], in0=pt[:, idx, c * 128:(c + 1) * 128], in1=tri,
                        op=ALU.mult), r=[bpt] + CONST, w=[bpt])

            def emit_pv(sp_):
                m = sp_["m"]
                ac = acc[m % 2]; bac = B_acc[m % 2]
                pt = PT[sp_["pt"]]; bpt = B_PT[sp_["pt"]]
                for c, tl in sp_["chunks"].items():
                    bk = sp_["ob"] + c // 2
                    po = bankf(bk)[:, (c % 2) * 129:(c % 2) * 129 + 129]
                    for ii, idx in enumerate(tl):
                        j = sp_["tiles"][idx][0]
                        P.op("pe", lambda e: e.matmul(po, lhsT=pt[:, idx, c * 128:(c + 1) * 128], rhs=vtile(j),
                                                      start=(ii == 0), stop=(ii == len(tl) - 1)),
                             r=[bpt, B_Vpre, B_Vown], w=[banks[bk]])
                for c, tl in sp_["chunks"].items():
                    bk = sp_["ob"] + c // 2
                    po = bankf(bk)[:, (c % 2) * 129:(c % 2) * 129 + 129]
                    b = sp_["blk"][c]
                    sc = 1.0 if b is None else sel[:, 4 * m + c, b:b + 1]
                    if sp_["first"]:
                        P.op("dve", lambda e: e.tensor_scalar(out=ac[:, c, :], in0=po, scalar1=sc, scalar2=None,
                                                              op0=ALU.mult), r=[banks[bk], B_gate], w=[bac])
                    else:
                        P.op("dve", lambda e: e.scalar_tensor_tensor(
                            out=ac[:, c, :], in0=po, scalar=sc, in1=ac[:, c, :], op0=ALU.mult, op1=ALU.add),
                            r=[banks[bk], B_gate, bac], w=[bac])

            def fin_dve(m):
                ac = acc[m % 2]; bac = B_acc[m % 2]
                P.op("dve", lambda e: e.tensor_scalar(out=rinv[:, 0:4], in0=ac[:, :, 128], scalar1=2.0, scalar2=None,
                                                      op0=ALU.mult), r=[bac], w=[B_rinv])
                P.op("dve", lambda e: e.reciprocal(out=rinv[:, 4:8], in_=rinv[:, 0:4]), r=[B_rinv], w=[B_rinv])
                P.op("dve", lambda e: e.tensor_tensor(out=attn, in0=ac[:, :, 0:128],
                                                      in1=rinv[:, 4:8].unsqueeze(2).to_broadcast([128, 4, 128]),
                                                      op=ALU.mult), r=[bac, B_rinv], w=[B_attn])

            def fin_pe(m, bk):
                for c in range(4):
                    P.op("pe", lambda e: e.transpose(out=bankb(bk)[:, c * 128:(c + 1) * 128], in_=attn[:, c, :],
                                                     identity=ident), r=[B_attn] + CONST, w=[banks[bk]])
                P.op("dve", lambda e: e.tensor_tensor(out=zaT[:, m * 512:(m + 1) * 512], in0=bankb(bk)[:, 0:512],
                                                      in1=zaT[:, m * 512:(m + 1) * 512], op=ALU.mult),
                     r=[banks[bk], B_zaT], w=[B_zaT])

            P.op("dve", lambda e: e.tensor_reduce(out=km[:, 0:8], in_=KTpre[:, h, :].rearrange("p (b k) -> p b k", b=8),
                                                  axis=AX.X, op=ALU.add), r=[B_KTpre[h]], w=[B_gate])
            P.op("dve", lambda e: e.tensor_reduce(out=km[:, 8:16], in_=QK[:, 1, :].rearrange("p (b k) -> p b k", b=8),
                                                  axis=AX.X, op=ALU.add), r=[B_QK], w=[B_gate])
            P.op("dve", lambda e: e.tensor_scalar(out=kmb, in0=km, scalar1=1.0 / 256.0, scalar2=None, op0=ALU.mult),
                 r=[B_gate], w=[B_gate])
            emit_qk(allsteps[0])
            for c in range(16):
                P.op("pe", lambda e: e.matmul(bankf(6)[:, c * 16:(c + 1) * 16], lhsT=QK[:, 0, c * 128:(c + 1) * 128],
                                              rhs=kmb, start=True, stop=True), r=[B_QK, B_gate], w=[banks[6]])
            P.op("dve", lambda e: e.tensor_tensor(out=gm.rearrange("p c b -> p (c b)"), in0=bankf(6)[:, 0:256],
                                                  in1=gbias, op=ALU.add), r=[banks[6]] + CONST, w=[B_gate])
            for c in range(16):
                P.op("dve", lambda e: e.max(out=m8[:, c, :], in_=gm[:, c, :]), r=[B_gate], w=[B_gate])
            P.op("dve", lambda e: e.tensor_scalar(out=thr, in0=m8[:, :, 2], scalar1=-1e29, scalar2=None, op0=ALU.max),
                 r=[B_gate], w=[B_gate])
            P.op("dve", lambda e: e.tensor_tensor(out=sel, in0=gm, in1=thr.unsqueeze(2).to_broadcast([128, 16, 16]),
                                                  op=ALU.is_ge), r=[B_gate], w=[B_gate])
            if debug and h == 0:
                P.dma("sp", dbg["d_QK"], QK.rearrange("p f t -> p (f t)"), r=[B_QK], dbuf=dbuf("dq"))
                P.dma("sp", dbg["d_sel"], sel.rearrange("p c b -> p (c b)"), r=[B_gate], dbuf=dbuf("ds"))
                P.dma("sp", dbg["d_gm"], gm.rearrange("p c b -> p (c b)"), r=[B_gate], dbuf=dbuf("dg"))
                P.dma("sp", dbg["d_V"], Vown.rearrange("p t c -> p (t c)"), r=[B_Vown], dbuf=dbuf("dv"))

            pend = None
            for si, sp_ in enumerate(allsteps):
                if si + 1 < len(allsteps):
                    emit_qk(allsteps[si + 1])
                emit_pv(sp_)
                if pend is not None:
                    fin_pe(pend, 4 + 2 * ((si + 1) % 2))
                    pend = None
                if sp_["last"]:
                    fin_dve(sp_["m"])
                    pend = sp_["m"]
            fin_pe(pend, 4)
            P.op("pool", lambda e, h=h: e.tensor_copy(out=AT[:, h, :], in_=zaT), r=[B_zaT], w=[B_KTpre[h]])

        if debug:
            P.dma("sp", dbg["d_hT"], hT.rearrange("p k t -> p (k t)"), r=[B_hT], dbuf=dbuf("dh"))
            P.dma("sp", dbg["d_AT"], AT.rearrange("p h t -> p (h t)"), r=B_AT, dbuf=dbuf("da"))

        B_ph1 = [B_QK, B_Vown, B_zaT] + B_PT + B_acc + B_qkn + [B_attn, B_th, B_rinv, B_junk1]
        wsl2 = [carve(OFF_D + i * 16384, 12288, BF16).rearrange("p (k f c) -> p k f c", k=16, f=3) for i in range(2)]
        B_ws2 = [Buf("ws2_0", inherit=[B_ws[0]]), Buf("ws2_1", inherit=[B_ws[1]])]
        oe = OFF_E
        a1 = [carve(oe + i * 2048, 2048, F32) for i in range(2)]; oe += 4096
        a2 = [carve(oe + i * 2048, 2048, F32) for i in range(2)]; oe += 4096
        a3 = carve(oe, 2048, F32); oe += 2048
        ug2 = carve(oe, 1024, BF16); oe += 1024
        zs2 = carve(oe, 1024, BF16); oe += 1024
        uz = [carve(oe + i * 1024, 1024, BF16) for i in range(2)]; oe += 2048
        vg2 = carve(oe, 2048, F32); oe += 2048
        vn = [carve(oe + i * 1024, 1024, BF16) for i in range(2)]; oe += 2048
        s1 = carve(oe, 2048, F32); oe += 2048
        junk2 = carve(oe, 256, BF16); oe += 256
        assert oe <= OFF_E + 35840
        B_a1 = [Buf(f"a1_{i}", inherit=B_ph1) for i in range(2)]
        B_a2 = [Buf(f"a2_{i}", inherit=B_ph1) for i in range(2)]
        B_a3 = Buf("a3", inherit=B_ph1); B_ug2 = Buf("ug2", inherit=B_ph1); B_zs2 = Buf("zs2", inherit=B_ph1)
        B_uz = [Buf(f"uz{i}", inherit=B_ph1) for i in range(2)]; B_vg2 = Buf("vg2", inherit=B_ph1)
        B_vn = [Buf(f"vn{i}", inherit=B_ph1) for i in range(2)]
        B_s1 = Buf("s1", inherit=B_ph1); B_junk2 = Buf("junk2", inherit=B_ph1)
        B_ST = Buf("ST", inherit=[B_Vpre])
        B_stv = [Buf("stva"), Buf("stvb")]

        def gelu2(bk, i, dst, dst_buf):
            P.op("act", lambda e: e.activation(out=a1[i], in_=bankf(bk), func=AF.Square, scale=float(np.sqrt(C2))),
                 r=[banks[bk]], w=[B_a1[i]])
            P.op("dve", lambda e: e.scalar_tensor_tensor(out=a2[i], in0=a1[i], scalar=1.0, in1=bankf(bk),
                                                         op0=ALU.add, op1=ALU.mult),
                 r=[B_a1[i], banks[bk]], w=[B_a2[i]])
            P.op("act", lambda e: e.activation(out=a1[i], in_=a2[i], func=AF.Tanh, scale=C1),
                 r=[B_a2[i]], w=[B_a1[i]])
            P.op("dve", lambda e: e.scalar_tensor_tensor(out=dst, in0=a1[i], scalar=1.0, in1=bankf(bk),
                                                         op0=ALU.add, op1=ALU.mult),
                 r=[B_a1[i], banks[bk]], w=[dst_buf])

        def load_sgu_w(g):
            P.dma("pool", carve(OFF_D + (g % 2) * 16384, 12288, BF16), wsgu[g], w=[B_ws2[g % 2]])

        def st2_C(g, G):
            ws = wsl2[g % 2]; bws = B_ws2[g % 2]
            par = G % 2
            bu, bz, bv = 0 + par, 2 + par, 4 + par
            for kt in range(16):
                P.op("pe", lambda e: e.matmul(bankf(bu), lhsT=ws[:, kt, 0, :], rhs=hT[:, kt, G * 512:(G + 1) * 512],
                                              start=(kt == 0), stop=(kt == 15)), r=[B_hT, bws], w=[banks[bu]])
            for kt in range(16):
                P.op("pe", lambda e: e.matmul(bankf(bz), lhsT=ws[:, kt, 2, :], rhs=hT[:, kt, G * 512:(G + 1) * 512],
                                              start=(kt == 0), stop=(kt == 15)), r=[B_hT, bws], w=[banks[bz]])
            for tt in range(4):
                t = G * 4 + tt
                for kt in range(16):
                    P.op("pe", lambda e: e.matmul(bankf(bv)[:, tt * 128:(tt + 1) * 128],
                                                  lhsT=hT[:, kt, t * 128:(t + 1) * 128], rhs=ws[:, kt, 1, :],
                                                  start=(kt == 0), stop=(kt == 15)), r=[B_hT, bws], w=[banks[bv]])

        def st2_D(g, G):
            par = G % 2
            bu, bz, bv = 0 + par, 2 + par, 4 + par
            gelu2(bu, 0, ug2, B_ug2)
            P.op("act", lambda e: e.activation(out=a3, in_=bankf(bz), func=AF.Tanh, scale=0.5),
                 r=[banks[bz]], w=[B_a3])
            P.op("dve", lambda e: e.scalar_tensor_tensor(out=zs2, in0=a3, scalar=1.0, in1=bankf(bz),
                                                         op0=ALU.add, op1=ALU.mult), r=[B_a3, banks[bz]], w=[B_zs2])
            P.op("dve", lambda e: e.scalar_tensor_tensor(out=uz[par], in0=ug2, scalar=0.25, in1=zs2,
                                                         op0=ALU.mult, op1=ALU.mult), r=[B_ug2, B_zs2], w=[B_uz[par]])
            gelu2(bv, 1, vg2, B_vg2)
            ssv = stat[:, 48 + 8 * par:52 + 8 * par]; rsv = stat[:, 52 + 8 * par:56 + 8 * par]
            bst = B_stv[par]
            for tt in range(4):
                P.op("act", lambda e: e.activation(out=junk2, in_=vg2[:, tt * 128:(tt + 1) * 128],
                                                   func=AF.Square, accum_out=ssv[:, tt:tt + 1]),
                     r=[B_vg2], w=[B_junk2, bst])
            P.op("dve", lambda e: e.tensor_scalar(out=ssv, in0=ssv, scalar1=512.0 * EPS, scalar2=None, op0=ALU.add),
                 r=[bst], w=[bst])
            P.op("pool", lambda e: e.tensor_tensor(out=rsv, in0=ssv, in1=m05[:, 0:4], op=ALU.pow),
                 r=[bst] + CONST, w=[bst])
            P.op("dve", lambda e: e.tensor_tensor(out=vn[par].rearrange("p (t c) -> p t c", t=4),
                                                  in0=vg2.rearrange("p (t c) -> p t c", t=4),
                                                  in1=rsv.unsqueeze(2).to_broadcast([128, 4, 128]), op=ALU.mult),
                 r=[B_vg2, bst], w=[B_vn[par]])

        def st2_M(g, G):
            par = G % 2
            bm = 6 + par
            for tt in range(4):
                P.op("pe", lambda e: e.matmul(bankf(bm)[:, tt * 128:(tt + 1) * 128],
                                              lhsT=vn[par][:, tt * 128:(tt + 1) * 128], rhs=WT[:, g, :],
                                              start=True, stop=True), r=[B_vn[par]] + CONST, w=[banks[bm]])
            P.op("dve", lambda e: e.scalar_tensor_tensor(
                out=s1.rearrange("p (t c) -> p t c", t=4), in0=bankf(bm).rearrange("p (t c) -> p t c", t=4),
                scalar=vec[:, 22:23], in1=Bsp[:, g, :].unsqueeze(1).to_broadcast([128, 4, 128]),
                op0=ALU.mult, op1=ALU.add), r=[banks[bm]] + CONST, w=[B_s1])
            P.op("dve", lambda e: e.tensor_tensor(out=ST[:, g, G * 512:(G + 1) * 512], in0=s1, in1=uz[par],
                                                  op=ALU.mult), r=[B_s1, B_uz[par]], w=[B_ST])

        load_sgu_w(0)
        items = [(g, G) for g in range(8) for G in range(4)]
        for i, (g, G) in enumerate(items):
            if G == 0 and g + 1 < 8:
                load_sgu_w(g + 1)
            st2_C(g, G)
            st2_D(g, G)
            if i >= 1:
                st2_M(*items[i - 1])
        st2_M(*items[-1])
        if debug:
            P.dma("sp", dbg["d_ST"], ST.rearrange("p g t -> p (g t)"), r=[B_ST], dbuf=dbuf("dst"))

        B_ph2 = B_a1 + B_a2 + [B_a3, B_ug2, B_zs2, B_vg2, B_s1, B_junk2] + B_uz + B_vn
        od = OFF_D
        gsl = []
        psl = []
        sl3 = []
        for i in range(2):
            sl3.append(carve(od, 12288, BF16))
            gsl.append(carve(od, 8192, BF16).rearrange("p (k f c) -> p k f c", k=16, f=2)); od += 8192
            psl.append(carve(od, 4096, BF16).rearrange("p (k f c) -> p k f c", k=8, f=2)); od += 4096
        wosl = []
        for i in range(2):
            wosl.append(carve(od, 8192, BF16).rearrange("p (k c) -> p k c", k=16)); od += 8192
        mT = carve(od, 16384, BF16).rearrange("p (k t) -> p k t", k=16); od += 16384
        xr = []
        for i in range(2):
            xr.append(carve(od, 2048, F32)); od += 2048
        tha = carve(od, 2048, F32); od += 2048
        thb = carve(od, 2048, F32); od += 2048
        t1 = carve(od, 1024, BF16); od += 1024
        t2 = carve(od, 1024, BF16); od += 1024
        assert od <= OFF_F, od
        inh = B_ph2 + B_ws2 + B_ph1 + B_ws + B_ph0
        B_gsl = [Buf(f"gsl{i}", inherit=inh) for i in range(2)]
        B_wo = [Buf(f"wo{i}", inherit=inh) for i in range(2)]
        B_mT = Buf("mT", inherit=inh)
        B_xr = [Buf(f"xr{i}", inherit=inh) for i in range(4)]
        B_tha = Buf("tha", inherit=inh); B_thb = Buf("thb", inherit=inh)
        B_t1 = Buf("t1", inherit=inh); B_t2 = Buf("t2", inherit=inh)

        def load_ct(i):
            ct = i % 16
            if i < 16:
                P.dma("pool", sl3[i % 2], wph3[ct], w=[B_gsl[i % 2]])
                P.dma("sp", wph3_b[ct], sl3[i % 2], r=[B_gsl[i % 2]], w=[B_pre3[ct]], dbuf=B_pre3[ct])
            else:
                P.dma("pool", sl3[i % 2], wph3_b[ct], r=[B_pre3[ct]], w=[B_gsl[i % 2]])

        def load_wo(i):
            wflat = wosl[i % 2].rearrange("p k c -> p (k c)")
            if i < 8:
                P.dma("pool", wflat, wout[i], w=[B_wo[i % 2]])
                P.dma("sp", wout_b[i], wflat, r=[B_wo[i % 2]], w=[B_preo[i]], dbuf=B_preo[i])
            else:
                P.dma("pool", wflat, wout_b[i % 8], r=[B_preo[i % 8]], w=[B_wo[i % 2]])

        load_ct(0)
        load_wo(0)
        ict = 0
        iwo = 0
        ixr = 0
        for G in range(4):
            gs = slice(G * 512, (G + 1) * 512)
            for ct in range(16):
                if ict + 1 < 64:
                    load_ct(ict + 1)
                gw = gsl[ict % 2]; pw = psl[ict % 2]; bw = B_gsl[ict % 2]
                par = ct % 2
                bya, byb, bga, bgb = 4 * par, 4 * par + 1, 4 * par + 2, 4 * par + 3
                for kt in range(8):
                    P.op("pe", lambda e, kt=kt, bya=bya, pw=pw: e.matmul(
                        bankf(bya), lhsT=pw[:, kt, 0, :], rhs=AT[:, kt, gs], start=(kt == 0), stop=(kt == 7)),
                        r=B_AT + [bw], w=[banks[bya]])
                for kt in range(8):
                    P.op("pe", lambda e, kt=kt, byb=byb, pw=pw: e.matmul(
                        bankf(byb), lhsT=pw[:, kt, 1, :], rhs=ST[:, kt, gs], start=(kt == 0), stop=(kt == 7)),
                        r=[B_ST, bw], w=[banks[byb]])
                for kt in range(16):
                    P.op("pe", lambda e, kt=kt, bga=bga, gw=gw: e.matmul(
                        bankf(bga), lhsT=gw[:, kt, 0, :], rhs=hT[:, kt, gs], start=(kt == 0), stop=(kt == 15)),
                        r=[B_hT, bw], w=[banks[bga]])
                for kt in range(16):
                    P.op("pe", lambda e, kt=kt, bgb=bgb, gw=gw: e.matmul(
                        bankf(bgb), lhsT=gw[:, kt, 1, :], rhs=hT[:, kt, gs], start=(kt == 0), stop=(kt == 15)),
                        r=[B_hT, bw], w=[banks[bgb]])
                P.op("act", lambda e, bga=bga: e.activation(out=tha, in_=bankf(bga), func=AF.Tanh, scale=0.5),
                     r=[banks[bga]], w=[B_tha])
                P.op("act", lambda e, bgb=bgb: e.activation(out=thb, in_=bankf(bgb), func=AF.Tanh, scale=0.5),
                     r=[banks[bgb]], w=[B_thb])
                P.op("dve", lambda e, bya=bya: e.scalar_tensor_tensor(out=t1, in0=tha, scalar=1.0, in1=bankf(bya),
                                                                      op0=ALU.add, op1=ALU.mult),
                     r=[B_tha, banks[bya]], w=[B_t1])
                P.op("dve", lambda e, byb=byb: e.scalar_tensor_tensor(out=t2, in0=thb, scalar=1.0, in1=bankf(byb),
                                                                      op0=ALU.add, op1=ALU.mult),
                     r=[B_thb, banks[byb]], w=[B_t2])
                P.op("dve", lambda e, ct=ct: e.tensor_tensor(out=mT[:, ct, :], in0=t1, in1=t2, op=ALU.add),
                     r=[B_t1, B_t2], w=[B_mT])
                ict += 1
            for cb in range(8):
                if iwo + 1 < 32:
                    load_wo(iwo + 1)
                wo = wosl[iwo % 2]; bwo = B_wo[iwo % 2]
                for tt in range(4):
                    row0 = G * 512 + tt * 128
                    xi = ixr % 4
                    xb = xr[xi // 2][:, (xi % 2) * 256:(xi % 2) * 256 + 256]
                    bxb = B_xr[xi]
                    bk = (ixr % 8)
                    P.dma("act", xb, xo[row0:row0 + 128, cb * 256:(cb + 1) * 256], w=[bxb])
                    for kt in range(16):
                        P.op("pe", lambda e, kt=kt, tt=tt, bk=bk, wo=wo: e.matmul(
                            bankf(bk)[:, 0:256], lhsT=mT[:, kt, tt * 128:(tt + 1) * 128], rhs=wo[:, kt, :],
                            start=(kt == 0), stop=(kt == 15)), r=[B_mT, bwo], w=[banks[bk]])
                    P.op("dve", lambda e, bk=bk, xb=xb: e.scalar_tensor_tensor(
                        out=xb, in0=bankf(bk)[:, 0:256], scalar=0.5, in1=xb, op0=ALU.mult, op1=ALU.add),
                        r=[banks[bk], bxb], w=[bxb])
                    P.dma("sp", out_d[row0:row0 + 128, cb * 256:(cb + 1) * 256], xb, r=[bxb], dbuf=bxb)
                    ixr += 1
                iwo += 1
        fin = list(B_xr) + DBG_BUFS
        P.final_wait("sp", fin)

        block = st.enter_context(nc.Block())

        @block.tensor
        def _(e):
            P.emit(e, "pe")

        @block.scalar
        def _(e):
            P.emit(e, "act")

        @block.vector
        def _(e):
            P.emit(e, "dve")

        @block.gpsimd
        def _(e):
            P.emit(e, "pool")

        @block.sync
        def _(e):
            P.emit(e, "sp")
    return nc


def _host_layouts(x, norm_g, w_in, q_norm_g, k_norm_g, sgu_norm_g, w_spatial, b_spatial,
                  w_proj_a, w_proj_b, w_out):
    f32 = np.float32
    w_in = np.asarray(w_in, f32)[0]
    wi = w_in.reshape(16, 128, 11264)

    def fam(c0, n):
        blk = wi[:, :, c0:c0 + n].reshape(16, 128, n // 128, 128)
        return np.transpose(blk, (2, 1, 0, 3))
    q, k, v, za = fam(0, 1024), fam(1024, 1024), fam(2048, 1024), fam(3072, 1024)
    ub, vb, zb = fam(4096, 1024), fam(5120, 1024), fam(6144, 1024)
    ga, gb = fam(7168, 2048), fam(9216, 2048)
    wqkvz = np.ascontiguousarray(np.stack([q, k, v, za], axis=3)).reshape(8, 128, 8192)
    wkvpre = np.ascontiguousarray(np.transpose(np.stack([k, v], axis=3), (1, 0, 2, 3, 4))).reshape(128, 32768)
    wsgu = np.ascontiguousarray(np.stack([ub, vb, zb], axis=3)).reshape(8, 128, 6144)
    wgate = np.ascontiguousarray(np.stack([ga, gb], axis=3)).reshape(16, 128, 4096)
    pa = np.asarray(w_proj_a, f32)[0].reshape(8, 128, 16, 128)
    pb = np.asarray(w_proj_b, f32)[0].reshape(8, 128, 16, 128)
    wproj = np.ascontiguousarray(np.stack([np.transpose(pa, (2, 1, 0, 3)), np.transpose(pb, (2, 1, 0, 3))],
                                          axis=3)).reshape(16, 128, 2048)
    wph3 = np.ascontiguousarray(np.concatenate([wgate, wproj], axis=2))
    wo = np.asarray(w_out, f32)[0].reshape(16, 128, 8, 256)
    wout = np.ascontiguousarray(np.transpose(wo, (2, 1, 0, 3))).reshape(8, 128, 4096)
    vecs = np.zeros((128, 20), f32)
    vecs[:, 0:16] = np.asarray(norm_g, f32)[0].reshape(16, 128).T
    vecs[:, 16] = np.asarray(q_norm_g, f32)[0]
    vecs[:, 17] = np.asarray(k_norm_g, f32)[0]
    vecs[:, 18] = np.asarray(sgu_norm_g, f32)[0]
    bsp = np.ascontiguousarray(np.broadcast_to(np.asarray(b_spatial, f32)[0].reshape(1, 1024), (128, 1024)))
    wspT = np.ascontiguousarray(np.transpose(np.asarray(w_spatial, f32)[0], (2, 0, 1)))
    ident = np.eye(128, dtype=f32)
    tri = np.triu(np.ones((128, 128), f32))
    shared = dict(wqkvz=wqkvz, wkvpre=wkvpre, wsgu=wsgu, wph3=wph3, wout=wout, vecs=vecs, bsp=bsp,
                  wspT=wspT, ident=ident, tri=tri)
    x = np.asarray(x, f32)
    in_maps = []
    for c in range(8):
        b, j = c // 2, c % 2
        gb_ = np.full((16, 16), -1e30, f32)
        for i in range(16):
            lo = 0 if j == 1 else 8
            gb_[i, lo:8 + i // 2] = 0.0
        m = dict(shared)
        m["xo"] = np.ascontiguousarray(x[b, j * 2048:(j + 1) * 2048])
        m["xp"] = np.ascontiguousarray(x[b, 0:2048])
        m["gbias"] = np.ascontiguousarray(np.broadcast_to(gb_.reshape(1, 256), (128, 256)))
        in_maps.append(m)
    return in_maps


_NC_CACHE = {}


def kernel(x, norm_g, w_in, q_norm_g, k_norm_g, sgu_norm_g, w_spatial, b_spatial, w_proj_a, w_proj_b, w_out,
           _debug=False):
    in_maps = _host_layouts(x, norm_g, w_in, q_norm_g, k_norm_g, sgu_norm_g, w_spatial, b_spatial,
                            w_proj_a, w_proj_b, w_out)
    nc = build_program(debug=_debug)
    res = run_bass_kernel_spmd(nc, in_maps, core_ids=list(range(8)))
    out = np.empty((4, 4096, 2048), np.float32)
    for c in range(8):
        b, j = c // 2, c % 2
        out[b, j * 2048:(j + 1) * 2048] = res.results[c]["out"]
    if _debug:
        return out, res
    return out
```

```python
import numpy as np
from contextlib import ExitStack
import concourse.bass as bass
import concourse.mybir as mybir
from concourse.bass_utils import run_bass_kernel_spmd

F32 = mybir.dt.float32
BF16 = mybir.dt.bfloat16
U8 = mybir.dt.uint8
ALU = mybir.AluOpType
AF = mybir.ActivationFunctionType
AX = mybir.AxisListType

D = 2048
NH = 8
DH = 128
S_OWN = 2048
NT = 16
EPS = 1e-6
C1 = 0.7978845608028654
C2 = 0.044715
SQ128 = float(np.sqrt(128.0))


class Buf:
    def __init__(self, name, inherit=()):
        self.name = name
        self.writers = {}
        self.readers = {}
        self.dsem = None
        self.dcount = 0
        for b in inherit:
            for d in (b.writers, b.readers):
                for k, v in d.items():
                    if self.readers.get(k, (None, 0))[1] < v[1]:
                        self.readers[k] = v


class _Rec:
    def __init__(self):
        self.call = None

    def __getattr__(self, name):
        def f(*args, **kwargs):
            self.call = (name, args, kwargs)
        return f


class Eng:
    def __init__(self, name, sem):
        self.name = name
        self.sem = sem
        self.n = 0
        self.ops = []
        self.seen = {}


class Prog:
    def __init__(self, nc, st):
        self.nc = nc
        self.st = st
        self.E = {}
        for n in ("pe", "act", "dve", "pool", "sp"):
            self.E[n] = Eng(n, st.enter_context(nc.semaphore("s_" + n)))
        self.nsem = 5

    def _waits(self, eng, r, w, is_dma=False):
        E = self.E[eng]
        need = {}

        def add(d, raw, skip_dma=False):
            for k, (sem, val) in d.items():
                if skip_dma and k.startswith("dma_"):
                    continue
                if k == eng:
                    if eng == "pe" or not raw:
                        continue
                if need.get(k, (None, 0))[1] < val:
                    need[k] = (sem, val)
        for b in r:
            add(b.writers, True)
        for b in w:
            add(b.readers, False)
            add(b.writers, False, skip_dma=is_dma)
        out = []
        for k, (sem, val) in need.items():
            if E.seen.get(k, 0) >= val:
                continue
            E.seen[k] = val
            out.append((sem, val))
        return out

    def _record(self, key, tok, r, w):
        for b in r:
            b.readers[key] = tok
        for b in w:
            if b.readers:
                b.readers = {}
                b.writers = {}
            b.writers[key] = tok

    def op(self, eng, fn, r=(), w=()):
        E = self.E[eng]
        waits = self._waits(eng, r, w)
        E.n += 1
        tok = (E.sem, E.n)
        rec = _Rec()
        fn(rec)
        name, args, kwargs = rec.call
        E.ops.append((waits, lambda e: getattr(e, name)(*args, **kwargs), E.sem, 1))
        self._record(eng, tok, r, w)

    def dma(self, eng, out, in_, r=(), w=(), dbuf=None):
        E = self.E[eng]
        b = dbuf if dbuf is not None else (w[0] if w else r[0])
        if b.dsem is None:
            b.dsem = self.st.enter_context(self.nc.semaphore("d_" + b.name))
            self.nsem += 1
        waits = self._waits(eng, r, w, is_dma=True)
        b.dcount += 16
        tok = (b.dsem, b.dcount)
        E.ops.append((waits, lambda e: e.dma_start(out=out, in_=in_), b.dsem, 16))
        self._record("dma_" + b.name, tok, r, w)

    def final_wait(self, eng, bufs):
        E = self.E[eng]
        for b in bufs:
            E.ops.append(([(b.dsem, b.dcount)], None, None, 0))

    def emit(self, e, name):
        for waits, fn, sem, inc in self.E[name].ops:
            for s, v in waits:
                e.wait_ge(s, v)
            if fn is None:
                continue
            ins = fn(e)
            ins.then_inc(sem, inc)


def build_program(debug=False):
    nc = bass.Bass("TRN2", target_bir_lowering=False)

    def din(name, shape):
        return nc.dram_tensor(name, list(shape), F32, kind="ExternalInput").ap()

    xo = din("xo", (S_OWN, D))
    xp = din("xp", (S_OWN, D))
    wqkvz = din("wqkvz", (NH, 128, 8192))
    wkvpre = din("wkvpre", (128, 32768))
    wsgu = din("wsgu", (8, 128, 6144))
    wph3 = din("wph3", (16, 128, 6144))
    wout = din("wout", (8, 128, 4096))
    wph3_b = nc.dram_tensor("wph3_b", [16, 128, 6144], BF16, kind="Internal").ap()
    wout_b = nc.dram_tensor("wout_b", [8, 128, 4096], BF16, kind="Internal").ap()
    vecs = din("vecs", (128, 20))
    gbias_d = din("gbias", (128, 256))
    bsp_d = din("bsp", (128, 1024))
    wspT_d = din("wspT", (128, 8, 128))
    ident_d = din("ident", (128, 128))
    tri_d = din("tri", (128, 128))
    out_d = nc.dram_tensor("out", [S_OWN, D], F32, kind="ExternalOutput").ap()
    dbg = {}
    if debug:
        for nm, shp in (("d_hT", (128, 16 * 2048)), ("d_AT", (128, 8 * 2048)), ("d_ST", (128, 8 * 2048)),
                        ("d_KTpre", (128, 8 * 2048)), ("d_QK", (128, 2 * 2048)), ("d_sel", (128, 256)),
                        ("d_gm", (128, 256)), ("d_V", (128, 16 * 129)), ("d_za", (128, 2048))):
            dbg[nm] = nc.dram_tensor(nm, list(shp), BF16 if nm not in ("d_sel", "d_gm") else F32,
                                     kind="ExternalOutput").ap()

    with ExitStack() as st:
        ARENA_BYTES = 212480
        arena = st.enter_context(nc.sbuf_tensor("arena", [128, ARENA_BYTES], U8))
        psum_all = st.enter_context(nc.psum_tensor("ps", [128, 4096], F32))
        P = Prog(nc, st)

        def carve(off, nbytes, dt):
            assert off % 32 == 0 and off + nbytes <= ARENA_BYTES, (off, nbytes)
            return arena[:, off:off + nbytes].bitcast(dt)

        OFF_A = 0
        OFF_B = 65536
        OFF_C = 98304
        OFF_D = 131584
        OFF_E = 164352
        OFF_F = 200192
        hT = carve(OFF_A, 65536, BF16).rearrange("p (k t) -> p k t", k=16)
        Wkv = carve(OFF_A, 65536, BF16).rearrange("p (h k c) -> p h k c", h=8, k=16)
        KTpre = carve(OFF_B, 32768, BF16).rearrange("p (h t) -> p h t", h=8)
        AT = KTpre
        Vpre = carve(OFF_C, 33024, BF16).rearrange("p (t h c) -> p t h c", t=16, h=8)
        ST = carve(OFF_C, 32768, BF16).rearrange("p (g t) -> p g t", g=8)

        o = OFF_F
        ident = carve(o, 256, BF16); o += 256
        tri = carve(o, 256, BF16); o += 256
        WT = carve(o, 2048, BF16).rearrange("p (g t) -> p g t", g=8); o += 2048
        Bsp = carve(o, 4096, F32).rearrange("p (g t) -> p g t", g=8); o += 4096
        gbias = carve(o, 1024, F32); o += 1024
        vec = carve(o, 96, F32); o += 96
        m05 = carve(o, 64, F32); o += 64
        stat = carve(o, 512, F32); o += 512
        km = carve(o, 64, F32); o += 64
        kmb = carve(o, 32, BF16); o += 32
        gm = carve(o, 1024, F32).rearrange("p (c b) -> p c b", c=16); o += 1024
        sel = carve(o, 1024, F32).rearrange("p (c b) -> p c b", c=16); o += 1024
        m8 = carve(o, 512, F32).rearrange("p (c b) -> p c b", c=16); o += 512
        thr = carve(o, 64, F32); o += 64
        assert o <= ARENA_BYTES, o

        DBG_BUFS = []

        def dbuf(n):
            b = Buf(n)
            DBG_BUFS.append(b)
            return b

        B_const = Buf("const")
        B_Aq = [Buf(f"A{i}") for i in range(4)]
        banks = [Buf(f"bank{i}") for i in range(8)]

        def bankf(i):
            return psum_all[:, i * 512:(i + 1) * 512]

        def bankb(i):
            return psum_all[:, i * 512:(i + 1) * 512].bitcast(BF16)

        B_constp = Buf("constp")
        P.dma("sp", vec[:, 0:20], vecs, w=[B_const])
        P.dma("sp", gbias, gbias_d, w=[B_const])
        P.dma("sp", Bsp.rearrange("p g t -> p (g t)"), bsp_d, w=[B_const])
        P.dma("pool", ident, ident_d, w=[B_constp])
        P.dma("pool", tri, tri_d, w=[B_constp])
        P.dma("pool", WT.rearrange("p g t -> p (g t)"), wspT_d.rearrange("p g t -> p (g t)"), w=[B_constp])
        B_c2 = Buf("const2")
        P.op("dve", lambda e: e.tensor_scalar(out=vec[:, 20:23], in0=vec[:, 16:19], scalar1=SQ128, scalar2=None,
                                              op0=ALU.mult), r=[B_const], w=[B_c2])
        P.op("pool", lambda e: e.memset(m05, -0.5), w=[B_c2])
        P.op("dve", lambda e: e.tensor_tensor(out=WT, in0=WT, in1=tri.unsqueeze(1).to_broadcast([128, 8, 128]),
                                              op=ALU.mult), r=[B_constp], w=[B_c2])
        CONST = [B_const, B_constp, B_c2]

        od = OFF_E
        xt = [carve(od + i * 8192, 8192, F32) for i in range(2)]; od += 16384
        xn = [carve(od + i * 4096, 4096, BF16) for i in range(2)]; od += 8192
        hTt = [carve(od + i * 4096, 4096, BF16).rearrange("p (k t) -> p k t", k=16) for i in range(2)]; od += 8192
        junk = carve(od, 256, BF16); od += 256
        assert od <= OFF_F
        kn = [carve(OFF_D + 28672 + i * 2048, 2048, BF16).rearrange("p (h c) -> p h c", h=8) for i in range(2)]
        B_xt = [Buf("xt0"), Buf("xt1")]
        B_xn = [Buf("xn0"), Buf("xn1")]; B_hTt = [Buf("hTt0"), Buf("hTt1")]; B_kn = [Buf("kn0"), Buf("kn1")]
        B_junk = Buf("junk")
        B_st0 = [Buf("st0a"), Buf("st0b")]; B_stk = [Buf(f"stk{i}") for i in range(8)]
        B_KTpre = [Buf(f"KTpre{h}") for h in range(8)]; B_Vpre = Buf("Vpre")

        P.op("pool", lambda e: e.memset(Vpre[:, :, :, 128:129], 1.0), w=[B_Vpre])
        for q4 in range(4):
            P.dma("pool", carve(OFF_A + q4 * 16384, 16384, BF16), wkvpre[:, q4 * 8192:(q4 + 1) * 8192], w=[B_Aq[q4]])
        B_hT = None

        def st_A(src_dram, t):
            xb = xt[t % 2]; bx = B_xt[t % 2]; xnb = xn[t % 2]; bxn = B_xn[t % 2]
            P.dma("sp", xb, src_dram[(t % 16) * 128:(t % 16 + 1) * 128, :], w=[bx])
            ss = stat[:, 2 * (t % 2):2 * (t % 2) + 1]; rs = stat[:, 2 * (t % 2) + 1:2 * (t % 2) + 2]
            bst = B_st0[t % 2]
            P.op("act", lambda e: e.activation(out=xnb, in_=xb, func=AF.Square, accum_out=ss), r=[bx], w=[bxn, bst])
            P.op("dve", lambda e: e.tensor_scalar(out=ss, in0=ss, scalar1=1.0 / D, scalar2=EPS, op0=ALU.mult,
                                                  op1=ALU.add), r=[bst], w=[bst])
            P.op("pool", lambda e: e.tensor_tensor(out=rs, in0=ss, in1=m05[:, 0:1], op=ALU.pow),
                 r=[bst] + CONST, w=[bst])
            P.op("dve", lambda e: e.tensor_scalar(out=xnb, in0=xb, scalar1=rs, scalar2=None, op0=ALU.mult),
                 r=[bx, bst], w=[bxn])

        def st_B(t, dst_ap, dst_buf):
            xnb = xn[t % 2]; bxn = B_xn[t % 2]
            for half in range(2):
                bk = half
                for j in range(8):
                    kt = half * 8 + j
                    P.op("pe", lambda e: e.transpose(out=bankb(bk)[:, j * 128:(j + 1) * 128],
                                                     in_=xnb[:, kt * 128:(kt + 1) * 128], identity=ident),
                         r=[bxn] + CONST, w=[banks[bk]])
                P.op("dve", lambda e: e.tensor_tensor(
                    out=dst_ap[:, half * 8:(half + 1) * 8, :], in0=bankb(bk).rearrange("p (k t) -> p k t", k=8),
                    in1=vec[:, half * 8:(half + 1) * 8].unsqueeze(2).to_broadcast([128, 8, 128]), op=ALU.mult),
                    r=[banks[bk]] + CONST, w=[dst_buf])

        def st_CD(t):
            hb = hTt[t % 2]; bhb = B_hTt[t % 2]; knb = kn[t % 2]; bkn = B_kn[t % 2]
            for b2 in range(4):
                bk = 2 + b2
                for hh in range(2):
                    h = 2 * b2 + hh
                    for kt in range(16):
                        P.op("pe", lambda e: e.matmul(bankf(bk)[:, hh * 256:hh * 256 + 256], lhsT=hb[:, kt, :],
                                                      rhs=Wkv[:, h, kt, :], start=(kt == 0), stop=(kt == 15)),
                             r=[bhb, B_Aq[b2]], w=[banks[bk]])
                so = 8 + ((t % 2) * 4 + b2) * 4
                ssk = stat[:, so:so + 2]; rsk = stat[:, so + 2:so + 4]; bst = B_stk[(t % 2) * 4 + b2]
                pv = bankf(bk).rearrange("p (h c) -> p h c", h=2)
                for hh in range(2):
                    P.op("act", lambda e: e.activation(out=junk, in_=pv[:, hh, 0:128], func=AF.Square,
                                                       accum_out=ssk[:, hh:hh + 1]), r=[banks[bk]], w=[B_junk, bst])
                P.op("dve", lambda e: e.tensor_scalar(out=ssk, in0=ssk, scalar1=128.0 * EPS, scalar2=None,
                                                      op0=ALU.add), r=[bst], w=[bst])
                P.op("pool", lambda e: e.tensor_tensor(out=rsk, in0=ssk, in1=m05[:, 0:2], op=ALU.pow),
                     r=[bst] + CONST, w=[bst])
                P.op("dve", lambda e: e.tensor_tensor(
                    out=knb[:, 2 * b2:2 * b2 + 2, :], in0=pv[:, :, 0:128],
                    in1=rsk.unsqueeze(2).to_broadcast([128, 2, 128]), op=ALU.mult), r=[banks[bk], bst], w=[bkn])
                P.op("act", lambda e: e.copy(out=Vpre[:, t, 2 * b2:2 * b2 + 2, 0:128], in_=pv[:, :, 128:256]),
                     r=[banks[bk]], w=[B_Vpre])

        def st_E(t):
            knb = kn[t % 2]; bkn = B_kn[t % 2]
            for h in range(8):
                P.op("pe", lambda e: e.transpose(out=bankb(6)[:, h * 128:(h + 1) * 128], in_=knb[:, h, :],
                                                 identity=ident), r=[bkn] + CONST, w=[banks[6]])
            P.op("dve", lambda e: e.tensor_scalar(
                out=KTpre[:, :, t * 128:(t + 1) * 128], in0=bankb(6).rearrange("p (h t) -> p h t", h=8),
                scalar1=vec[:, 21:22], scalar2=None, op0=ALU.mult), r=[banks[6]] + CONST, w=B_KTpre)

        st_A(xp, 0)
        st_B(0, hTt[0], B_hTt[0])
        st_A(xp, 1)
        for t in range(NT):
            if t + 1 < NT:
                st_B(t + 1, hTt[(t + 1) % 2], B_hTt[(t + 1) % 2])
            if t + 2 < NT:
                st_A(xp, t + 2)
            elif t + 2 == NT:
                st_A(xo, 16)
            st_CD(t)
            if t >= 1:
                st_E(t - 1)
        st_E(NT - 1)
        B_hT = Buf("hT", inherit=B_Aq)
        for t in range(16, 32):
            if t + 1 < 32:
                st_A(xo, t + 1)
            st_B(t, hT[:, :, (t - 16) * 128:(t - 15) * 128], B_hT)

        B_pre3 = [Buf(f"pre3_{i}") for i in range(16)]; B_preo = [Buf(f"preo_{i}") for i in range(8)]
        wslot = [carve(OFF_D + i * 16384, 16384, BF16).rearrange("p (k f c) -> p k f c", k=16, f=4) for i in range(2)]
        B_ws = [Buf("ws0"), Buf("ws1", inherit=B_kn)]
        oe = OFF_E
        B_ph0 = B_xt + B_xn + B_hTt + B_kn + [B_junk]
        QK = carve(oe, 8192, BF16).rearrange("p (f t) -> p f t", f=2); oe += 8192
        Vown = carve(oe, 4160, BF16)[:, 0:16 * 129].rearrange("p (t c) -> p t c", t=16); oe += 4160
        zaT = carve(oe, 4096, BF16); oe += 4096
        PT = [carve(oe + i * 2048, 2048, BF16).rearrange("p (j q) -> p j q", j=2) for i in range(3)]; oe += 6144
        acc = [carve(oe + i * 2080, 2064, F32).rearrange("p (c d) -> p c d", c=4) for i in range(2)]; oe += 4160
        qkn = [carve(oe + i * 512, 512, BF16).rearrange("p (f c) -> p f c", f=2) for i in range(3)]; oe += 1536
        attn = carve(oe, 1024, BF16).rearrange("p (c d) -> p c d", c=4); oe += 1024
        th = carve(oe, 2048, F32); oe += 2048
        rinv = carve(oe, 32, F32); oe += 32
        junk1 = carve(oe, 256, BF16); oe += 256
        assert oe <= OFF_E + 35840, oe
        B_QK = Buf("QK", inherit=B_ph0); B_Vown = Buf("Vown", inherit=B_ph0); B_zaT = Buf("zaT", inherit=B_ph0)
        B_PT = [Buf(f"PT{i}", inherit=B_ph0) for i in range(3)]
        B_acc = [[Buf(f"acc{i}_{c}", inherit=B_ph0) for c in range(4)] for i in range(2)]
        B_qkn = [Buf(f"qkn{i}", inherit=B_ph0) for i in range(3)]
        B_st2 = [Buf(f"st2_{i}") for i in range(3)]
        B_attn = Buf("attn", inherit=B_ph0); B_th = Buf("th", inherit=B_ph0); B_rinv = Buf("rinv", inherit=B_ph0)
        B_junk1 = Buf("junk1", inherit=B_ph0)
        B_gate = Buf("gate")
        B_AT = B_KTpre

        P.op("pool", lambda e: e.memset(Vown[:, :, 128:129], 1.0), w=[B_Vown])

        def load_head_w(h):
            s = h % 2
            P.dma("pool", carve(OFF_D + s * 16384, 16384, BF16), wqkvz[h], w=[B_ws[s]])

        load_head_w(0)
        pt_rr = 0
        for h in range(NH):
            if h + 1 < NH:
                load_head_w(h + 1)
            ws = wslot[h % 2]; bws = B_ws[h % 2]
            def st_C(t):
                bk = t % 3
                for kt in range(16):
                    P.op("pe", lambda e: e.matmul(
                        bankf(bk)[:, 0:384], lhsT=hT[:, kt, t * 128:(t + 1) * 128],
                        rhs=ws[:, kt, 0:3, :].rearrange("p f c -> p (f c)"), start=(kt == 0), stop=(kt == 15)),
                        r=[B_hT, bws], w=[banks[bk]])
                so = 72 + (t % 3) * 4
                ss2 = stat[:, so:so + 2]; rs2 = stat[:, so + 2:so + 4]; bst = B_st2[t % 3]
                for f in range(2):
                    P.op("act", lambda e: e.activation(out=junk1, in_=bankf(bk)[:, f * 128:(f + 1) * 128],
                                                       func=AF.Square, accum_out=ss2[:, f:f + 1]),
                         r=[banks[bk]], w=[B_junk1, bst])
                P.op("dve", lambda e: e.tensor_scalar(out=ss2, in0=ss2, scalar1=128.0 * EPS, scalar2=None,
                                                      op0=ALU.add), r=[bst], w=[bst])
                P.op("pool", lambda e: e.tensor_tensor(out=rs2, in0=ss2, in1=m05[:, 0:2], op=ALU.pow),
                     r=[bst] + CONST, w=[bst])
                qn = qkn[t % 3]; bqn = B_qkn[t % 3]
                P.op("dve", lambda e: e.tensor_tensor(
                    out=qn, in0=bankf(bk)[:, 0:256].rearrange("p (f c) -> p f c", f=2),
                    in1=rs2.unsqueeze(2).to_broadcast([128, 2, 128]), op=ALU.mult), r=[banks[bk], bst], w=[bqn])
                P.op("act", lambda e: e.copy(out=Vown[:, t, 0:128], in_=bankf(bk)[:, 256:384]),
                     r=[banks[bk]], w=[B_Vown])

            def st_Eq(t):
                qn = qkn[t % 3]; bqn = B_qkn[t % 3]
                for f in range(2):
                    P.op("pe", lambda e: e.transpose(out=bankb(3)[:, f * 128:(f + 1) * 128], in_=qn[:, f, :],
                                                     identity=ident), r=[bqn] + CONST, w=[banks[3]])
                P.op("dve", lambda e: e.tensor_tensor(
                    out=QK[:, :, t * 128:(t + 1) * 128], in0=bankb(3)[:, 0:256].rearrange("p (f c) -> p f c", f=2),
                    in1=vec[:, 20:22].unsqueeze(2).to_broadcast([128, 2, 128]), op=ALU.mult),
                    r=[banks[3]] + CONST, w=[B_QK])

            def st_za(G):
                bk = 4 + G % 2
                for kt in range(16):
                    P.op("pe", lambda e: e.matmul(bankf(bk), lhsT=ws[:, kt, 3, :], rhs=hT[:, kt, G * 512:(G + 1) * 512],
                                                  start=(kt == 0), stop=(kt == 15)), r=[B_hT, bws], w=[banks[bk]])
                P.op("act", lambda e: e.activation(out=th, in_=bankf(bk), func=AF.Tanh, scale=0.5),
                     r=[banks[bk]], w=[B_th])
                P.op("dve", lambda e: e.scalar_tensor_tensor(
                    out=zaT[:, G * 512:(G + 1) * 512], in0=th, scalar=1.0, in1=bankf(bk), op0=ALU.add, op1=ALU.mult),
                    r=[B_th, banks[bk]], w=[B_zaT])

            for t in range(NT):
                st_C(t)
                if t >= 2:
                    st_Eq(t - 2)
                if t % 4 == 3:
                    st_za(t // 4)
            st_Eq(NT - 2)
            st_Eq(NT - 1)
            def ktile(j):
                return KTpre[:, h, j * 128:(j + 1) * 128] if j < 16 else QK[:, 1, (j - 16) * 128:(j - 15) * 128]

            def vtile(j):
                return Vpre[:, j, h, :] if j < 16 else Vown[:, j - 16, :]

            allsteps = []
            for m in range(4):
                steps = []
                for b in range(8 + 2 * m):
                    steps.append(dict(tiles=[(2 * b, 0), (2 * b + 1, 0)],
                                      chunks={c: [0, 1] for c in range(4)}, blk={c: b for c in range(4)}, diag=[]))
                r0 = 16 + 4 * m
                steps.append(dict(tiles=[(r0, 0), (r0 + 1, 1)],
                                  chunks={0: [0], 1: [0, 1], 2: [0, 1], 3: [0, 1]},
                                  blk={0: None, 1: None, 2: 8 + 2 * m, 3: 8 + 2 * m}, diag=[(0, 0), (1, 1)]))
                steps.append(dict(tiles=[(r0 + 2, 2), (r0 + 3, 3)],
                                  chunks={2: [0], 3: [0, 1]}, blk={2: None, 3: None}, diag=[(0, 2), (1, 3)]))
                for k_, sp_ in enumerate(steps):
                    sp_["m"] = m
                    sp_["first"] = (k_ == 0)
                    sp_["last"] = (k_ == len(steps) - 1)
                allsteps += steps
            for si, sp_ in enumerate(allsteps):
                sp_["pt"] = (pt_rr + si) % 3
                sp_["sb"] = 0 + 2 * (si % 2)
                sp_["ob"] = 4 + 2 * (si % 2)
            pt_rr += len(allsteps)

            def emit_qk(sp_):
                m = sp_["m"]
                pt = PT[sp_["pt"]]; bpt = B_PT[sp_["pt"]]
                sb = sp_["sb"]
                for idx, (j, c0) in enumerate(sp_["tiles"]):
                    bk = sb + idx
                    n = (4 - c0) * 128
                    P.op("pe", lambda e: e.matmul(
                        bankf(bk)[:, 0:n], lhsT=ktile(j), rhs=QK[:, 0, m * 512 + c0 * 128:(m + 1) * 512],
                        start=True, stop=True), r=[B_KTpre[h], B_QK], w=[banks[bk]])
                if all(c0 == 0 for (_, c0) in sp_["tiles"]):
                    P.op("act", lambda e: e.activation(
                        out=pt, in_=psum_all[:, sb * 512:(sb + 2) * 512].rearrange("p (j q) -> p j q", j=2),
                        func=AF.Exp, scale=float(DH ** -0.5)), r=[banks[sb], banks[sb + 1]], w=[bpt])
                else:
                    for idx, (j, c0) in enumerate(sp_["tiles"]):
                        bk = sb + idx
                        n = (4 - c0) * 128
                        P.op("act", lambda e: e.activation(
                            out=pt[:, idx, c0 * 128:512], in_=bankf(bk)[:, 0:n], func=AF.Exp,
                            scale=float(DH ** -0.5)), r=[banks[bk]], w=[bpt])
                for (idx, c) in sp_["diag"]:
                    P.op("pool", lambda e: e.tensor_tensor(
                        out=pt[:, idx, c * 128:(c + 1) * 128], in0=pt[:, idx, c * 128:(c + 1) * 128], in1=tri,
                        op=ALU.mult), r=[bpt] + CONST, w=[bpt])

            def emit_pv(sp_):
                m = sp_["m"]
                ac = acc[m % 2]
                pt = PT[sp_["pt"]]; bpt = B_PT[sp_["pt"]]
                for c, tl in sp_["chunks"].items():
                    bk = sp_["ob"] + c // 2
                    po = bankf(bk)[:, (c % 2) * 129:(c % 2) * 129 + 129]
                    for ii, idx in enumerate(tl):
                        j = sp_["tiles"][idx][0]
                        P.op("pe", lambda e: e.matmul(po, lhsT=pt[:, idx, c * 128:(c + 1) * 128], rhs=vtile(j),
                                                      start=(ii == 0), stop=(ii == len(tl) - 1)),
                             r=[bpt, B_Vpre, B_Vown], w=[banks[bk]])
                for c, tl in sp_["chunks"].items():
                    bk = sp_["ob"] + c // 2
                    po = bankf(bk)[:, (c % 2) * 129:(c % 2) * 129 + 129]
                    b = sp_["blk"][c]
                    bac = B_acc[m % 2][c]
                    sc = 1.0 if b is None else sel[:, 4 * m + c, b:b + 1]
                    if sp_["first"]:
                        P.op("dve", lambda e: e.tensor_scalar(out=ac[:, c, :], in0=po, scalar1=sc, scalar2=None,
                                                              op0=ALU.mult), r=[banks[bk], B_gate], w=[bac])
                    else:
                        P.op("dve", lambda e: e.scalar_tensor_tensor(
                            out=ac[:, c, :], in0=po, scalar=sc, in1=ac[:, c, :], op0=ALU.mult, op1=ALU.add),
                            r=[banks[bk], B_gate, bac], w=[bac])

            def fin_dve(m):
                ac = acc[m % 2]; bac = B_acc[m % 2]
                P.op("dve", lambda e: e.tensor_scalar(out=rinv[:, 0:4], in0=ac[:, :, 128], scalar1=2.0, scalar2=None,
                                                      op0=ALU.mult), r=bac, w=[B_rinv])
                P.op("dve", lambda e: e.reciprocal(out=rinv[:, 4:8], in_=rinv[:, 0:4]), r=[B_rinv], w=[B_rinv])
                P.op("dve", lambda e: e.tensor_tensor(out=attn, in0=ac[:, :, 0:128],
                                                      in1=rinv[:, 4:8].unsqueeze(2).to_broadcast([128, 4, 128]),
                                                      op=ALU.mult), r=bac + [B_rinv], w=[B_attn])

            def fin_pe(m, bk):
                for c in range(4):
                    P.op("pe", lambda e: e.transpose(out=bankb(bk)[:, c * 128:(c + 1) * 128], in_=attn[:, c, :],
                                                     identity=ident), r=[B_attn] + CONST, w=[banks[bk]])
                P.op("dve", lambda e: e.tensor_tensor(out=zaT[:, m * 512:(m + 1) * 512], in0=bankb(bk)[:, 0:512],
                                                      in1=zaT[:, m * 512:(m + 1) * 512], op=ALU.mult),
                     r=[banks[bk], B_zaT], w=[B_zaT])

            P.op("dve", lambda e: e.tensor_reduce(out=km[:, 0:8], in_=KTpre[:, h, :].rearrange("p (b k) -> p b k", b=8),
                                                  axis=AX.X, op=ALU.add), r=[B_KTpre[h]], w=[B_gate])
            P.op("dve", lambda e: e.tensor_reduce(out=km[:, 8:16], in_=QK[:, 1, :].rearrange("p (b k) -> p b k", b=8),
                                                  axis=AX.X, op=ALU.add), r=[B_QK], w=[B_gate])
            P.op("dve", lambda e: e.tensor_scalar(out=kmb, in0=km, scalar1=1.0 / 256.0, scalar2=None, op0=ALU.mult),
                 r=[B_gate], w=[B_gate])
            emit_qk(allsteps[0])
            emit_qk(allsteps[1])
            for c in range(16):
                P.op("pe", lambda e: e.matmul(bankf(6)[:, c * 16:(c + 1) * 16], lhsT=QK[:, 0, c * 128:(c + 1) * 128],
                                              rhs=kmb, start=True, stop=True), r=[B_QK, B_gate], w=[banks[6]])
            P.op("dve", lambda e: e.tensor_tensor(out=gm.rearrange("p c b -> p (c b)"), in0=bankf(6)[:, 0:256],
                                                  in1=gbias, op=ALU.add), r=[banks[6]] + CONST, w=[B_gate])
            for c in range(16):
                P.op("dve", lambda e: e.max(out=m8[:, c, :], in_=gm[:, c, :]), r=[B_gate], w=[B_gate])
            P.op("dve", lambda e: e.tensor_scalar(out=thr, in0=m8[:, :, 2], scalar1=-1e29, scalar2=None, op0=ALU.max),
                 r=[B_gate], w=[B_gate])
            P.op("dve", lambda e: e.tensor_tensor(out=sel, in0=gm, in1=thr.unsqueeze(2).to_broadcast([128, 16, 16]),
                                                  op=ALU.is_ge), r=[B_gate], w=[B_gate])
            if debug and h == 0:
                P.dma("sp", dbg["d_QK"], QK.rearrange("p f t -> p (f t)"), r=[B_QK], dbuf=dbuf("dq"))
                P.dma("sp", dbg["d_sel"], sel.rearrange("p c b -> p (c b)"), r=[B_gate], dbuf=dbuf("ds"))
                P.dma("sp", dbg["d_gm"], gm.rearrange("p c b -> p (c b)"), r=[B_gate], dbuf=dbuf("dg"))
                P.dma("sp", dbg["d_V"], Vown.rearrange("p t c -> p (t c)"), r=[B_Vown], dbuf=dbuf("dv"))

            pend = None
            for si, sp_ in enumerate(allsteps):
                if si + 2 < len(allsteps):
                    emit_qk(allsteps[si + 2])
                emit_pv(sp_)
                if pend is not None:
                    fin_pe(pend, 4 + 2 * ((si + 1) % 2))
                    pend = None
                if sp_["last"]:
                    fin_dve(sp_["m"])
                    pend = sp_["m"]
            fin_pe(pend, 4)
            P.op("pool", lambda e, h=h: e.tensor_copy(out=AT[:, h, :], in_=zaT), r=[B_zaT], w=[B_KTpre[h]])

        if debug:
            P.dma("sp", dbg["d_hT"], hT.rearrange("p k t -> p (k t)"), r=[B_hT], dbuf=dbuf("dh"))
            P.dma("sp", dbg["d_AT"], AT.rearrange("p h t -> p (h t)"), r=B_AT, dbuf=dbuf("da"))

        B_ph1 = [B_QK, B_Vown, B_zaT] + B_PT + B_acc[0] + B_acc[1] + B_qkn + [B_attn, B_th, B_rinv, B_junk1]
        wsl2 = [carve(OFF_D + i * 16384, 12288, BF16).rearrange("p (k f c) -> p k f c", k=16, f=3) for i in range(2)]
        B_ws2 = [Buf("ws2_0", inherit=[B_ws[0]]), Buf("ws2_1", inherit=[B_ws[1]])]
        oe = OFF_E
        a1 = [carve(oe + i * 2048, 2048, F32) for i in range(2)]; oe += 4096
        a2 = [carve(oe + i * 2048, 2048, F32) for i in range(2)]; oe += 4096
        a3 = carve(oe, 2048, F32); oe += 2048
        ug2 = carve(oe, 1024, BF16); oe += 1024
        zs2 = carve(oe, 1024, BF16); oe += 1024
        uz = [carve(oe + i * 1024, 1024, BF16) for i in range(2)]; oe += 2048
        vg2 = carve(oe, 2048, F32); oe += 2048
        vn = [carve(oe + i * 1024, 1024, BF16) for i in range(2)]; oe += 2048
        s1 = carve(oe, 2048, F32); oe += 2048
        junk2 = carve(oe, 256, BF16); oe += 256
        assert oe <= OFF_E + 35840
        B_a1 = [Buf(f"a1_{i}", inherit=B_ph1) for i in range(2)]
        B_a2 = [Buf(f"a2_{i}", inherit=B_ph1) for i in range(2)]
        B_a3 = Buf("a3", inherit=B_ph1); B_ug2 = Buf("ug2", inherit=B_ph1); B_zs2 = Buf("zs2", inherit=B_ph1)
        B_uz = [Buf(f"uz{i}", inherit=B_ph1) for i in range(2)]; B_vg2 = Buf("vg2", inherit=B_ph1)
        B_vn = [Buf(f"vn{i}", inherit=B_ph1) for i in range(2)]
        B_s1 = Buf("s1", inherit=B_ph1); B_junk2 = Buf("junk2", inherit=B_ph1)
        B_ST = Buf("ST", inherit=[B_Vpre])
        B_stv = [Buf("stva"), Buf("stvb")]

        def gelu2(bk, i, dst, dst_buf):
            P.op("act", lambda e: e.activation(out=a1[i], in_=bankf(bk), func=AF.Square, scale=float(np.sqrt(C2))),
                 r=[banks[bk]], w=[B_a1[i]])
            P.op("dve", lambda e: e.scalar_tensor_tensor(out=a2[i], in0=a1[i], scalar=1.0, in1=bankf(bk),
                                                         op0=ALU.add, op1=ALU.mult),
                 r=[B_a1[i], banks[bk]], w=[B_a2[i]])
            P.op("act", lambda e: e.activation(out=a1[i], in_=a2[i], func=AF.Tanh, scale=C1),
                 r=[B_a2[i]], w=[B_a1[i]])
            P.op("dve", lambda e: e.scalar_tensor_tensor(out=dst, in0=a1[i], scalar=1.0, in1=bankf(bk),
                                                         op0=ALU.add, op1=ALU.mult),
                 r=[B_a1[i], banks[bk]], w=[dst_buf])

        def load_sgu_w(g):
            P.dma("pool", carve(OFF_D + (g % 2) * 16384, 12288, BF16), wsgu[g], w=[B_ws2[g % 2]])

        def st2_C(g, G):
            ws = wsl2[g % 2]; bws = B_ws2[g % 2]
            par = G % 2
            bu, bz, bv = 0 + par, 2 + par, 4 + par
            for kt in range(16):
                P.op("pe", lambda e: e.matmul(bankf(bu), lhsT=ws[:, kt, 0, :], rhs=hT[:, kt, G * 512:(G + 1) * 512],
                                              start=(kt == 0), stop=(kt == 15)), r=[B_hT, bws], w=[banks[bu]])
            for kt in range(16):
                P.op("pe", lambda e: e.matmul(bankf(bz), lhsT=ws[:, kt, 2, :], rhs=hT[:, kt, G * 512:(G + 1) * 512],
                                              start=(kt == 0), stop=(kt == 15)), r=[B_hT, bws], w=[banks[bz]])
            for tt in range(4):
                t = G * 4 + tt
                for kt in range(16):
                    P.op("pe", lambda e: e.matmul(bankf(bv)[:, tt * 128:(tt + 1) * 128],
                                                  lhsT=hT[:, kt, t * 128:(t + 1) * 128], rhs=ws[:, kt, 1, :],
                                                  start=(kt == 0), stop=(kt == 15)), r=[B_hT, bws], w=[banks[bv]])

        def st2_D(g, G):
            par = G % 2
            bu, bz, bv = 0 + par, 2 + par, 4 + par
            gelu2(bu, 0, ug2, B_ug2)
            P.op("act", lambda e: e.activation(out=a3, in_=bankf(bz), func=AF.Tanh, scale=0.5),
                 r=[banks[bz]], w=[B_a3])
            P.op("dve", lambda e: e.scalar_tensor_tensor(out=zs2, in0=a3, scalar=1.0, in1=bankf(bz),
                                                         op0=ALU.add, op1=ALU.mult), r=[B_a3, banks[bz]], w=[B_zs2])
            P.op("dve", lambda e: e.scalar_tensor_tensor(out=uz[par], in0=ug2, scalar=0.25, in1=zs2,
                                                         op0=ALU.mult, op1=ALU.mult), r=[B_ug2, B_zs2], w=[B_uz[par]])
            gelu2(bv, 1, vg2, B_vg2)
            ssv = stat[:, 48 + 8 * par:52 + 8 * par]; rsv = stat[:, 52 + 8 * par:56 + 8 * par]
            bst = B_stv[par]
            for tt in range(4):
                P.op("act", lambda e: e.activation(out=junk2, in_=vg2[:, tt * 128:(tt + 1) * 128],
                                                   func=AF.Square, accum_out=ssv[:, tt:tt + 1]),
                     r=[B_vg2], w=[B_junk2, bst])
            P.op("dve", lambda e: e.tensor_scalar(out=ssv, in0=ssv, scalar1=512.0 * EPS, scalar2=None, op0=ALU.add),
                 r=[bst], w=[bst])
            P.op("pool", lambda e: e.tensor_tensor(out=rsv, in0=ssv, in1=m05[:, 0:4], op=ALU.pow),
                 r=[bst] + CONST, w=[bst])
            P.op("dve", lambda e: e.tensor_tensor(out=vn[par].rearrange("p (t c) -> p t c", t=4),
                                                  in0=vg2.rearrange("p (t c) -> p t c", t=4),
                                                  in1=rsv.unsqueeze(2).to_broadcast([128, 4, 128]), op=ALU.mult),
                 r=[B_vg2, bst], w=[B_vn[par]])

        def st2_M(g, G):
            par = G % 2
            bm = 6 + par
            for tt in range(4):
                P.op("pe", lambda e: e.matmul(bankf(bm)[:, tt * 128:(tt + 1) * 128],
                                              lhsT=vn[par][:, tt * 128:(tt + 1) * 128], rhs=WT[:, g, :],
                                              start=True, stop=True), r=[B_vn[par]] + CONST, w=[banks[bm]])
            P.op("dve", lambda e: e.scalar_tensor_tensor(
                out=s1.rearrange("p (t c) -> p t c", t=4), in0=bankf(bm).rearrange("p (t c) -> p t c", t=4),
                scalar=vec[:, 22:23], in1=Bsp[:, g, :].unsqueeze(1).to_broadcast([128, 4, 128]),
                op0=ALU.mult, op1=ALU.add), r=[banks[bm]] + CONST, w=[B_s1])
            P.op("dve", lambda e: e.tensor_tensor(out=ST[:, g, G * 512:(G + 1) * 512], in0=s1, in1=uz[par],
                                                  op=ALU.mult), r=[B_s1, B_uz[par]], w=[B_ST])

        load_sgu_w(0)
        items = [(g, G) for g in range(8) for G in range(4)]
        for i, (g, G) in enumerate(items):
            if G == 0 and g + 1 < 8:
                load_sgu_w(g + 1)
            st2_C(g, G)
            st2_D(g, G)
            if i >= 1:
                st2_M(*items[i - 1])
        st2_M(*items[-1])
        if debug:
            P.dma("sp", dbg["d_ST"], ST.rearrange("p g t -> p (g t)"), r=[B_ST], dbuf=dbuf("dst"))

        B_ph2 = B_a1 + B_a2 + [B_a3, B_ug2, B_zs2, B_vg2, B_s1, B_junk2] + B_uz + B_vn
        od = OFF_D
        gsl = []
        psl = []
        sl3 = []
        for i in range(2):
            sl3.append(carve(od, 12288, BF16))
            gsl.append(carve(od, 8192, BF16).rearrange("p (k f c) -> p k f c", k=16, f=2)); od += 8192
            psl.append(carve(od, 4096, BF16).rearrange("p (k f c) -> p k f c", k=8, f=2)); od += 4096
        wosl = []
        for i in range(2):
            wosl.append(carve(od, 8192, BF16).rearrange("p (k c) -> p k c", k=16)); od += 8192
        mT = carve(od, 16384, BF16).rearrange("p (k t) -> p k t", k=16); od += 16384
        xr = []
        for i in range(2):
            xr.append(carve(od, 2048, F32)); od += 2048
        tha = carve(od, 2048, F32); od += 2048
        thb = carve(od, 2048, F32); od += 2048
        t1 = carve(od, 1024, BF16); od += 1024
        t2 = carve(od, 1024, BF16); od += 1024
        assert od <= OFF_F, od
        inh = B_ph2 + B_ws2 + B_ph1 + B_ws + B_ph0
        B_gsl = [Buf(f"gsl{i}", inherit=inh) for i in range(2)]
        B_wo = [Buf(f"wo{i}", inherit=inh) for i in range(2)]
        B_mT = Buf("mT", inherit=inh)
        B_xr = [Buf(f"xr{i}", inherit=inh) for i in range(4)]
        B_tha = Buf("tha", inherit=inh); B_thb = Buf("thb", inherit=inh)
        B_t1 = Buf("t1", inherit=inh); B_t2 = Buf("t2", inherit=inh)

        def load_ct(i):
            ct = i % 16
            if i < 16:
                P.dma("pool", sl3[i % 2], wph3[ct], w=[B_gsl[i % 2]])
                P.dma("sp", wph3_b[ct], sl3[i % 2], r=[B_gsl[i % 2]], w=[B_pre3[ct]], dbuf=B_pre3[ct])
            else:
                P.dma("pool", sl3[i % 2], wph3_b[ct], r=[B_pre3[ct]], w=[B_gsl[i % 2]])

        def load_wo(i):
            wflat = wosl[i % 2].rearrange("p k c -> p (k c)")
            if i < 8:
                P.dma("pool", wflat, wout[i], w=[B_wo[i % 2]])
                P.dma("sp", wout_b[i], wflat, r=[B_wo[i % 2]], w=[B_preo[i]], dbuf=B_preo[i])
            else:
                P.dma("pool", wflat, wout_b[i % 8], r=[B_preo[i % 8]], w=[B_wo[i % 2]])

        load_ct(0)
        load_wo(0)
        ict = 0
        iwo = 0
        ixr = 0
        for G in range(4):
            gs = slice(G * 512, (G + 1) * 512)
            for ct in range(16):
                if ict + 1 < 64:
                    load_ct(ict + 1)
                gw = gsl[ict % 2]; pw = psl[ict % 2]; bw = B_gsl[ict % 2]
                par = ct % 2
                bya, byb, bga, bgb = 4 * par, 4 * par + 1, 4 * par + 2, 4 * par + 3
                for kt in range(8):
                    P.op("pe", lambda e, kt=kt, bya=bya, pw=pw: e.matmul(
                        bankf(bya), lhsT=pw[:, kt, 0, :], rhs=AT[:, kt, gs], start=(kt == 0), stop=(kt == 7)),
                        r=B_AT + [bw], w=[banks[bya]])
                for kt in range(8):
                    P.op("pe", lambda e, kt=kt, byb=byb, pw=pw: e.matmul(
                        bankf(byb), lhsT=pw[:, kt, 1, :], rhs=ST[:, kt, gs], start=(kt == 0), stop=(kt == 7)),
                        r=[B_ST, bw], w=[banks[byb]])
                for kt in range(16):
                    P.op("pe", lambda e, kt=kt, bga=bga, gw=gw: e.matmul(
                        bankf(bga), lhsT=gw[:, kt, 0, :], rhs=hT[:, kt, gs], start=(kt == 0), stop=(kt == 15)),
                        r=[B_hT, bw], w=[banks[bga]])
                for kt in range(16):
                    P.op("pe", lambda e, kt=kt, bgb=bgb, gw=gw: e.matmul(
                        bankf(bgb), lhsT=gw[:, kt, 1, :], rhs=hT[:, kt, gs], start=(kt == 0), stop=(kt == 15)),
                        r=[B_hT, bw], w=[banks[bgb]])
                P.op("act", lambda e, bga=bga: e.activation(out=tha, in_=bankf(bga), func=AF.Tanh, scale=0.5),
                     r=[banks[bga]], w=[B_tha])
                P.op("act", lambda e, bgb=bgb: e.activation(out=thb, in_=bankf(bgb), func=AF.Tanh, scale=0.5),
                     r=[banks[bgb]], w=[B_thb])
                P.op("dve", lambda e, bya=bya: e.scalar_tensor_tensor(out=t1, in0=tha, scalar=1.0, in1=bankf(bya),
                                                                      op0=ALU.add, op1=ALU.mult),
                     r=[B_tha, banks[bya]], w=[B_t1])
                P.op("dve", lambda e, byb=byb: e.scalar_tensor_tensor(out=t2, in0=thb, scalar=1.0, in1=bankf(byb),
                                                                      op0=ALU.add, op1=ALU.mult),
                     r=[B_thb, banks[byb]], w=[B_t2])
                P.op("dve", lambda e, ct=ct: e.tensor_tensor(out=mT[:, ct, :], in0=t1, in1=t2, op=ALU.add),
                     r=[B_t1, B_t2], w=[B_mT])
                ict += 1
            for cb in range(8):
                if iwo + 1 < 32:
                    load_wo(iwo + 1)
                wo = wosl[iwo % 2]; bwo = B_wo[iwo % 2]
                for tt in range(4):
                    row0 = G * 512 + tt * 128
                    xi = ixr % 4
                    xb = xr[xi // 2][:, (xi % 2) * 256:(xi % 2) * 256 + 256]
                    bxb = B_xr[xi]
                    bk = (ixr % 8)
                    P.dma("act", xb, xo[row0:row0 + 128, cb * 256:(cb + 1) * 256], w=[bxb])
                    for kt in range(16):
                        P.op("pe", lambda e, kt=kt, tt=tt, bk=bk, wo=wo: e.matmul(
                            bankf(bk)[:, 0:256], lhsT=mT[:, kt, tt * 128:(tt + 1) * 128], rhs=wo[:, kt, :],
                            start=(kt == 0), stop=(kt == 15)), r=[B_mT, bwo], w=[banks[bk]])
                    P.op("dve", lambda e, bk=bk, xb=xb: e.scalar_tensor_tensor(
                        out=xb, in0=bankf(bk)[:, 0:256], scalar=0.5, in1=xb, op0=ALU.mult, op1=ALU.add),
                        r=[banks[bk], bxb], w=[bxb])
                    P.dma("sp", out_d[row0:row0 + 128, cb * 256:(cb + 1) * 256], xb, r=[bxb], dbuf=bxb)
                    ixr += 1
                iwo += 1
        fin = list(B_xr) + DBG_BUFS
        P.final_wait("sp", fin)

        block = st.enter_context(nc.Block())

        @block.tensor
        def _(e):
            P.emit(e, "pe")

        @block.scalar
        def _(e):
            P.emit(e, "act")

        @block.vector
        def _(e):
            P.emit(e, "dve")

        @block.gpsimd
        def _(e):
            P.emit(e, "pool")

        @block.sync
        def _(e):
            P.emit(e, "sp")
    return nc


def _host_layouts(x, norm_g, w_in, q_norm_g, k_norm_g, sgu_norm_g, w_spatial, b_spatial,
                  w_proj_a, w_proj_b, w_out):
    f32 = np.float32
    w_in = np.asarray(w_in, f32)[0]
    wi = w_in.reshape(16, 128, 11264)

    def fam(c0, n):
        blk = wi[:, :, c0:c0 + n].reshape(16, 128, n // 128, 128)
        return np.transpose(blk, (2, 1, 0, 3))
    q, k, v, za = fam(0, 1024), fam(1024, 1024), fam(2048, 1024), fam(3072, 1024)
    ub, vb, zb = fam(4096, 1024), fam(5120, 1024), fam(6144, 1024)
    ga, gb = fam(7168, 2048), fam(9216, 2048)
    wqkvz = np.ascontiguousarray(np.stack([q, k, v, za], axis=3)).reshape(8, 128, 8192)
    wkvpre = np.ascontiguousarray(np.transpose(np.stack([k, v], axis=3), (1, 0, 2, 3, 4))).reshape(128, 32768)
    wsgu = np.ascontiguousarray(np.stack([ub, vb, zb], axis=3)).reshape(8, 128, 6144)
    wgate = np.ascontiguousarray(np.stack([ga, gb], axis=3)).reshape(16, 128, 4096)
    pa = np.asarray(w_proj_a, f32)[0].reshape(8, 128, 16, 128)
    pb = np.asarray(w_proj_b, f32)[0].reshape(8, 128, 16, 128)
    wproj = np.ascontiguousarray(np.stack([np.transpose(pa, (2, 1, 0, 3)), np.transpose(pb, (2, 1, 0, 3))],
                                          axis=3)).reshape(16, 128, 2048)
    wph3 = np.ascontiguousarray(np.concatenate([wgate, wproj], axis=2))
    wo = np.asarray(w_out, f32)[0].reshape(16, 128, 8, 256)
    wout = np.ascontiguousarray(np.transpose(wo, (2, 1, 0, 3))).reshape(8, 128, 4096)
    vecs = np.zeros((128, 20), f32)
    vecs[:, 0:16] = np.asarray(norm_g, f32)[0].reshape(16, 128).T
    vecs[:, 16] = np.asarray(q_norm_g, f32)[0]
    vecs[:, 17] = np.asarray(k_norm_g, f32)[0]
    vecs[:, 18] = np.asarray(sgu_norm_g, f32)[0]
    bsp = np.ascontiguousarray(np.broadcast_to(np.asarray(b_spatial, f32)[0].reshape(1, 1024), (128, 1024)))
    wspT = np.ascontiguousarray(np.transpose(np.asarray(w_spatial, f32)[0], (2, 0, 1)))
    ident = np.eye(128, dtype=f32)
    tri = np.triu(np.ones((128, 128), f32))
    shared = dict(wqkvz=wqkvz, wkvpre=wkvpre, wsgu=wsgu, wph3=wph3, wout=wout, vecs=vecs, bsp=bsp,
                  wspT=wspT, ident=ident, tri=tri)
    x = np.asarray(x, f32)
    in_maps = []
    for c in range(8):
        b, j = c // 2, c % 2
        gb_ = np.full((16, 16), -1e30, f32)
        for i in range(16):
            lo = 0 if j == 1 else 8
            gb_[i, lo:8 + i // 2] = 0.0
        m = dict(shared)
        m["xo"] = np.ascontiguousarray(x[b, j * 2048:(j + 1) * 2048])
        m["xp"] = np.ascontiguousarray(x[b, 0:2048])
        m["gbias"] = np.ascontiguousarray(np.broadcast_to(gb_.reshape(1, 256), (128, 256)))
        in_maps.append(m)
    return in_maps


_NC_CACHE = {}


def kernel(x, norm_g, w_in, q_norm_g, k_norm_g, sgu_norm_g, w_spatial, b_spatial, w_proj_a, w_proj_b, w_out,
           _debug=False):
    in_maps = _host_layouts(x, norm_g, w_in, q_norm_g, k_norm_g, sgu_norm_g, w_spatial, b_spatial,
                            w_proj_a, w_proj_b, w_out)
    nc = build_program(debug=_debug)
    res = run_bass_kernel_spmd(nc, in_maps, core_ids=list(range(8)))
    out = np.empty((4, 4096, 2048), np.float32)
    for c in range(8):
        b, j = c // 2, c % 2
        out[b, j * 2048:(j + 1) * 2048] = res.results[c]["out"]
    if _debug:
        return out, res
    return out
```

```python
import numpy as np
from contextlib import ExitStack
import concourse.bass as bass
import concourse.mybir as mybir
from concourse.bass_utils import run_bass_kernel_spmd

F32 = mybir.dt.float32
BF16 = mybir.dt.bfloat16
U8 = mybir.dt.uint8
ALU = mybir.AluOpType
AF = mybir.ActivationFunctionType
AX = mybir.AxisListType

D = 2048
NH = 8
DH = 128
S_OWN = 2048
NT = 16
EPS = 1e-6
C1 = 0.7978845608028654
C2 = 0.044715
SQ128 = float(np.sqrt(128.0))


class Buf:
    def __init__(self, name, inherit=()):
        self.name = name
        self.writers = {}
        self.readers = {}
        self.dsem = None
        self.dcount = 0
        for b in inherit:
            for d in (b.writers, b.readers):
                for k, v in d.items():
                    if self.readers.get(k, (None, 0))[1] < v[1]:
                        self.readers[k] = v


class _Rec:
    def __init__(self):
        self.call = None

    def __getattr__(self, name):
        def f(*args, **kwargs):
            self.call = (name, args, kwargs)
        return f


class Eng:
    def __init__(self, name, sem):
        self.name = name
        self.sem = sem
        self.n = 0
        self.ops = []
        self.seen = {}


class Prog:
    def __init__(self, nc, st):
        self.nc = nc
        self.st = st
        self.E = {}
        for n in ("pe", "act", "dve", "pool", "sp"):
            self.E[n] = Eng(n, st.enter_context(nc.semaphore("s_" + n)))
        self.nsem = 5

    def _waits(self, eng, r, w, is_dma=False):
        E = self.E[eng]
        need = {}

        def add(d, raw, skip_dma=False):
            for k, (sem, val) in d.items():
                if skip_dma and k.startswith("dma_"):
                    continue
                if k == eng:
                    if eng == "pe" or not raw:
                        continue
                if need.get(k, (None, 0))[1] < val:
                    need[k] = (sem, val)
        for b in r:
            add(b.writers, True)
        for b in w:
            add(b.readers, False)
            add(b.writers, False, skip_dma=is_dma)
        out = []
        for k, (sem, val) in need.items():
            if E.seen.get(k, 0) >= val:
                continue
            E.seen[k] = val
            out.append((sem, val))
        return out

    def _record(self, key, tok, r, w):
        for b in r:
            b.readers[key] = tok
        for b in w:
            if b.readers:
                b.readers = {}
                b.writers = {}
            b.writers[key] = tok

    def op(self, eng, fn, r=(), w=()):
        E = self.E[eng]
        waits = self._waits(eng, r, w)
        E.n += 1
        tok = (E.sem, E.n)
        rec = _Rec()
        fn(rec)
        name, args, kwargs = rec.call
        E.ops.append((waits, lambda e: getattr(e, name)(*args, **kwargs), E.sem, 1))
        self._record(eng, tok, r, w)

    def dma(self, eng, out, in_, r=(), w=(), dbuf=None):
        E = self.E[eng]
        b = dbuf if dbuf is not None else (w[0] if w else r[0])
        if b.dsem is None:
            b.dsem = self.st.enter_context(self.nc.semaphore("d_" + b.name))
            self.nsem += 1
        waits = self._waits(eng, r, w, is_dma=True)
        b.dcount += 16
        tok = (b.dsem, b.dcount)
        E.ops.append((waits, lambda e: e.dma_start(out=out, in_=in_), b.dsem, 16))
        self._record("dma_" + b.name, tok, r, w)

    def final_wait(self, eng, bufs):
        E = self.E[eng]
        for b in bufs:
            E.ops.append(([(b.dsem, b.dcount)], None, None, 0))

    def emit(self, e, name):
        for waits, fn, sem, inc in self.E[name].ops:
            for s, v in waits:
                e.wait_ge(s, v)
            if fn is None:
                continue
            ins = fn(e)
            ins.then_inc(sem, inc)


def build_program(debug=False):
    nc = bass.Bass("TRN2", target_bir_lowering=False)

    def din(name, shape):
        return nc.dram_tensor(name, list(shape), F32, kind="ExternalInput").ap()

    xo = din("xo", (S_OWN, D))
    xp = din("xp", (S_OWN, D))
    wqkvz = din("wqkvz", (NH, 128, 8192))
    wkvpre = din("wkvpre", (128, 32768))
    wsgu = din("wsgu", (8, 128, 6144))
    wph3 = din("wph3", (16, 128, 6144))
    wout = din("wout", (8, 128, 4096))
    wph3_b = nc.dram_tensor("wph3_b", [16, 128, 6144], BF16, kind="Internal").ap()
    wout_b = nc.dram_tensor("wout_b", [8, 128, 4096], BF16, kind="Internal").ap()
    vecs = din("vecs", (128, 20))
    gbias_d = din("gbias", (128, 256))
    bsp_d = din("bsp", (128, 1024))
    wspT_d = din("wspT", (128, 8, 128))
    ident_d = din("ident", (128, 128))
    tri_d = din("tri", (128, 128))
    negtri_d = din("negtri", (128, 128))
    out_d = nc.dram_tensor("out", [S_OWN, D], F32, kind="ExternalOutput").ap()
    dbg = {}
    if debug:
        for nm, shp in (("d_hT", (128, 16 * 2048)), ("d_AT", (128, 8 * 2048)), ("d_ST", (128, 8 * 2048)),
                        ("d_KTpre", (128, 8 * 2048)), ("d_QK", (128, 2 * 2048)), ("d_sel", (128, 256)),
                        ("d_gm", (128, 256)), ("d_V", (128, 16 * 129)), ("d_za", (128, 2048))):
            dbg[nm] = nc.dram_tensor(nm, list(shp), BF16 if nm not in ("d_sel", "d_gm") else F32,
                                     kind="ExternalOutput").ap()

    with ExitStack() as st:
        ARENA_BYTES = 212480
        arena = st.enter_context(nc.sbuf_tensor("arena", [128, ARENA_BYTES], U8))
        psum_all = st.enter_context(nc.psum_tensor("ps", [128, 4096], F32))
        P = Prog(nc, st)

        def carve(off, nbytes, dt):
            assert off % 32 == 0 and off + nbytes <= ARENA_BYTES, (off, nbytes)
            return arena[:, off:off + nbytes].bitcast(dt)

        OFF_A = 0
        OFF_B = 65536
        OFF_C = 98304
        OFF_D = 131584
        OFF_E = 164352
        OFF_F = 200192
        hT = carve(OFF_A, 65536, BF16).rearrange("p (k t) -> p k t", k=16)
        Wkv = carve(OFF_A, 65536, BF16).rearrange("p (h k c) -> p h k c", h=8, k=16)
        KTpre = carve(OFF_B, 32768, BF16).rearrange("p (h t) -> p h t", h=8)
        AT = KTpre
        Vpre = carve(OFF_C, 33024, BF16).rearrange("p (t h c) -> p t h c", t=16, h=8)
        ST = carve(OFF_C, 32768, BF16).rearrange("p (g t) -> p g t", g=8)

        o = OFF_F
        ident = carve(o, 256, BF16); o += 256
        tri = carve(o, 256, BF16); o += 256
        negtri = carve(o, 256, BF16); o += 256
        WT = carve(o, 2048, BF16).rearrange("p (g t) -> p g t", g=8); o += 2048
        Bsp = carve(o, 4096, F32).rearrange("p (g t) -> p g t", g=8); o += 4096
        gbias = carve(o, 1024, F32); o += 1024
        vec = carve(o, 96, F32); o += 96
        m05 = carve(o, 64, F32); o += 64
        stat = carve(o, 512, F32); o += 512
        km = carve(o, 64, F32); o += 64
        kmb = carve(o, 32, BF16); o += 32
        gm = carve(o, 1024, F32).rearrange("p (c b) -> p c b", c=16); o += 1024
        sel = carve(o, 1024, F32).rearrange("p (c b) -> p c b", c=16); o += 1024
        m8 = carve(o, 512, F32).rearrange("p (c b) -> p c b", c=16); o += 512
        thr = carve(o, 64, F32); o += 64
        assert o <= ARENA_BYTES, o

        DBG_BUFS = []

        def dbuf(n):
            b = Buf(n)
            DBG_BUFS.append(b)
            return b

        B_const = Buf("const")
        B_Aq = [Buf(f"A{i}") for i in range(4)]
        banks = [Buf(f"bank{i}") for i in range(8)]

        def bankf(i):
            return psum_all[:, i * 512:(i + 1) * 512]

        def bankb(i):
            return psum_all[:, i * 512:(i + 1) * 512].bitcast(BF16)

        B_constp = Buf("constp")
        P.dma("sp", vec[:, 0:20], vecs, w=[B_const])
        P.dma("sp", gbias, gbias_d, w=[B_const])
        P.dma("sp", Bsp.rearrange("p g t -> p (g t)"), bsp_d, w=[B_const])
        P.dma("pool", ident, ident_d, w=[B_constp])
        P.dma("pool", tri, tri_d, w=[B_constp])
        P.dma("pool", negtri, negtri_d, w=[B_constp])
        P.dma("pool", WT.rearrange("p g t -> p (g t)"), wspT_d.rearrange("p g t -> p (g t)"), w=[B_constp])
        B_c2 = Buf("const2")
        P.op("dve", lambda e: e.tensor_scalar(out=vec[:, 20:23], in0=vec[:, 16:19], scalar1=SQ128, scalar2=None,
                                              op0=ALU.mult), r=[B_const], w=[B_c2])
        P.op("pool", lambda e: e.memset(m05, -0.5), w=[B_c2])
        P.op("dve", lambda e: e.tensor_tensor(out=WT, in0=WT, in1=tri.unsqueeze(1).to_broadcast([128, 8, 128]),
                                              op=ALU.mult), r=[B_constp], w=[B_c2])
        CONST = [B_const, B_constp, B_c2]

        od = OFF_E
        xt = [carve(od + i * 8192, 8192, F32) for i in range(2)]; od += 16384
        xn = [carve(od + i * 4096, 4096, BF16) for i in range(2)]; od += 8192
        hTt = [carve(od + i * 4096, 4096, BF16).rearrange("p (k t) -> p k t", k=16) for i in range(2)]; od += 8192
        junk = carve(od, 256, BF16); od += 256
        assert od <= OFF_F
        kn = [carve(OFF_D + 28672 + i * 2048, 2048, BF16).rearrange("p (h c) -> p h c", h=8) for i in range(2)]
        B_xt = [Buf("xt0"), Buf("xt1")]
        B_xn = [Buf("xn0"), Buf("xn1")]; B_hTt = [Buf("hTt0"), Buf("hTt1")]; B_kn = [Buf("kn0"), Buf("kn1")]
        B_junk = Buf("junk")
        B_st0 = [Buf("st0a"), Buf("st0b")]; B_stk = [Buf(f"stk{i}") for i in range(8)]
        B_KTpre = [Buf(f"KTpre{h}") for h in range(8)]; B_Vpre = Buf("Vpre")

        P.op("pool", lambda e: e.memset(Vpre[:, :, :, 128:129], 1.0), w=[B_Vpre])
        for q4 in range(4):
            P.dma("pool", carve(OFF_A + q4 * 16384, 16384, BF16), wkvpre[:, q4 * 8192:(q4 + 1) * 8192], w=[B_Aq[q4]])
        B_hT = None

        def st_A(src_dram, t):
            xb = xt[t % 2]; bx = B_xt[t % 2]; xnb = xn[t % 2]; bxn = B_xn[t % 2]
            P.dma("sp", xb, src_dram[(t % 16) * 128:(t % 16 + 1) * 128, :], w=[bx])
            ss = stat[:, 2 * (t % 2):2 * (t % 2) + 1]; rs = stat[:, 2 * (t % 2) + 1:2 * (t % 2) + 2]
            bst = B_st0[t % 2]
            P.op("act", lambda e: e.activation(out=xnb, in_=xb, func=AF.Square, accum_out=ss), r=[bx], w=[bxn, bst])
            P.op("dve", lambda e: e.tensor_scalar(out=ss, in0=ss, scalar1=1.0 / D, scalar2=EPS, op0=ALU.mult,
                                                  op1=ALU.add), r=[bst], w=[bst])
            P.op("pool", lambda e: e.tensor_tensor(out=rs, in0=ss, in1=m05[:, 0:1], op=ALU.pow),
                 r=[bst] + CONST, w=[bst])
            P.op("dve", lambda e: e.tensor_scalar(out=xnb, in0=xb, scalar1=rs, scalar2=None, op0=ALU.mult),
                 r=[bx, bst], w=[bxn])

        def st_B(t, dst_ap, dst_buf):
            xnb = xn[t % 2]; bxn = B_xn[t % 2]
            for half in range(2):
                bk = half
                for j in range(8):
                    kt = half * 8 + j
                    P.op("pe", lambda e: e.transpose(out=bankb(bk)[:, j * 128:(j + 1) * 128],
                                                     in_=xnb[:, kt * 128:(kt + 1) * 128], identity=ident),
                         r=[bxn] + CONST, w=[banks[bk]])
                P.op("dve", lambda e: e.tensor_tensor(
                    out=dst_ap[:, half * 8:(half + 1) * 8, :], in0=bankb(bk).rearrange("p (k t) -> p k t", k=8),
                    in1=vec[:, half * 8:(half + 1) * 8].unsqueeze(2).to_broadcast([128, 8, 128]), op=ALU.mult),
                    r=[banks[bk]] + CONST, w=[dst_buf])

        def st_CD(t):
            hb = hTt[t % 2]; bhb = B_hTt[t % 2]; knb = kn[t % 2]; bkn = B_kn[t % 2]
            for b2 in range(4):
                bk = 2 + b2
                for hh in range(2):
                    h = 2 * b2 + hh
                    for kt in range(16):
                        P.op("pe", lambda e: e.matmul(bankf(bk)[:, hh * 256:hh * 256 + 256], lhsT=hb[:, kt, :],
                                                      rhs=Wkv[:, h, kt, :], start=(kt == 0), stop=(kt == 15)),
                             r=[bhb, B_Aq[b2]], w=[banks[bk]])
                so = 8 + ((t % 2) * 4 + b2) * 4
                ssk = stat[:, so:so + 2]; rsk = stat[:, so + 2:so + 4]; bst = B_stk[(t % 2) * 4 + b2]
                pv = bankf(bk).rearrange("p (h c) -> p h c", h=2)
                for hh in range(2):
                    P.op("act", lambda e: e.activation(out=junk, in_=pv[:, hh, 0:128], func=AF.Square,
                                                       accum_out=ssk[:, hh:hh + 1]), r=[banks[bk]], w=[B_junk, bst])
                P.op("dve", lambda e: e.tensor_scalar(out=ssk, in0=ssk, scalar1=128.0 * EPS, scalar2=None,
                                                      op0=ALU.add), r=[bst], w=[bst])
                P.op("pool", lambda e: e.tensor_tensor(out=rsk, in0=ssk, in1=m05[:, 0:2], op=ALU.pow),
                     r=[bst] + CONST, w=[bst])
                P.op("dve", lambda e: e.tensor_tensor(
                    out=knb[:, 2 * b2:2 * b2 + 2, :], in0=pv[:, :, 0:128],
                    in1=rsk.unsqueeze(2).to_broadcast([128, 2, 128]), op=ALU.mult), r=[banks[bk], bst], w=[bkn])
                P.op("act", lambda e: e.copy(out=Vpre[:, t, 2 * b2:2 * b2 + 2, 0:128], in_=pv[:, :, 128:256]),
                     r=[banks[bk]], w=[B_Vpre])

        def st_E(t):
            knb = kn[t % 2]; bkn = B_kn[t % 2]
            for h in range(8):
                P.op("pe", lambda e: e.transpose(out=bankb(6)[:, h * 128:(h + 1) * 128], in_=knb[:, h, :],
                                                 identity=ident), r=[bkn] + CONST, w=[banks[6]])
            P.op("dve", lambda e: e.tensor_scalar(
                out=KTpre[:, :, t * 128:(t + 1) * 128], in0=bankb(6).rearrange("p (h t) -> p h t", h=8),
                scalar1=vec[:, 21:22], scalar2=None, op0=ALU.mult), r=[banks[6]] + CONST, w=B_KTpre)

        st_A(xp, 0)
        st_B(0, hTt[0], B_hTt[0])
        st_A(xp, 1)
        for t in range(NT):
            if t + 1 < NT:
                st_B(t + 1, hTt[(t + 1) % 2], B_hTt[(t + 1) % 2])
            if t + 2 < NT:
                st_A(xp, t + 2)
            elif t + 2 == NT:
                st_A(xo, 16)
            st_CD(t)
            if t >= 1:
                st_E(t - 1)
        st_E(NT - 1)
        B_hT = Buf("hT", inherit=B_Aq)
        for t in range(16, 32):
            if t + 1 < 32:
                st_A(xo, t + 1)
            st_B(t, hT[:, :, (t - 16) * 128:(t - 15) * 128], B_hT)

        B_pre3 = [Buf(f"pre3_{i}") for i in range(16)]; B_preo = [Buf(f"preo_{i}") for i in range(8)]
        wslot = [carve(OFF_D + i * 16384, 16384, BF16).rearrange("p (k f c) -> p k f c", k=16, f=4) for i in range(2)]
        B_ws = [Buf("ws0"), Buf("ws1", inherit=B_kn)]
        oe = OFF_E
        B_ph0 = B_xt + B_xn + B_hTt + B_kn + [B_junk]
        QK = carve(oe, 8192, BF16).rearrange("p (f t) -> p f t", f=2); oe += 8192
        Vown = carve(oe, 4160, BF16)[:, 0:16 * 129].rearrange("p (t c) -> p t c", t=16); oe += 4160
        zaT = carve(oe, 4096, BF16); oe += 4096
        PT = [carve(oe + i * 2048, 2048, BF16).rearrange("p (j q) -> p j q", j=2) for i in range(3)]; oe += 6144
        acc = [carve(oe + i * 2080, 2064, F32).rearrange("p (c d) -> p c d", c=4) for i in range(2)]; oe += 4160
        qkn = [carve(oe + i * 512, 512, BF16).rearrange("p (f c) -> p f c", f=2) for i in range(3)]; oe += 1536
        attn = carve(oe, 1024, BF16).rearrange("p (c d) -> p c d", c=4); oe += 1024
        th = carve(oe, 2048, F32); oe += 2048
        rinv = carve(oe, 32, F32); oe += 32
        junk1 = carve(oe, 256, BF16); oe += 256
        assert oe <= OFF_E + 35840, oe
        B_QK = Buf("QK", inherit=B_ph0); B_Vown = Buf("Vown", inherit=B_ph0); B_zaT = Buf("zaT", inherit=B_ph0)
        B_PT = [Buf(f"PT{i}", inherit=B_ph0) for i in range(3)]
        B_acc = [[Buf(f"acc{i}_{c}", inherit=B_ph0) for c in range(4)] for i in range(2)]
        B_qkn = [Buf(f"qkn{i}", inherit=B_ph0) for i in range(3)]
        B_st2 = [Buf(f"st2_{i}") for i in range(3)]
        B_attn = Buf("attn", inherit=B_ph0); B_th = Buf("th", inherit=B_ph0); B_rinv = Buf("rinv", inherit=B_ph0)
        B_junk1 = Buf("junk1", inherit=B_ph0)
        B_gate = Buf("gate"); B_km = Buf("km"); B_kmb = Buf("kmb"); B_gm = Buf("gm"); B_m8 = Buf("m8"); B_thr = Buf("thr")
        B_AT = B_KTpre

        P.op("pool", lambda e: e.memset(Vown[:, :, 128:129], 1.0), w=[B_Vown])

        def load_head_w(h):
            s = h % 2
            P.dma("pool", carve(OFF_D + s * 16384, 16384, BF16), wqkvz[h], w=[B_ws[s]])

        load_head_w(0)
        pt_rr = 0
        for h in range(NH):
            if h + 1 < NH:
                load_head_w(h + 1)
            ws = wslot[h % 2]; bws = B_ws[h % 2]
            def st_C(t):
                bk = t % 3
                for kt in range(16):
                    P.op("pe", lambda e: e.matmul(
                        bankf(bk)[:, 0:384], lhsT=hT[:, kt, t * 128:(t + 1) * 128],
                        rhs=ws[:, kt, 0:3, :].rearrange("p f c -> p (f c)"), start=(kt == 0), stop=(kt == 15)),
                        r=[B_hT, bws], w=[banks[bk]])
                so = 72 + (t % 3) * 4
                ss2 = stat[:, so:so + 2]; rs2 = stat[:, so + 2:so + 4]; bst = B_st2[t % 3]
                for f in range(2):
                    P.op("act", lambda e: e.activation(out=junk1, in_=bankf(bk)[:, f * 128:(f + 1) * 128],
                                                       func=AF.Square, accum_out=ss2[:, f:f + 1]),
                         r=[banks[bk]], w=[B_junk1, bst])
                P.op("dve", lambda e: e.tensor_scalar(out=ss2, in0=ss2, scalar1=128.0 * EPS, scalar2=None,
                                                      op0=ALU.add), r=[bst], w=[bst])
                P.op("pool", lambda e: e.tensor_tensor(out=rs2, in0=ss2, in1=m05[:, 0:2], op=ALU.pow),
                     r=[bst] + CONST, w=[bst])
                qn = qkn[t % 3]; bqn = B_qkn[t % 3]
                P.op("dve", lambda e: e.tensor_tensor(
                    out=qn, in0=bankf(bk)[:, 0:256].rearrange("p (f c) -> p f c", f=2),
                    in1=rs2.unsqueeze(2).to_broadcast([128, 2, 128]), op=ALU.mult), r=[banks[bk], bst], w=[bqn])
                P.op("act", lambda e: e.copy(out=Vown[:, t, 0:128], in_=bankf(bk)[:, 256:384]),
                     r=[banks[bk]], w=[B_Vown])

            def st_Eq(t):
                qn = qkn[t % 3]; bqn = B_qkn[t % 3]
                for f in range(2):
                    P.op("pe", lambda e: e.transpose(out=bankb(3)[:, f * 128:(f + 1) * 128], in_=qn[:, f, :],
                                                     identity=ident), r=[bqn] + CONST, w=[banks[3]])
                P.op("dve", lambda e: e.tensor_tensor(
                    out=QK[:, :, t * 128:(t + 1) * 128], in0=bankb(3)[:, 0:256].rearrange("p (f c) -> p f c", f=2),
                    in1=vec[:, 20:22].unsqueeze(2).to_broadcast([128, 2, 128]), op=ALU.mult),
                    r=[banks[3]] + CONST, w=[B_QK])

            def st_za(G):
                bk = 4 + G % 2
                for kt in range(16):
                    P.op("pe", lambda e: e.matmul(bankf(bk), lhsT=ws[:, kt, 3, :], rhs=hT[:, kt, G * 512:(G + 1) * 512],
                                                  start=(kt == 0), stop=(kt == 15)), r=[B_hT, bws], w=[banks[bk]])
                P.op("act", lambda e: e.activation(out=th, in_=bankf(bk), func=AF.Tanh, scale=0.5),
                     r=[banks[bk]], w=[B_th])
                P.op("dve", lambda e: e.scalar_tensor_tensor(
                    out=zaT[:, G * 512:(G + 1) * 512], in0=th, scalar=1.0, in1=bankf(bk), op0=ALU.add, op1=ALU.mult),
                    r=[B_th, banks[bk]], w=[B_zaT])

            def km_own(g):
                P.op("dve", lambda e: e.tensor_reduce(
                    out=km[:, 8 + 2 * g:10 + 2 * g], in_=QK[:, 1, g * 512:(g + 1) * 512].rearrange("p (b k) -> p b k", b=2),
                    axis=AX.X, op=ALU.add), r=[B_QK], w=[B_km])

            P.op("dve", lambda e: e.tensor_reduce(out=km[:, 0:8], in_=KTpre[:, h, :].rearrange("p (b k) -> p b k", b=8),
                                                  axis=AX.X, op=ALU.add), r=[B_KTpre[h]], w=[B_km])
            for t in range(NT):
                st_C(t)
                if t >= 2:
                    st_Eq(t - 2)
                    if (t - 2) % 4 == 3:
                        km_own((t - 2) // 4)
                if t % 4 == 3:
                    st_za(t // 4)
            st_Eq(NT - 2)
            st_Eq(NT - 1)
            km_own(3)
            def ktile(j):
                return KTpre[:, h, j * 128:(j + 1) * 128] if j < 16 else QK[:, 1, (j - 16) * 128:(j - 15) * 128]

            def vtile(j):
                return Vpre[:, j, h, :] if j < 16 else Vown[:, j - 16, :]

            allsteps = []
            for m in range(4):
                steps = []
                for b in range(8 + 2 * m):
                    steps.append(dict(tiles=[(2 * b, 0), (2 * b + 1, 0)],
                                      chunks={c: [0, 1] for c in range(4)}, blk={c: b for c in range(4)}, diag=[]))
                r0 = 16 + 4 * m
                steps.append(dict(tiles=[(r0, 0), (r0 + 1, 1)],
                                  chunks={0: [0], 1: [0, 1], 2: [0, 1], 3: [0, 1]},
                                  blk={0: None, 1: None, 2: 8 + 2 * m, 3: 8 + 2 * m}, diag=[(0, 0), (1, 1)]))
                steps.append(dict(tiles=[(r0 + 2, 2), (r0 + 3, 3)],
                                  chunks={2: [0], 3: [0, 1]}, blk={2: None, 3: None}, diag=[(0, 2), (1, 3)]))
                for k_, sp_ in enumerate(steps):
                    sp_["m"] = m
                    sp_["first"] = (k_ == 0)
                    sp_["last"] = (k_ == len(steps) - 1)
                allsteps += steps
            for si, sp_ in enumerate(allsteps):
                sp_["pt"] = (pt_rr + si) % 3
                sp_["sb"] = 0 + 2 * (si % 2)
                sp_["ob"] = 4 + 2 * (si % 2)
            pt_rr += len(allsteps)

            def emit_qk(sp_):
                m = sp_["m"]
                pt = PT[sp_["pt"]]; bpt = B_PT[sp_["pt"]]
                sb = sp_["sb"]
                dg = dict(sp_["diag"])
                for idx, (j, c0) in enumerate(sp_["tiles"]):
                    bk = sb + idx
                    n = (4 - c0) * 128
                    P.op("pe", lambda e: e.matmul(
                        bankf(bk)[:, 0:n], lhsT=ktile(j), rhs=QK[:, 0, m * 512 + c0 * 128:(m + 1) * 512],
                        start=True, stop=(idx not in dg)), r=[B_KTpre[h], B_QK], w=[banks[bk]])
                    if idx in dg:
                        cc = dg[idx] - c0
                        P.op("pe", lambda e: e.matmul(bankf(bk)[:, cc * 128:(cc + 1) * 128], lhsT=ident, rhs=negtri,
                                                      start=False, stop=True), r=CONST, w=[banks[bk]])
                if all(c0 == 0 for (_, c0) in sp_["tiles"]):
                    P.op("act", lambda e: e.activation(
                        out=pt, in_=psum_all[:, sb * 512:(sb + 2) * 512].rearrange("p (j q) -> p j q", j=2),
                        func=AF.Exp, scale=float(DH ** -0.5)), r=[banks[sb], banks[sb + 1]], w=[bpt])
                else:
                    for idx, (j, c0) in enumerate(sp_["tiles"]):
                        bk = sb + idx
                        n = (4 - c0) * 128
                        P.op("act", lambda e: e.activation(
                            out=pt[:, idx, c0 * 128:512], in_=bankf(bk)[:, 0:n], func=AF.Exp,
                            scale=float(DH ** -0.5)), r=[banks[bk]], w=[bpt])

            def emit_pv(sp_):
                m = sp_["m"]
                ac = acc[m % 2]
                pt = PT[sp_["pt"]]; bpt = B_PT[sp_["pt"]]
                for c, tl in sp_["chunks"].items():
                    bk = sp_["ob"] + c // 2
                    po = bankf(bk)[:, (c % 2) * 129:(c % 2) * 129 + 129]
                    for ii, idx in enumerate(tl):
                        j = sp_["tiles"][idx][0]
                        P.op("pe", lambda e: e.matmul(po, lhsT=pt[:, idx, c * 128:(c + 1) * 128], rhs=vtile(j),
                                                      start=(ii == 0), stop=(ii == len(tl) - 1)),
                             r=[bpt, B_Vpre, B_Vown], w=[banks[bk]])
                for c, tl in sp_["chunks"].items():
                    bk = sp_["ob"] + c // 2
                    po = bankf(bk)[:, (c % 2) * 129:(c % 2) * 129 + 129]
                    b = sp_["blk"][c]
                    bac = B_acc[m % 2][c]
                    sc = 1.0 if b is None else sel[:, 4 * m + c, b:b + 1]
                    if sp_["first"]:
                        P.op("dve", lambda e: e.tensor_scalar(out=ac[:, c, :], in0=po, scalar1=sc, scalar2=None,
                                                              op0=ALU.mult), r=[banks[bk], B_gate], w=[bac])
                    else:
                        P.op("dve", lambda e: e.scalar_tensor_tensor(
                            out=ac[:, c, :], in0=po, scalar=sc, in1=ac[:, c, :], op0=ALU.mult, op1=ALU.add),
                            r=[banks[bk], B_gate, bac], w=[bac])

            def fin_dve(m):
                ac = acc[m % 2]; bac = B_acc[m % 2]
                P.op("dve", lambda e: e.tensor_scalar(out=rinv[:, 0:4], in0=ac[:, :, 128], scalar1=2.0, scalar2=None,
                                                      op0=ALU.mult), r=bac, w=[B_rinv])
                P.op("dve", lambda e: e.reciprocal(out=rinv[:, 4:8], in_=rinv[:, 0:4]), r=[B_rinv], w=[B_rinv])
                P.op("dve", lambda e: e.tensor_tensor(out=attn, in0=ac[:, :, 0:128],
                                                      in1=rinv[:, 4:8].unsqueeze(2).to_broadcast([128, 4, 128]),
                                                      op=ALU.mult), r=bac + [B_rinv], w=[B_attn])

            def fin_pe(m, bk):
                for c in range(4):
                    P.op("pe", lambda e: e.transpose(out=bankb(bk)[:, c * 128:(c + 1) * 128], in_=attn[:, c, :],
                                                     identity=ident), r=[B_attn] + CONST, w=[banks[bk]])
                P.op("dve", lambda e: e.tensor_tensor(out=zaT[:, m * 512:(m + 1) * 512], in0=bankb(bk)[:, 0:512],
                                                      in1=zaT[:, m * 512:(m + 1) * 512], op=ALU.mult),
                     r=[banks[bk], B_zaT], w=[B_zaT])

            P.op("dve", lambda e: e.tensor_scalar(out=kmb, in0=km, scalar1=1.0 / 256.0, scalar2=None, op0=ALU.mult),
                 r=[B_km], w=[B_kmb])
            emit_qk(allsteps[0])
            emit_qk(allsteps[1])
            for c in range(16):
                P.op("pe", lambda e: e.matmul(bankf(6)[:, c * 16:(c + 1) * 16], lhsT=QK[:, 0, c * 128:(c + 1) * 128],
                                              rhs=kmb, start=True, stop=True), r=[B_QK, B_kmb], w=[banks[6]])
            P.op("dve", lambda e: e.tensor_tensor(out=gm.rearrange("p c b -> p (c b)"), in0=bankf(6)[:, 0:256],
                                                  in1=gbias, op=ALU.add), r=[banks[6]] + CONST, w=[B_gm])
            for c in range(16):
                P.op("dve", lambda e: e.max(out=m8[:, c, :], in_=gm[:, c, :]), r=[B_gm], w=[B_m8])
            P.op("dve", lambda e: e.tensor_scalar(out=thr, in0=m8[:, :, 2], scalar1=-1e29, scalar2=None, op0=ALU.max),
                 r=[B_m8], w=[B_thr])
            P.op("dve", lambda e: e.tensor_tensor(out=sel, in0=gm, in1=thr.unsqueeze(2).to_broadcast([128, 16, 16]),
                                                  op=ALU.is_ge), r=[B_gm, B_thr], w=[B_gate])
            if debug and h == 0:
                P.dma("sp", dbg["d_QK"], QK.rearrange("p f t -> p (f t)"), r=[B_QK], dbuf=dbuf("dq"))
                P.dma("sp", dbg["d_sel"], sel.rearrange("p c b -> p (c b)"), r=[B_gate], dbuf=dbuf("ds"))
                P.dma("sp", dbg["d_gm"], gm.rearrange("p c b -> p (c b)"), r=[B_gate], dbuf=dbuf("dg"))
                P.dma("sp", dbg["d_V"], Vown.rearrange("p t c -> p (t c)"), r=[B_Vown], dbuf=dbuf("dv"))

            pend = None
            for si, sp_ in enumerate(allsteps):
                if si + 2 < len(allsteps):
                    emit_qk(allsteps[si + 2])
                emit_pv(sp_)
                if pend is not None:
                    fin_pe(pend, 4 + 2 * ((si + 1) % 2))
                    pend = None
                if sp_["last"]:
                    fin_dve(sp_["m"])
                    pend = sp_["m"]
            fin_pe(pend, 4)
            P.op("pool", lambda e, h=h: e.tensor_copy(out=AT[:, h, :], in_=zaT), r=[B_zaT], w=[B_KTpre[h]])

        if debug:
            P.dma("sp", dbg["d_hT"], hT.rearrange("p k t -> p (k t)"), r=[B_hT], dbuf=dbuf("dh"))
            P.dma("sp", dbg["d_AT"], AT.rearrange("p h t -> p (h t)"), r=B_AT, dbuf=dbuf("da"))

        B_ph1 = [B_QK, B_Vown, B_zaT] + B_PT + B_acc[0] + B_acc[1] + B_qkn + [B_attn, B_th, B_rinv, B_junk1]
        wsl2 = [carve(OFF_D + i * 16384, 12288, BF16).rearrange("p (k f c) -> p k f c", k=16, f=3) for i in range(2)]
        B_ws2 = [Buf("ws2_0", inherit=[B_ws[0]]), Buf("ws2_1", inherit=[B_ws[1]])]
        oe = OFF_E
        a1 = [carve(oe + i * 2048, 2048, F32) for i in range(2)]; oe += 4096
        a2 = [carve(oe + i * 2048, 2048, F32) for i in range(2)]; oe += 4096
        a3 = carve(oe, 2048, F32); oe += 2048
        ug2 = carve(oe, 1024, BF16); oe += 1024
        zs2 = carve(oe, 1024, BF16); oe += 1024
        uz = [carve(oe + i * 1024, 1024, BF16) for i in range(2)]; oe += 2048
        vg2 = carve(oe, 2048, F32); oe += 2048
        vn = [carve(oe + i * 1024, 1024, BF16) for i in range(2)]; oe += 2048
        s1 = carve(oe, 2048, F32); oe += 2048
        junk2 = carve(oe, 256, BF16); oe += 256
        assert oe <= OFF_E + 35840
        B_a1 = [Buf(f"a1_{i}", inherit=B_ph1) for i in range(2)]
        B_a2 = [Buf(f"a2_{i}", inherit=B_ph1) for i in range(2)]
        B_a3 = Buf("a3", inherit=B_ph1); B_ug2 = Buf("ug2", inherit=B_ph1); B_zs2 = Buf("zs2", inherit=B_ph1)
        B_uz = [Buf(f"uz{i}", inherit=B_ph1) for i in range(2)]; B_vg2 = Buf("vg2", inherit=B_ph1)
        B_vn = [Buf(f"vn{i}", inherit=B_ph1) for i in range(2)]
        B_s1 = Buf("s1", inherit=B_ph1); B_junk2 = Buf("junk2", inherit=B_ph1)
        B_ST = Buf("ST", inherit=[B_Vpre])
        B_stv = [Buf("stva"), Buf("stvb")]

        def gelu2(bk, i, dst, dst_buf):
            P.op("act", lambda e: e.activation(out=a1[i], in_=bankf(bk), func=AF.Square, scale=float(np.sqrt(C2))),
                 r=[banks[bk]], w=[B_a1[i]])
            P.op("dve", lambda e: e.scalar_tensor_tensor(out=a2[i], in0=a1[i], scalar=1.0, in1=bankf(bk),
                                                         op0=ALU.add, op1=ALU.mult),
                 r=[B_a1[i], banks[bk]], w=[B_a2[i]])
            P.op("act", lambda e: e.activation(out=a1[i], in_=a2[i], func=AF.Tanh, scale=C1),
                 r=[B_a2[i]], w=[B_a1[i]])
            P.op("dve", lambda e: e.scalar_tensor_tensor(out=dst, in0=a1[i], scalar=1.0, in1=bankf(bk),
                                                         op0=ALU.add, op1=ALU.mult),
                 r=[B_a1[i], banks[bk]], w=[dst_buf])

        def load_sgu_w(g):
            P.dma("pool", carve(OFF_D + (g % 2) * 16384, 12288, BF16), wsgu[g], w=[B_ws2[g % 2]])

        def st2_C(g, G):
            ws = wsl2[g % 2]; bws = B_ws2[g % 2]
            par = G % 2
            bu, bz, bv = 0 + par, 2 + par, 4 + par
            for kt in range(16):
                P.op("pe", lambda e: e.matmul(bankf(bu), lhsT=ws[:, kt, 0, :], rhs=hT[:, kt, G * 512:(G + 1) * 512],
                                              start=(kt == 0), stop=(kt == 15)), r=[B_hT, bws], w=[banks[bu]])
            for kt in range(16):
                P.op("pe", lambda e: e.matmul(bankf(bz), lhsT=ws[:, kt, 2, :], rhs=hT[:, kt, G * 512:(G + 1) * 512],
                                              start=(kt == 0), stop=(kt == 15)), r=[B_hT, bws], w=[banks[bz]])
            for tt in range(4):
                t = G * 4 + tt
                for kt in range(16):
                    P.op("pe", lambda e: e.matmul(bankf(bv)[:, tt * 128:(tt + 1) * 128],
                                                  lhsT=hT[:, kt, t * 128:(t + 1) * 128], rhs=ws[:, kt, 1, :],
                                                  start=(kt == 0), stop=(kt == 15)), r=[B_hT, bws], w=[banks[bv]])

        def st2_D(g, G):
            par = G % 2
            bu, bz, bv = 0 + par, 2 + par, 4 + par
            gelu2(bu, 0, ug2, B_ug2)
            P.op("act", lambda e: e.activation(out=a3, in_=bankf(bz), func=AF.Tanh, scale=0.5),
                 r=[banks[bz]], w=[B_a3])
            P.op("dve", lambda e: e.scalar_tensor_tensor(out=zs2, in0=a3, scalar=1.0, in1=bankf(bz),
                                                         op0=ALU.add, op1=ALU.mult), r=[B_a3, banks[bz]], w=[B_zs2])
            P.op("dve", lambda e: e.scalar_tensor_tensor(out=uz[par], in0=ug2, scalar=0.25, in1=zs2,
                                                         op0=ALU.mult, op1=ALU.mult), r=[B_ug2, B_zs2], w=[B_uz[par]])
            gelu2(bv, 1, vg2, B_vg2)
            ssv = stat[:, 48 + 8 * par:52 + 8 * par]; rsv = stat[:, 52 + 8 * par:56 + 8 * par]
            bst = B_stv[par]
            for tt in range(4):
                P.op("act", lambda e: e.activation(out=junk2, in_=vg2[:, tt * 128:(tt + 1) * 128],
                                                   func=AF.Square, accum_out=ssv[:, tt:tt + 1]),
                     r=[B_vg2], w=[B_junk2, bst])
            P.op("dve", lambda e: e.tensor_scalar(out=ssv, in0=ssv, scalar1=512.0 * EPS, scalar2=None, op0=ALU.add),
                 r=[bst], w=[bst])
            P.op("pool", lambda e: e.tensor_tensor(out=rsv, in0=ssv, in1=m05[:, 0:4], op=ALU.pow),
                 r=[bst] + CONST, w=[bst])
            P.op("dve", lambda e: e.tensor_tensor(out=vn[par].rearrange("p (t c) -> p t c", t=4),
                                                  in0=vg2.rearrange("p (t c) -> p t c", t=4),
                                                  in1=rsv.unsqueeze(2).to_broadcast([128, 4, 128]), op=ALU.mult),
                 r=[B_vg2, bst], w=[B_vn[par]])

        def st2_M(g, G):
            par = G % 2
            bm = 6 + par
            for tt in range(4):
                P.op("pe", lambda e: e.matmul(bankf(bm)[:, tt * 128:(tt + 1) * 128],
                                              lhsT=vn[par][:, tt * 128:(tt + 1) * 128], rhs=WT[:, g, :],
                                              start=True, stop=True), r=[B_vn[par]] + CONST, w=[banks[bm]])
            P.op("dve", lambda e: e.scalar_tensor_tensor(
                out=s1.rearrange("p (t c) -> p t c", t=4), in0=bankf(bm).rearrange("p (t c) -> p t c", t=4),
                scalar=vec[:, 22:23], in1=Bsp[:, g, :].unsqueeze(1).to_broadcast([128, 4, 128]),
                op0=ALU.mult, op1=ALU.add), r=[banks[bm]] + CONST, w=[B_s1])
            P.op("dve", lambda e: e.tensor_tensor(out=ST[:, g, G * 512:(G + 1) * 512], in0=s1, in1=uz[par],
                                                  op=ALU.mult), r=[B_s1, B_uz[par]], w=[B_ST])

        load_sgu_w(0)
        items = [(g, G) for g in range(8) for G in range(4)]
        for i, (g, G) in enumerate(items):
            if G == 0 and g + 1 < 8:
                load_sgu_w(g + 1)
            st2_C(g, G)
            st2_D(g, G)
            if i >= 1:
                st2_M(*items[i - 1])
        st2_M(*items[-1])
        if debug:
            P.dma("sp", dbg["d_ST"], ST.rearrange("p g t -> p (g t)"), r=[B_ST], dbuf=dbuf("dst"))

        B_ph2 = B_a1 + B_a2 + [B_a3, B_ug2, B_zs2, B_vg2, B_s1, B_junk2] + B_uz + B_vn
        od = OFF_D
        gsl = []
        psl = []
        sl3 = []
        for i in range(2):
            sl3.append(carve(od, 12288, BF16))
            gsl.append(carve(od, 8192, BF16).rearrange("p (k f c) -> p k f c", k=16, f=2)); od += 8192
            psl.append(carve(od, 4096, BF16).rearrange("p (k f c) -> p k f c", k=8, f=2)); od += 4096
        wosl = []
        for i in range(2):
            wosl.append(carve(od, 8192, BF16).rearrange("p (k c) -> p k c", k=16)); od += 8192
        mT = carve(od, 16384, BF16).rearrange("p (k t) -> p k t", k=16); od += 16384
        xr = []
        for i in range(2):
            xr.append(carve(od, 2048, F32)); od += 2048
        tha = carve(od, 2048, F32); od += 2048
        thb = carve(od, 2048, F32); od += 2048
        t1 = carve(od, 1024, BF16); od += 1024
        t2 = carve(od, 1024, BF16); od += 1024
        assert od <= OFF_F, od
        inh = B_ph2 + B_ws2 + B_ph1 + B_ws + B_ph0
        B_gsl = [Buf(f"gsl{i}", inherit=inh) for i in range(2)]
        B_wo = [Buf(f"wo{i}", inherit=inh) for i in range(2)]
        B_mT = Buf("mT", inherit=inh)
        B_xr = [Buf(f"xr{i}", inherit=inh) for i in range(4)]
        B_tha = Buf("tha", inherit=inh); B_thb = Buf("thb", inherit=inh)
        B_t1 = Buf("t1", inherit=inh); B_t2 = Buf("t2", inherit=inh)

        def load_ct(i):
            ct = i % 16
            if i < 16:
                P.dma("pool", sl3[i % 2], wph3[ct], w=[B_gsl[i % 2]])
                P.dma("sp", wph3_b[ct], sl3[i % 2], r=[B_gsl[i % 2]], w=[B_pre3[ct]], dbuf=B_pre3[ct])
            else:
                P.dma("pool", sl3[i % 2], wph3_b[ct], r=[B_pre3[ct]], w=[B_gsl[i % 2]])

        def load_wo(i):
            wflat = wosl[i % 2].rearrange("p k c -> p (k c)")
            if i < 8:
                P.dma("pool", wflat, wout[i], w=[B_wo[i % 2]])
                P.dma("sp", wout_b[i], wflat, r=[B_wo[i % 2]], w=[B_preo[i]], dbuf=B_preo[i])
            else:
                P.dma("pool", wflat, wout_b[i % 8], r=[B_preo[i % 8]], w=[B_wo[i % 2]])

        load_ct(0)
        load_wo(0)
        ict = 0
        iwo = 0
        ixr = 0
        for G in range(4):
            gs = slice(G * 512, (G + 1) * 512)
            for ct in range(16):
                if ict + 1 < 64:
                    load_ct(ict + 1)
                gw = gsl[ict % 2]; pw = psl[ict % 2]; bw = B_gsl[ict % 2]
                par = ct % 2
                bya, byb, bga, bgb = 4 * par, 4 * par + 1, 4 * par + 2, 4 * par + 3
                for kt in range(8):
                    P.op("pe", lambda e, kt=kt, bya=bya, pw=pw: e.matmul(
                        bankf(bya), lhsT=pw[:, kt, 0, :], rhs=AT[:, kt, gs], start=(kt == 0), stop=(kt == 7)),
                        r=B_AT + [bw], w=[banks[bya]])
                for kt in range(8):
                    P.op("pe", lambda e, kt=kt, byb=byb, pw=pw: e.matmul(
                        bankf(byb), lhsT=pw[:, kt, 1, :], rhs=ST[:, kt, gs], start=(kt == 0), stop=(kt == 7)),
                        r=[B_ST, bw], w=[banks[byb]])
                for kt in range(16):
                    P.op("pe", lambda e, kt=kt, bga=bga, gw=gw: e.matmul(
                        bankf(bga), lhsT=gw[:, kt, 0, :], rhs=hT[:, kt, gs], start=(kt == 0), stop=(kt == 15)),
                        r=[B_hT, bw], w=[banks[bga]])
                for kt in range(16):
                    P.op("pe", lambda e, kt=kt, bgb=bgb, gw=gw: e.matmul(
                        bankf(bgb), lhsT=gw[:, kt, 1, :], rhs=hT[:, kt, gs], start=(kt == 0), stop=(kt == 15)),
                        r=[B_hT, bw], w=[banks[bgb]])
                P.op("act", lambda e, bga=bga: e.activation(out=tha, in_=bankf(bga), func=AF.Tanh, scale=0.5),
                     r=[banks[bga]], w=[B_tha])
                P.op("act", lambda e, bgb=bgb: e.activation(out=thb, in_=bankf(bgb), func=AF.Tanh, scale=0.5),
                     r=[banks[bgb]], w=[B_thb])
                P.op("dve", lambda e, bya=bya: e.scalar_tensor_tensor(out=t1, in0=tha, scalar=1.0, in1=bankf(bya),
                                                                      op0=ALU.add, op1=ALU.mult),
                     r=[B_tha, banks[bya]], w=[B_t1])
                P.op("dve", lambda e, byb=byb: e.scalar_tensor_tensor(out=t2, in0=thb, scalar=1.0, in1=bankf(byb),
                                                                      op0=ALU.add, op1=ALU.mult),
                     r=[B_thb, banks[byb]], w=[B_t2])
                P.op("dve", lambda e, ct=ct: e.tensor_tensor(out=mT[:, ct, :], in0=t1, in1=t2, op=ALU.add),
                     r=[B_t1, B_t2], w=[B_mT])
                ict += 1
            for cb in range(8):
                if iwo + 1 < 32:
                    load_wo(iwo + 1)
                wo = wosl[iwo % 2]; bwo = B_wo[iwo % 2]
                for tt in range(4):
                    row0 = G * 512 + tt * 128
                    xi = ixr % 4
                    xb = xr[xi // 2][:, (xi % 2) * 256:(xi % 2) * 256 + 256]
                    bxb = B_xr[xi]
                    bk = (ixr % 8)
                    P.dma("act", xb, xo[row0:row0 + 128, cb * 256:(cb + 1) * 256], w=[bxb])
                    for kt in range(16):
                        P.op("pe", lambda e, kt=kt, tt=tt, bk=bk, wo=wo: e.matmul(
                            bankf(bk)[:, 0:256], lhsT=mT[:, kt, tt * 128:(tt + 1) * 128], rhs=wo[:, kt, :],
                            start=(kt == 0), stop=(kt == 15)), r=[B_mT, bwo], w=[banks[bk]])
                    P.op("dve", lambda e, bk=bk, xb=xb: e.scalar_tensor_tensor(
                        out=xb, in0=bankf(bk)[:, 0:256], scalar=0.5, in1=xb, op0=ALU.mult, op1=ALU.add),
                        r=[banks[bk], bxb], w=[bxb])
                    P.dma("sp", out_d[row0:row0 + 128, cb * 256:(cb + 1) * 256], xb, r=[bxb], dbuf=bxb)
                    ixr += 1
                iwo += 1
        fin = list(B_xr) + DBG_BUFS
        P.final_wait("sp", fin)

        block = st.enter_context(nc.Block())

        @block.tensor
        def _(e):
            P.emit(e, "pe")

        @block.scalar
        def _(e):
            P.emit(e, "act")

        @block.vector
        def _(e):
            P.emit(e, "dve")

        @block.gpsimd
        def _(e):
            P.emit(e, "pool")

        @block.sync
        def _(e):
            P.emit(e, "sp")
    return nc


def _host_layouts(x, norm_g, w_in, q_norm_g, k_norm_g, sgu_norm_g, w_spatial, b_spatial,
                  w_proj_a, w_proj_b, w_out):
    f32 = np.float32
    w_in = np.asarray(w_in, f32)[0]
    wi = w_in.reshape(16, 128, 11264)

    def fam(c0, n):
        blk = wi[:, :, c0:c0 + n].reshape(16, 128, n // 128, 128)
        return np.transpose(blk, (2, 1, 0, 3))
    q, k, v, za = fam(0, 1024), fam(1024, 1024), fam(2048, 1024), fam(3072, 1024)
    ub, vb, zb = fam(4096, 1024), fam(5120, 1024), fam(6144, 1024)
    ga, gb = fam(7168, 2048), fam(9216, 2048)
    wqkvz = np.ascontiguousarray(np.stack([q, k, v, za], axis=3)).reshape(8, 128, 8192)
    wkvpre = np.ascontiguousarray(np.transpose(np.stack([k, v], axis=3), (1, 0, 2, 3, 4))).reshape(128, 32768)
    wsgu = np.ascontiguousarray(np.stack([ub, vb, zb], axis=3)).reshape(8, 128, 6144)
    wgate = np.ascontiguousarray(np.stack([ga, gb], axis=3)).reshape(16, 128, 4096)
    pa = np.asarray(w_proj_a, f32)[0].reshape(8, 128, 16, 128)
    pb = np.asarray(w_proj_b, f32)[0].reshape(8, 128, 16, 128)
    wproj = np.ascontiguousarray(np.stack([np.transpose(pa, (2, 1, 0, 3)), np.transpose(pb, (2, 1, 0, 3))],
                                          axis=3)).reshape(16, 128, 2048)
    wph3 = np.ascontiguousarray(np.concatenate([wgate, wproj], axis=2))
    wo = np.asarray(w_out, f32)[0].reshape(16, 128, 8, 256)
    wout = np.ascontiguousarray(np.transpose(wo, (2, 1, 0, 3))).reshape(8, 128, 4096)
    vecs = np.zeros((128, 20), f32)
    vecs[:, 0:16] = np.asarray(norm_g, f32)[0].reshape(16, 128).T
    vecs[:, 16] = np.asarray(q_norm_g, f32)[0]
    vecs[:, 17] = np.asarray(k_norm_g, f32)[0]
    vecs[:, 18] = np.asarray(sgu_norm_g, f32)[0]
    bsp = np.ascontiguousarray(np.broadcast_to(np.asarray(b_spatial, f32)[0].reshape(1, 1024), (128, 1024)))
    wspT = np.ascontiguousarray(np.transpose(np.asarray(w_spatial, f32)[0], (2, 0, 1)))
    ident = np.eye(128, dtype=f32)
    tri = np.triu(np.ones((128, 128), f32))
    negtri = np.ascontiguousarray((1.0 - tri) * np.float32(-30000.0))
    shared = dict(wqkvz=wqkvz, wkvpre=wkvpre, wsgu=wsgu, wph3=wph3, wout=wout, vecs=vecs, bsp=bsp,
                  wspT=wspT, ident=ident, tri=tri, negtri=negtri)
    x = np.asarray(x, f32)
    in_maps = []
    for c in range(8):
        b, j = c // 2, c % 2
        gb_ = np.full((16, 16), -1e30, f32)
        for i in range(16):
            lo = 0 if j == 1 else 8
            gb_[i, lo:8 + i // 2] = 0.0
        m = dict(shared)
        m["xo"] = np.ascontiguousarray(x[b, j * 2048:(j + 1) * 2048])
        m["xp"] = np.ascontiguousarray(x[b, 0:2048])
        m["gbias"] = np.ascontiguousarray(np.broadcast_to(gb_.reshape(1, 256), (128, 256)))
        in_maps.append(m)
    return in_maps


_NC_CACHE = {}


def kernel(x, norm_g, w_in, q_norm_g, k_norm_g, sgu_norm_g, w_spatial, b_spatial, w_proj_a, w_proj_b, w_out,
           _debug=False):
    in_maps = _host_layouts(x, norm_g, w_in, q_norm_g, k_norm_g, sgu_norm_g, w_spatial, b_spatial,
                            w_proj_a, w_proj_b, w_out)
    nc = build_program(debug=_debug)
    res = run_bass_kernel_spmd(nc, in_maps, core_ids=list(range(8)))
    out = np.empty((4, 4096, 2048), np.float32)
    for c in range(8):
        b, j = c // 2, c % 2
        out[b, j * 2048:(j + 1) * 2048] = res.results[c]["out"]
    if _debug:
        return out, res
    return out
```

```python
import numpy as np
from contextlib import ExitStack
import concourse.bass as bass
import concourse.mybir as mybir
from concourse.bass_utils import run_bass_kernel_spmd

F32 = mybir.dt.float32
BF16 = mybir.dt.bfloat16
U8 = mybir.dt.uint8
ALU = mybir.AluOpType
AF = mybir.ActivationFunctionType
AX = mybir.AxisListType

D = 2048
NH = 8
DH = 128
S_OWN = 2048
NT = 16
EPS = 1e-6
C1 = 0.7978845608028654
C2 = 0.044715
SQ128 = float(np.sqrt(128.0))


class Buf:
    def __init__(self, name, inherit=()):
        self.name = name
        self.writers = {}
        self.readers = {}
        self.dsem = None
        self.dcount = 0
        for b in inherit:
            for d in (b.writers, b.readers):
                for k, v in d.items():
                    if self.readers.get(k, (None, 0))[1] < v[1]:
                        self.readers[k] = v


class _Rec:
    def __init__(self):
        self.call = None

    def __getattr__(self, name):
        def f(*args, **kwargs):
            self.call = (name, args, kwargs)
        return f


class Eng:
    def __init__(self, name, sem):
        self.name = name
        self.sem = sem
        self.n = 0
        self.ops = []
        self.seen = {}


class Prog:
    def __init__(self, nc, st):
        self.nc = nc
        self.st = st
        self.E = {}
        for n in ("pe", "act", "dve", "pool", "sp"):
            self.E[n] = Eng(n, st.enter_context(nc.semaphore("s_" + n)))
        self.nsem = 5

    def _waits(self, eng, r, w, is_dma=False):
        E = self.E[eng]
        need = {}

        def add(d, raw, skip_dma=False):
            for k, (sem, val) in d.items():
                if skip_dma and k.startswith("dma_"):
                    continue
                if k == eng:
                    if eng == "pe" or not raw:
                        continue
                if need.get(k, (None, 0))[1] < val:
                    need[k] = (sem, val)
        for b in r:
            add(b.writers, True)
        for b in w:
            add(b.readers, False)
            add(b.writers, False, skip_dma=is_dma)
        out = []
        for k, (sem, val) in need.items():
            if E.seen.get(k, 0) >= val:
                continue
            E.seen[k] = val
            out.append((sem, val))
        return out

    def _record(self, key, tok, r, w):
        for b in r:
            b.readers[key] = tok
        for b in w:
            if b.readers:
                b.readers = {}
                b.writers = {}
            b.writers[key] = tok

    def op(self, eng, fn, r=(), w=()):
        E = self.E[eng]
        waits = self._waits(eng, r, w)
        E.n += 1
        tok = (E.sem, E.n)
        rec = _Rec()
        fn(rec)
        name, args, kwargs = rec.call
        E.ops.append((waits, lambda e: getattr(e, name)(*args, **kwargs), E.sem, 1))
        self._record(eng, tok, r, w)

    def dma(self, eng, out, in_, r=(), w=(), dbuf=None):
        E = self.E[eng]
        b = dbuf if dbuf is not None else (w[0] if w else r[0])
        if b.dsem is None:
            b.dsem = self.st.enter_context(self.nc.semaphore("d_" + b.name))
            self.nsem += 1
        waits = self._waits(eng, r, w, is_dma=True)
        b.dcount += 16
        tok = (b.dsem, b.dcount)
        E.ops.append((waits, lambda e: e.dma_start(out=out, in_=in_), b.dsem, 16))
        self._record("dma_" + b.name, tok, r, w)

    def final_wait(self, eng, bufs):
        E = self.E[eng]
        for b in bufs:
            E.ops.append(([(b.dsem, b.dcount)], None, None, 0))

    def emit(self, e, name):
        for waits, fn, sem, inc in self.E[name].ops:
            for s, v in waits:
                e.wait_ge(s, v)
            if fn is None:
                continue
            ins = fn(e)
            ins.then_inc(sem, inc)


def build_program(debug=False):
    nc = bass.Bass("TRN2", target_bir_lowering=False)

    def din(name, shape):
        return nc.dram_tensor(name, list(shape), F32, kind="ExternalInput").ap()

    xo = din("xo", (S_OWN, D))
    xp = din("xp", (S_OWN, D))
    wqkvz = din("wqkvz", (NH, 128, 8192))
    wkvpre = din("wkvpre", (128, 32768))
    wsgu = din("wsgu", (8, 128, 6144))
    wph3 = din("wph3", (16, 128, 6144))
    wout = din("wout", (8, 128, 4096))
    wph3_b = nc.dram_tensor("wph3_b", [16, 128, 6144], BF16, kind="Internal").ap()
    wout_b = nc.dram_tensor("wout_b", [8, 128, 4096], BF16, kind="Internal").ap()
    vecs = din("vecs", (128, 20))
    gbias_d = din("gbias", (128, 256))
    bsp_d = din("bsp", (128, 1024))
    wspT_d = din("wspT", (128, 8, 128))
    ident_d = din("ident", (128, 128))
    tri_d = din("tri", (128, 128))
    negtri_d = din("negtri", (128, 128))
    out_d = nc.dram_tensor("out", [S_OWN, D], F32, kind="ExternalOutput").ap()
    dbg = {}
    if debug:
        for nm, shp in (("d_hT", (128, 16 * 2048)), ("d_AT", (128, 8 * 2048)), ("d_ST", (128, 8 * 2048)),
                        ("d_KTpre", (128, 8 * 2048)), ("d_QK", (128, 2 * 2048)), ("d_sel", (128, 256)),
                        ("d_gm", (128, 256)), ("d_V", (128, 16 * 129)), ("d_za", (128, 2048))):
            dbg[nm] = nc.dram_tensor(nm, list(shp), BF16 if nm not in ("d_sel", "d_gm") else F32,
                                     kind="ExternalOutput").ap()

    with ExitStack() as st:
        ARENA_BYTES = 212480
        arena = st.enter_context(nc.sbuf_tensor("arena", [128, ARENA_BYTES], U8))
        psum_all = st.enter_context(nc.psum_tensor("ps", [128, 4096], F32))
        P = Prog(nc, st)

        def carve(off, nbytes, dt):
            assert off % 32 == 0 and off + nbytes <= ARENA_BYTES, (off, nbytes)
            return arena[:, off:off + nbytes].bitcast(dt)

        OFF_A = 0
        OFF_B = 65536
        OFF_C = 98304
        OFF_D = 131584
        OFF_E = 164352
        OFF_F = 200192
        hT = carve(OFF_A, 65536, BF16).rearrange("p (k t) -> p k t", k=16)
        Wkv = carve(OFF_A, 65536, BF16).rearrange("p (h k c) -> p h k c", h=8, k=16)
        KTpre = carve(OFF_B, 32768, BF16).rearrange("p (h t) -> p h t", h=8)
        AT = KTpre
        Vpre = carve(OFF_C, 33024, BF16).rearrange("p (t h c) -> p t h c", t=16, h=8)
        ST = carve(OFF_C, 32768, BF16).rearrange("p (g t) -> p g t", g=8)

        o = OFF_F
        ident = carve(o, 256, BF16); o += 256
        tri = carve(o, 256, BF16); o += 256
        negtri = carve(o, 256, BF16); o += 256
        WT = carve(o, 2048, BF16).rearrange("p (g t) -> p g t", g=8); o += 2048
        Bsp = carve(o, 4096, F32).rearrange("p (g t) -> p g t", g=8); o += 4096
        gbias = carve(o, 1024, F32); o += 1024
        vec = carve(o, 96, F32); o += 96
        m05 = carve(o, 64, F32); o += 64
        stat = carve(o, 512, F32); o += 512
        km = carve(o, 64, F32); o += 64
        kmp = carve(o, 64, F32); o += 64
        kmo = carve(o, 64, F32); o += 64
        kmb = carve(o, 32, BF16); o += 32
        gm = carve(o, 1024, F32).rearrange("p (c b) -> p c b", c=16); o += 1024
        sel = carve(o, 1024, F32).rearrange("p (c b) -> p c b", c=16); o += 1024
        m8 = carve(o, 512, F32).rearrange("p (c b) -> p c b", c=16); o += 512
        thr = carve(o, 64, F32); o += 64
        assert o <= ARENA_BYTES, o

        DBG_BUFS = []

        def dbuf(n):
            b = Buf(n)
            DBG_BUFS.append(b)
            return b

        B_const = Buf("const")
        B_Aq = [Buf(f"A{i}") for i in range(4)]
        banks = [Buf(f"bank{i}") for i in range(8)]

        def bankf(i):
            return psum_all[:, i * 512:(i + 1) * 512]

        def bankb(i):
            return psum_all[:, i * 512:(i + 1) * 512].bitcast(BF16)

        B_constp = Buf("constp")
        P.dma("sp", vec[:, 0:20], vecs, w=[B_const])
        P.dma("sp", gbias, gbias_d, w=[B_const])
        P.dma("sp", Bsp.rearrange("p g t -> p (g t)"), bsp_d, w=[B_const])
        P.dma("pool", ident, ident_d, w=[B_constp])
        P.dma("pool", tri, tri_d, w=[B_constp])
        P.dma("pool", negtri, negtri_d, w=[B_constp])
        P.dma("pool", WT.rearrange("p g t -> p (g t)"), wspT_d.rearrange("p g t -> p (g t)"), w=[B_constp])
        B_c2 = Buf("const2")
        P.op("dve", lambda e: e.tensor_scalar(out=vec[:, 20:23], in0=vec[:, 16:19], scalar1=SQ128, scalar2=None,
                                              op0=ALU.mult), r=[B_const], w=[B_c2])
        P.op("pool", lambda e: e.memset(m05, -0.5), w=[B_c2])
        P.op("dve", lambda e: e.tensor_tensor(out=WT, in0=WT, in1=tri.unsqueeze(1).to_broadcast([128, 8, 128]),
                                              op=ALU.mult), r=[B_constp], w=[B_c2])
        CONST = [B_const, B_constp, B_c2]

        od = OFF_E
        xt = [carve(od + i * 8192, 8192, F32) for i in range(2)]; od += 16384
        xn = [carve(od + i * 4096, 4096, BF16) for i in range(2)]; od += 8192
        hTt = [carve(od + i * 4096, 4096, BF16).rearrange("p (k t) -> p k t", k=16) for i in range(2)]; od += 8192
        junk = carve(od, 256, BF16); od += 256
        assert od <= OFF_F
        kn = [carve(OFF_D + 28672 + i * 2048, 2048, BF16).rearrange("p (h c) -> p h c", h=8) for i in range(2)]
        B_xt = [Buf("xt0"), Buf("xt1")]
        B_xn = [Buf("xn0"), Buf("xn1")]; B_hTt = [Buf("hTt0"), Buf("hTt1")]; B_kn = [Buf("kn0"), Buf("kn1")]
        B_junk = Buf("junk")
        B_st0 = [Buf("st0a"), Buf("st0b")]; B_stk = [Buf(f"stk{i}") for i in range(8)]
        B_KTpre = [Buf(f"KTpre{h}") for h in range(8)]; B_Vpre = Buf("Vpre")

        P.op("pool", lambda e: e.memset(Vpre[:, :, :, 128:129], 1.0), w=[B_Vpre])
        for q4 in range(4):
            P.dma("pool", carve(OFF_A + q4 * 16384, 16384, BF16), wkvpre[:, q4 * 8192:(q4 + 1) * 8192], w=[B_Aq[q4]])
        B_hT = None

        def st_A(src_dram, t):
            xb = xt[t % 2]; bx = B_xt[t % 2]; xnb = xn[t % 2]; bxn = B_xn[t % 2]
            P.dma("sp", xb, src_dram[(t % 16) * 128:(t % 16 + 1) * 128, :], w=[bx])
            ss = stat[:, 2 * (t % 2):2 * (t % 2) + 1]; rs = stat[:, 2 * (t % 2) + 1:2 * (t % 2) + 2]
            bst = B_st0[t % 2]
            P.op("act", lambda e: e.activation(out=xnb, in_=xb, func=AF.Square, accum_out=ss), r=[bx], w=[bxn, bst])
            P.op("dve", lambda e: e.tensor_scalar(out=ss, in0=ss, scalar1=1.0 / D, scalar2=EPS, op0=ALU.mult,
                                                  op1=ALU.add), r=[bst], w=[bst])
            P.op("pool", lambda e: e.tensor_tensor(out=rs, in0=ss, in1=m05[:, 0:1], op=ALU.pow),
                 r=[bst] + CONST, w=[bst])
            P.op("dve", lambda e: e.tensor_scalar(out=xnb, in0=xb, scalar1=rs, scalar2=None, op0=ALU.mult),
                 r=[bx, bst], w=[bxn])

        def st_B(t, dst_ap, dst_buf):
            xnb = xn[t % 2]; bxn = B_xn[t % 2]
            for half in range(2):
                bk = half
                for j in range(8):
                    kt = half * 8 + j
                    P.op("pe", lambda e: e.transpose(out=bankb(bk)[:, j * 128:(j + 1) * 128],
                                                     in_=xnb[:, kt * 128:(kt + 1) * 128], identity=ident),
                         r=[bxn] + CONST, w=[banks[bk]])
                P.op("dve", lambda e: e.tensor_tensor(
                    out=dst_ap[:, half * 8:(half + 1) * 8, :], in0=bankb(bk).rearrange("p (k t) -> p k t", k=8),
                    in1=vec[:, half * 8:(half + 1) * 8].unsqueeze(2).to_broadcast([128, 8, 128]), op=ALU.mult),
                    r=[banks[bk]] + CONST, w=[dst_buf])

        def st_CD(t):
            hb = hTt[t % 2]; bhb = B_hTt[t % 2]; knb = kn[t % 2]; bkn = B_kn[t % 2]
            for b2 in range(4):
                bk = 2 + b2
                for hh in range(2):
                    h = 2 * b2 + hh
                    for kt in range(16):
                        P.op("pe", lambda e: e.matmul(bankf(bk)[:, hh * 256:hh * 256 + 256], lhsT=hb[:, kt, :],
                                                      rhs=Wkv[:, h, kt, :], start=(kt == 0), stop=(kt == 15)),
                             r=[bhb, B_Aq[b2]], w=[banks[bk]])
                so = 8 + ((t % 2) * 4 + b2) * 4
                ssk = stat[:, so:so + 2]; rsk = stat[:, so + 2:so + 4]; bst = B_stk[(t % 2) * 4 + b2]
                pv = bankf(bk).rearrange("p (h c) -> p h c", h=2)
                for hh in range(2):
                    P.op("act", lambda e: e.activation(out=junk, in_=pv[:, hh, 0:128], func=AF.Square,
                                                       accum_out=ssk[:, hh:hh + 1]), r=[banks[bk]], w=[B_junk, bst])
                P.op("dve", lambda e: e.tensor_scalar(out=ssk, in0=ssk, scalar1=128.0 * EPS, scalar2=None,
                                                      op0=ALU.add), r=[bst], w=[bst])
                P.op("pool", lambda e: e.tensor_tensor(out=rsk, in0=ssk, in1=m05[:, 0:2], op=ALU.pow),
                     r=[bst] + CONST, w=[bst])
                P.op("dve", lambda e: e.tensor_tensor(
                    out=knb[:, 2 * b2:2 * b2 + 2, :], in0=pv[:, :, 0:128],
                    in1=rsk.unsqueeze(2).to_broadcast([128, 2, 128]), op=ALU.mult), r=[banks[bk], bst], w=[bkn])
                P.op("act", lambda e: e.copy(out=Vpre[:, t, 2 * b2:2 * b2 + 2, 0:128], in_=pv[:, :, 128:256]),
                     r=[banks[bk]], w=[B_Vpre])

        def st_E(t):
            knb = kn[t % 2]; bkn = B_kn[t % 2]
            for h in range(8):
                P.op("pe", lambda e: e.transpose(out=bankb(6)[:, h * 128:(h + 1) * 128], in_=knb[:, h, :],
                                                 identity=ident), r=[bkn] + CONST, w=[banks[6]])
            P.op("dve", lambda e: e.tensor_scalar(
                out=KTpre[:, :, t * 128:(t + 1) * 128], in0=bankb(6).rearrange("p (h t) -> p h t", h=8),
                scalar1=vec[:, 21:22], scalar2=None, op0=ALU.mult), r=[banks[6]] + CONST, w=B_KTpre)

        st_A(xp, 0)
        st_B(0, hTt[0], B_hTt[0])
        st_A(xp, 1)
        for t in range(NT):
            if t + 1 < NT:
                st_B(t + 1, hTt[(t + 1) % 2], B_hTt[(t + 1) % 2])
            if t + 2 < NT:
                st_A(xp, t + 2)
            elif t + 2 == NT:
                st_A(xo, 16)
            st_CD(t)
            if t >= 1:
                st_E(t - 1)
        st_E(NT - 1)
        B_hT = Buf("hT", inherit=B_Aq)
        for t in range(16, 32):
            if t + 1 < 32:
                st_A(xo, t + 1)
            st_B(t, hT[:, :, (t - 16) * 128:(t - 15) * 128], B_hT)

        B_pre3 = [Buf(f"pre3_{i}") for i in range(16)]; B_preo = [Buf(f"preo_{i}") for i in range(8)]
        wslot = [carve(OFF_D + i * 16384, 16384, BF16).rearrange("p (k f c) -> p k f c", k=16, f=4) for i in range(2)]
        B_ws = [Buf("ws0"), Buf("ws1", inherit=B_kn)]
        oe = OFF_E
        B_ph0 = B_xt + B_xn + B_hTt + B_kn + [B_junk]
        QK = carve(oe, 8192, BF16).rearrange("p (f t) -> p f t", f=2); oe += 8192
        Vown = carve(oe, 4160, BF16)[:, 0:16 * 129].rearrange("p (t c) -> p t c", t=16); oe += 4160
        zaT = carve(oe, 4096, BF16); oe += 4096
        PT = [carve(oe + i * 2048, 2048, BF16).rearrange("p (j q) -> p j q", j=2) for i in range(3)]; oe += 6144
        acc = [carve(oe + i * 2080, 2064, F32).rearrange("p (c d) -> p c d", c=4) for i in range(2)]; oe += 4160
        qkn = [carve(oe + i * 512, 512, BF16).rearrange("p (f c) -> p f c", f=2) for i in range(3)]; oe += 1536
        attn = carve(oe, 1024, BF16).rearrange("p (c d) -> p c d", c=4); oe += 1024
        th = carve(oe, 2048, F32); oe += 2048
        rinv = carve(oe, 32, F32); oe += 32
        junk1 = carve(oe, 256, BF16); oe += 256
        assert oe <= OFF_E + 35840, oe
        B_QK = Buf("QK", inherit=B_ph0); B_Vown = Buf("Vown", inherit=B_ph0); B_zaT = Buf("zaT", inherit=B_ph0)
        B_PT = [Buf(f"PT{i}", inherit=B_ph0) for i in range(3)]
        B_acc = [[Buf(f"acc{i}_{c}", inherit=B_ph0) for c in range(4)] for i in range(2)]
        B_qkn = [Buf(f"qkn{i}", inherit=B_ph0) for i in range(3)]
        B_st2 = [Buf(f"st2_{i}") for i in range(3)]
        B_attn = Buf("attn", inherit=B_ph0); B_th = Buf("th", inherit=B_ph0); B_rinv = Buf("rinv", inherit=B_ph0)
        B_junk1 = Buf("junk1", inherit=B_ph0)
        B_gate = Buf("gate"); B_km = Buf("km"); B_km2 = Buf("km2"); B_kmb = Buf("kmb"); B_gm = Buf("gm"); B_m8 = Buf("m8"); B_thr = Buf("thr")
        B_AT = B_KTpre

        P.op("pool", lambda e: e.memset(Vown[:, :, 128:129], 1.0), w=[B_Vown])

        def load_head_w(h):
            s = h % 2
            P.dma("pool", carve(OFF_D + s * 16384, 16384, BF16), wqkvz[h], w=[B_ws[s]])

        load_head_w(0)
        pt_rr = 0
        for h in range(NH):
            if h + 1 < NH:
                load_head_w(h + 1)
            ws = wslot[h % 2]; bws = B_ws[h % 2]
            def st_C(t):
                bk = t % 3
                for kt in range(16):
                    P.op("pe", lambda e: e.matmul(
                        bankf(bk)[:, 0:384], lhsT=hT[:, kt, t * 128:(t + 1) * 128],
                        rhs=ws[:, kt, 0:3, :].rearrange("p f c -> p (f c)"), start=(kt == 0), stop=(kt == 15)),
                        r=[B_hT, bws], w=[banks[bk]])
                so = 72 + (t % 3) * 4
                ss2 = stat[:, so:so + 2]; rs2 = stat[:, so + 2:so + 4]; bst = B_st2[t % 3]
                for f in range(2):
                    P.op("act", lambda e: e.activation(out=junk1, in_=bankf(bk)[:, f * 128:(f + 1) * 128],
                                                       func=AF.Square, accum_out=ss2[:, f:f + 1]),
                         r=[banks[bk]], w=[B_junk1, bst])
                P.op("dve", lambda e: e.tensor_scalar(out=ss2, in0=ss2, scalar1=128.0 * EPS, scalar2=None,
                                                      op0=ALU.add), r=[bst], w=[bst])
                P.op("pool", lambda e: e.tensor_tensor(out=rs2, in0=ss2, in1=m05[:, 0:2], op=ALU.pow),
                     r=[bst] + CONST, w=[bst])
                qn = qkn[t % 3]; bqn = B_qkn[t % 3]
                P.op("dve", lambda e: e.tensor_tensor(
                    out=qn, in0=bankf(bk)[:, 0:256].rearrange("p (f c) -> p f c", f=2),
                    in1=rs2.unsqueeze(2).to_broadcast([128, 2, 128]), op=ALU.mult), r=[banks[bk], bst], w=[bqn])
                P.op("act", lambda e: e.copy(out=Vown[:, t, 0:128], in_=bankf(bk)[:, 256:384]),
                     r=[banks[bk]], w=[B_Vown])

            def st_Eq(t):
                qn = qkn[t % 3]; bqn = B_qkn[t % 3]
                for f in range(2):
                    P.op("pe", lambda e: e.transpose(out=bankb(3)[:, f * 128:(f + 1) * 128], in_=qn[:, f, :],
                                                     identity=ident), r=[bqn] + CONST, w=[banks[3]])
                P.op("dve", lambda e: e.tensor_tensor(
                    out=QK[:, :, t * 128:(t + 1) * 128], in0=bankb(3)[:, 0:256].rearrange("p (f c) -> p f c", f=2),
                    in1=vec[:, 20:22].unsqueeze(2).to_broadcast([128, 2, 128]), op=ALU.mult),
                    r=[banks[3]] + CONST, w=[B_QK])

            def st_za(G):
                bk = 4 + G % 2
                for kt in range(16):
                    P.op("pe", lambda e: e.matmul(bankf(bk), lhsT=ws[:, kt, 3, :], rhs=hT[:, kt, G * 512:(G + 1) * 512],
                                                  start=(kt == 0), stop=(kt == 15)), r=[B_hT, bws], w=[banks[bk]])
                P.op("act", lambda e: e.activation(out=th, in_=bankf(bk), func=AF.Tanh, scale=0.5),
                     r=[banks[bk]], w=[B_th])
                P.op("dve", lambda e: e.scalar_tensor_tensor(
                    out=zaT[:, G * 512:(G + 1) * 512], in0=th, scalar=1.0, in1=bankf(bk), op0=ALU.add, op1=ALU.mult),
                    r=[B_th, banks[bk]], w=[B_zaT])

            def km_own(g):
                P.op("dve", lambda e: e.tensor_reduce(
                    out=kmo[:, 4 * g:4 * g + 4], in_=QK[:, 1, g * 512:(g + 1) * 512].rearrange("p (b k) -> p b k", b=4),
                    axis=AX.X, op=ALU.add), r=[B_QK], w=[B_km])

            P.op("dve", lambda e: e.tensor_reduce(out=kmp, in_=KTpre[:, h, :].rearrange("p (b k) -> p b k", b=16),
                                                  axis=AX.X, op=ALU.add), r=[B_KTpre[h]], w=[B_km])
            for t in range(NT):
                st_C(t)
                if t >= 2:
                    st_Eq(t - 2)
                    if (t - 2) % 4 == 3:
                        km_own((t - 2) // 4)
                if t % 4 == 3:
                    st_za(t // 4)
            st_Eq(NT - 2)
            st_Eq(NT - 1)
            km_own(3)
            def ktile(j):
                return KTpre[:, h, j * 128:(j + 1) * 128] if j < 16 else QK[:, 1, (j - 16) * 128:(j - 15) * 128]

            def vtile(j):
                return Vpre[:, j, h, :] if j < 16 else Vown[:, j - 16, :]

            allsteps = []
            for m in range(4):
                steps = []
                for b in range(4 * m):
                    steps.append(dict(tiles=[(b, 0), (16 + b, 0)], diag=[],
                                      outs=[(c, [0, 1], ("sel", b), None) for c in range(4)]))
                for r in range(4):
                    blk = 4 * m + r
                    outs = [(r, [1], ("one",), None), (r, [0], ("flag",), "extra")]
                    outs += [(c, [0, 1], ("sel", blk), None) for c in range(r + 1, 4)]
                    steps.append(dict(tiles=[(blk, r), (16 + blk, r)], diag=[(1, r)], outs=outs))
                for k_, sp_ in enumerate(steps):
                    sp_["m"] = m
                    sp_["last"] = (k_ == len(steps) - 1)
                allsteps += steps
            for si, sp_ in enumerate(allsteps):
                sp_["pt"] = (pt_rr + si) % 3
                sp_["sb"] = 0 + 2 * (si % 2)
                sp_["ob"] = 4 + 2 * (si % 2)
            pt_rr += len(allsteps)

            def emit_qk(sp_):
                m = sp_["m"]
                pt = PT[sp_["pt"]]; bpt = B_PT[sp_["pt"]]
                sb = sp_["sb"]
                dg = dict(sp_["diag"])
                for idx, (j, c0) in enumerate(sp_["tiles"]):
                    bk = sb + idx
                    n = (4 - c0) * 128
                    P.op("pe", lambda e: e.matmul(
                        bankf(bk)[:, 0:n], lhsT=ktile(j), rhs=QK[:, 0, m * 512 + c0 * 128:(m + 1) * 512],
                        start=True, stop=(idx not in dg)), r=[B_KTpre[h], B_QK], w=[banks[bk]])
                    if idx in dg:
                        cc = dg[idx] - c0
                        P.op("pe", lambda e: e.matmul(bankf(bk)[:, cc * 128:(cc + 1) * 128], lhsT=ident, rhs=negtri,
                                                      start=False, stop=True), r=CONST, w=[banks[bk]])
                if sp_["tiles"][0][1] == sp_["tiles"][1][1]:
                    c0 = sp_["tiles"][0][1]
                    n = (4 - c0) * 128
                    P.op("act", lambda e: e.activation(
                        out=pt[:, :, c0 * 128:512],
                        in_=psum_all[:, sb * 512:(sb + 2) * 512].rearrange("p (j q) -> p j q", j=2)[:, :, 0:n],
                        func=AF.Exp, scale=float(DH ** -0.5)), r=[banks[sb], banks[sb + 1]], w=[bpt])
                else:
                    for idx, (j, c0) in enumerate(sp_["tiles"]):
                        bk = sb + idx
                        n = (4 - c0) * 128
                        P.op("act", lambda e: e.activation(
                            out=pt[:, idx, c0 * 128:512], in_=bankf(bk)[:, 0:n], func=AF.Exp,
                            scale=float(DH ** -0.5)), r=[banks[bk]], w=[bpt])

            seen = {}

            def emit_pv(sp_):
                m = sp_["m"]
                ac = acc[m % 2]
                pt = PT[sp_["pt"]]; bpt = B_PT[sp_["pt"]]

                def region(c, where):
                    if where == "extra":
                        return sp_["ob"] + 1, bankf(sp_["ob"] + 1)[:, 258:387]
                    bk = sp_["ob"] + c // 2
                    return bk, bankf(bk)[:, (c % 2) * 129:(c % 2) * 129 + 129]
                for (c, tl, sc_, where) in sp_["outs"]:
                    bk, po = region(c, where)
                    for ii, idx in enumerate(tl):
                        j = sp_["tiles"][idx][0]
                        P.op("pe", lambda e: e.matmul(po, lhsT=pt[:, idx, c * 128:(c + 1) * 128], rhs=vtile(j),
                                                      start=(ii == 0), stop=(ii == len(tl) - 1)),
                             r=[bpt, B_Vpre, B_Vown], w=[banks[bk]])
                for (c, tl, sc_, where) in sp_["outs"]:
                    bk, po = region(c, where)
                    bac = B_acc[m % 2][c]
                    if sc_[0] == "one":
                        sc = 1.0
                    elif sc_[0] == "flag":
                        sc = vec[:, 19:20]
                    else:
                        sc = sel[:, 4 * m + c, sc_[1]:sc_[1] + 1]
                    if (m, c) not in seen:
                        seen[(m, c)] = True
                        P.op("dve", lambda e: e.tensor_scalar(out=ac[:, c, :], in0=po, scalar1=sc, scalar2=None,
                                                              op0=ALU.mult), r=[banks[bk], B_gate] + CONST, w=[bac])
                    else:
                        P.op("dve", lambda e: e.scalar_tensor_tensor(
                            out=ac[:, c, :], in0=po, scalar=sc, in1=ac[:, c, :], op0=ALU.mult, op1=ALU.add),
                            r=[banks[bk], B_gate, bac] + CONST, w=[bac])

            def fin_dve(m):
                ac = acc[m % 2]; bac = B_acc[m % 2]
                P.op("dve", lambda e: e.tensor_scalar(out=rinv[:, 0:4], in0=ac[:, :, 128], scalar1=2.0, scalar2=None,
                                                      op0=ALU.mult), r=bac, w=[B_rinv])
                P.op("dve", lambda e: e.reciprocal(out=rinv[:, 4:8], in_=rinv[:, 0:4]), r=[B_rinv], w=[B_rinv])
                P.op("dve", lambda e: e.tensor_tensor(out=attn, in0=ac[:, :, 0:128],
                                                      in1=rinv[:, 4:8].unsqueeze(2).to_broadcast([128, 4, 128]),
                                                      op=ALU.mult), r=bac + [B_rinv], w=[B_attn])

            def fin_pe(m, bk):
                for c in range(4):
                    P.op("pe", lambda e: e.transpose(out=bankb(bk)[:, c * 128:(c + 1) * 128], in_=attn[:, c, :],
                                                     identity=ident), r=[B_attn] + CONST, w=[banks[bk]])
                P.op("dve", lambda e: e.tensor_tensor(out=zaT[:, m * 512:(m + 1) * 512], in0=bankb(bk)[:, 0:512],
                                                      in1=zaT[:, m * 512:(m + 1) * 512], op=ALU.mult),
                     r=[banks[bk], B_zaT], w=[B_zaT])

            P.op("dve", lambda e: e.tensor_tensor(out=km, in0=kmp, in1=kmo, op=ALU.add), r=[B_km], w=[B_km2])
            P.op("dve", lambda e: e.tensor_scalar(out=kmb, in0=km, scalar1=1.0 / 256.0, scalar2=None, op0=ALU.mult),
                 r=[B_km2], w=[B_kmb])
            emit_qk(allsteps[0])
            emit_qk(allsteps[1])
            for c in range(16):
                P.op("pe", lambda e: e.matmul(bankf(6)[:, c * 16:(c + 1) * 16], lhsT=QK[:, 0, c * 128:(c + 1) * 128],
                                              rhs=kmb, start=True, stop=True), r=[B_QK, B_kmb], w=[banks[6]])
            P.op("dve", lambda e: e.tensor_tensor(out=gm.rearrange("p c b -> p (c b)"), in0=bankf(6)[:, 0:256],
                                                  in1=gbias, op=ALU.add), r=[banks[6]] + CONST, w=[B_gm])
            for c in range(16):
                P.op("dve", lambda e: e.max(out=m8[:, c, :], in_=gm[:, c, :]), r=[B_gm], w=[B_m8])
            P.op("dve", lambda e: e.tensor_scalar(out=thr, in0=m8[:, :, 2], scalar1=-1e29, scalar2=None, op0=ALU.max),
                 r=[B_m8], w=[B_thr])
            P.op("dve", lambda e: e.tensor_tensor(out=sel, in0=gm, in1=thr.unsqueeze(2).to_broadcast([128, 16, 16]),
                                                  op=ALU.is_ge), r=[B_gm, B_thr], w=[B_gate])
            if debug and h == 0:
                P.dma("sp", dbg["d_QK"], QK.rearrange("p f t -> p (f t)"), r=[B_QK], dbuf=dbuf("dq"))
                P.dma("sp", dbg["d_sel"], sel.rearrange("p c b -> p (c b)"), r=[B_gate], dbuf=dbuf("ds"))
                P.dma("sp", dbg["d_gm"], gm.rearrange("p c b -> p (c b)"), r=[B_gate], dbuf=dbuf("dg"))
                P.dma("sp", dbg["d_V"], Vown.rearrange("p t c -> p (t c)"), r=[B_Vown], dbuf=dbuf("dv"))

            pend = None
            for si, sp_ in enumerate(allsteps):
                if si + 2 < len(allsteps):
                    emit_qk(allsteps[si + 2])
                emit_pv(sp_)
                if pend is not None:
                    fin_pe(pend, 4 + 2 * ((si + 1) % 2))
                    pend = None
                if sp_["last"]:
                    fin_dve(sp_["m"])
                    pend = sp_["m"]
            fin_pe(pend, 4)
            P.op("pool", lambda e, h=h: e.tensor_copy(out=AT[:, h, :], in_=zaT), r=[B_zaT], w=[B_KTpre[h]])

        if debug:
            P.dma("sp", dbg["d_hT"], hT.rearrange("p k t -> p (k t)"), r=[B_hT], dbuf=dbuf("dh"))
            P.dma("sp", dbg["d_AT"], AT.rearrange("p h t -> p (h t)"), r=B_AT, dbuf=dbuf("da"))

        B_ph1 = [B_QK, B_Vown, B_zaT] + B_PT + B_acc[0] + B_acc[1] + B_qkn + [B_attn, B_th, B_rinv, B_junk1]
        wsl2 = [carve(OFF_D + i * 16384, 12288, BF16).rearrange("p (k f c) -> p k f c", k=16, f=3) for i in range(2)]
        B_ws2 = [Buf("ws2_0", inherit=[B_ws[0]]), Buf("ws2_1", inherit=[B_ws[1]])]
        oe = OFF_E
        a1 = [carve(oe + i * 2048, 2048, F32) for i in range(2)]; oe += 4096
        a2 = [carve(oe + i * 2048, 2048, F32) for i in range(2)]; oe += 4096
        a3 = carve(oe, 2048, F32); oe += 2048
        ug2 = carve(oe, 1024, BF16); oe += 1024
        zs2 = carve(oe, 1024, BF16); oe += 1024
        uz = [carve(oe + i * 1024, 1024, BF16) for i in range(2)]; oe += 2048
        vg2 = carve(oe, 2048, F32); oe += 2048
        vn = [carve(oe + i * 1024, 1024, BF16) for i in range(2)]; oe += 2048
        s1 = carve(oe, 2048, F32); oe += 2048
        junk2 = carve(oe, 256, BF16); oe += 256
        assert oe <= OFF_E + 35840
        B_a1 = [Buf(f"a1_{i}", inherit=B_ph1) for i in range(2)]
        B_a2 = [Buf(f"a2_{i}", inherit=B_ph1) for i in range(2)]
        B_a3 = Buf("a3", inherit=B_ph1); B_ug2 = Buf("ug2", inherit=B_ph1); B_zs2 = Buf("zs2", inherit=B_ph1)
        B_uz = [Buf(f"uz{i}", inherit=B_ph1) for i in range(2)]; B_vg2 = Buf("vg2", inherit=B_ph1)
        B_vn = [Buf(f"vn{i}", inherit=B_ph1) for i in range(2)]
        B_s1 = Buf("s1", inherit=B_ph1); B_junk2 = Buf("junk2", inherit=B_ph1)
        B_ST = Buf("ST", inherit=[B_Vpre])
        B_stv = [Buf("stva"), Buf("stvb")]

        def gelu2(bk, i, dst, dst_buf):
            P.op("act", lambda e: e.activation(out=a1[i], in_=bankf(bk), func=AF.Square, scale=float(np.sqrt(C2))),
                 r=[banks[bk]], w=[B_a1[i]])
            P.op("dve", lambda e: e.scalar_tensor_tensor(out=a2[i], in0=a1[i], scalar=1.0, in1=bankf(bk),
                                                         op0=ALU.add, op1=ALU.mult),
                 r=[B_a1[i], banks[bk]], w=[B_a2[i]])
            P.op("act", lambda e: e.activation(out=a1[i], in_=a2[i], func=AF.Tanh, scale=C1),
                 r=[B_a2[i]], w=[B_a1[i]])
            P.op("dve", lambda e: e.scalar_tensor_tensor(out=dst, in0=a1[i], scalar=1.0, in1=bankf(bk),
                                                         op0=ALU.add, op1=ALU.mult),
                 r=[B_a1[i], banks[bk]], w=[dst_buf])

        def load_sgu_w(g):
            P.dma("pool", carve(OFF_D + (g % 2) * 16384, 12288, BF16), wsgu[g], w=[B_ws2[g % 2]])

        def st2_C(g, G):
            ws = wsl2[g % 2]; bws = B_ws2[g % 2]
            par = G % 2
            bu, bz, bv = 0 + par, 2 + par, 4 + par
            for kt in range(16):
                P.op("pe", lambda e: e.matmul(bankf(bu), lhsT=ws[:, kt, 0, :], rhs=hT[:, kt, G * 512:(G + 1) * 512],
                                              start=(kt == 0), stop=(kt == 15)), r=[B_hT, bws], w=[banks[bu]])
            for kt in range(16):
                P.op("pe", lambda e: e.matmul(bankf(bz), lhsT=ws[:, kt, 2, :], rhs=hT[:, kt, G * 512:(G + 1) * 512],
                                              start=(kt == 0), stop=(kt == 15)), r=[B_hT, bws], w=[banks[bz]])
            for tt in range(4):
                t = G * 4 + tt
                for kt in range(16):
                    P.op("pe", lambda e: e.matmul(bankf(bv)[:, tt * 128:(tt + 1) * 128],
                                                  lhsT=hT[:, kt, t * 128:(t + 1) * 128], rhs=ws[:, kt, 1, :],
                                                  start=(kt == 0), stop=(kt == 15)), r=[B_hT, bws], w=[banks[bv]])

        def st2_D(g, G):
            par = G % 2
            bu, bz, bv = 0 + par, 2 + par, 4 + par
            gelu2(bu, 0, ug2, B_ug2)
            P.op("act", lambda e: e.activation(out=a3, in_=bankf(bz), func=AF.Tanh, scale=0.5),
                 r=[banks[bz]], w=[B_a3])
            P.op("dve", lambda e: e.scalar_tensor_tensor(out=zs2, in0=a3, scalar=1.0, in1=bankf(bz),
                                                         op0=ALU.add, op1=ALU.mult), r=[B_a3, banks[bz]], w=[B_zs2])
            P.op("dve", lambda e: e.scalar_tensor_tensor(out=uz[par], in0=ug2, scalar=0.25, in1=zs2,
                                                         op0=ALU.mult, op1=ALU.mult), r=[B_ug2, B_zs2], w=[B_uz[par]])
            gelu2(bv, 1, vg2, B_vg2)
            ssv = stat[:, 48 + 8 * par:52 + 8 * par]; rsv = stat[:, 52 + 8 * par:56 + 8 * par]
            bst = B_stv[par]
            for tt in range(4):
                P.op("act", lambda e: e.activation(out=junk2, in_=vg2[:, tt * 128:(tt + 1) * 128],
                                                   func=AF.Square, accum_out=ssv[:, tt:tt + 1]),
                     r=[B_vg2], w=[B_junk2, bst])
            P.op("dve", lambda e: e.tensor_scalar(out=ssv, in0=ssv, scalar1=512.0 * EPS, scalar2=None, op0=ALU.add),
                 r=[bst], w=[bst])
            P.op("pool", lambda e: e.tensor_tensor(out=rsv, in0=ssv, in1=m05[:, 0:4], op=ALU.pow),
                 r=[bst] + CONST, w=[bst])
            P.op("dve", lambda e: e.tensor_tensor(out=vn[par].rearrange("p (t c) -> p t c", t=4),
                                                  in0=vg2.rearrange("p (t c) -> p t c", t=4),
                                                  in1=rsv.unsqueeze(2).to_broadcast([128, 4, 128]), op=ALU.mult),
                 r=[B_vg2, bst], w=[B_vn[par]])

        def st2_M(g, G):
            par = G % 2
            bm = 6 + par
            for tt in range(4):
                P.op("pe", lambda e: e.matmul(bankf(bm)[:, tt * 128:(tt + 1) * 128],
                                              lhsT=vn[par][:, tt * 128:(tt + 1) * 128], rhs=WT[:, g, :],
                                              start=True, stop=True), r=[B_vn[par]] + CONST, w=[banks[bm]])
            P.op("dve", lambda e: e.scalar_tensor_tensor(
                out=s1.rearrange("p (t c) -> p t c", t=4), in0=bankf(bm).rearrange("p (t c) -> p t c", t=4),
                scalar=vec[:, 22:23], in1=Bsp[:, g, :].unsqueeze(1).to_broadcast([128, 4, 128]),
                op0=ALU.mult, op1=ALU.add), r=[banks[bm]] + CONST, w=[B_s1])
            P.op("dve", lambda e: e.tensor_tensor(out=ST[:, g, G * 512:(G + 1) * 512], in0=s1, in1=uz[par],
                                                  op=ALU.mult), r=[B_s1, B_uz[par]], w=[B_ST])

        load_sgu_w(0)
        items = [(g, G) for g in range(8) for G in range(4)]
        for i, (g, G) in enumerate(items):
            if G == 0 and g + 1 < 8:
                load_sgu_w(g + 1)
            st2_C(g, G)
            st2_D(g, G)
            if i >= 1:
                st2_M(*items[i - 1])
        st2_M(*items[-1])
        if debug:
            P.dma("sp", dbg["d_ST"], ST.rearrange("p g t -> p (g t)"), r=[B_ST], dbuf=dbuf("dst"))

        B_ph2 = B_a1 + B_a2 + [B_a3, B_ug2, B_zs2, B_vg2, B_s1, B_junk2] + B_uz + B_vn
        od = OFF_D
        gsl = []
        psl = []
        sl3 = []
        for i in range(2):
            sl3.append(carve(od, 12288, BF16))
            gsl.append(carve(od, 8192, BF16).rearrange("p (k f c) -> p k f c", k=16, f=2)); od += 8192
            psl.append(carve(od, 4096, BF16).rearrange("p (k f c) -> p k f c", k=8, f=2)); od += 4096
        wosl = []
        for i in range(2):
            wosl.append(carve(od, 8192, BF16).rearrange("p (k c) -> p k c", k=16)); od += 8192
        mT = carve(od, 16384, BF16).rearrange("p (k t) -> p k t", k=16); od += 16384
        xr = []
        for i in range(2):
            xr.append(carve(od, 2048, F32)); od += 2048
        tha = carve(od, 2048, F32); od += 2048
        thb = carve(od, 2048, F32); od += 2048
        t1 = carve(od, 1024, BF16); od += 1024
        t2 = carve(od, 1024, BF16); od += 1024
        assert od <= OFF_F, od
        inh = B_ph2 + B_ws2 + B_ph1 + B_ws + B_ph0
        B_gsl = [Buf(f"gsl{i}", inherit=inh) for i in range(2)]
        B_wo = [Buf(f"wo{i}", inherit=inh) for i in range(2)]
        B_mT = Buf("mT", inherit=inh)
        B_xr = [Buf(f"xr{i}", inherit=inh) for i in range(4)]
        B_tha = Buf("tha", inherit=inh); B_thb = Buf("thb", inherit=inh)
        B_t1 = Buf("t1", inherit=inh); B_t2 = Buf("t2", inherit=inh)

        def load_ct(i):
            ct = i % 16
            if i < 16:
                P.dma("pool", sl3[i % 2], wph3[ct], w=[B_gsl[i % 2]])
                P.dma("sp", wph3_b[ct], sl3[i % 2], r=[B_gsl[i % 2]], w=[B_pre3[ct]], dbuf=B_pre3[ct])
            else:
                P.dma("pool", sl3[i % 2], wph3_b[ct], r=[B_pre3[ct]], w=[B_gsl[i % 2]])

        def load_wo(i):
            wflat = wosl[i % 2].rearrange("p k c -> p (k c)")
            if i < 8:
                P.dma("pool", wflat, wout[i], w=[B_wo[i % 2]])
                P.dma("sp", wout_b[i], wflat, r=[B_wo[i % 2]], w=[B_preo[i]], dbuf=B_preo[i])
            else:
                P.dma("pool", wflat, wout_b[i % 8], r=[B_preo[i % 8]], w=[B_wo[i % 2]])

        load_ct(0)
        load_wo(0)
        ict = 0
        iwo = 0
        ixr = 0
        for G in range(4):
            gs = slice(G * 512, (G + 1) * 512)
            for ct in range(16):
                if ict + 1 < 64:
                    load_ct(ict + 1)
                gw = gsl[ict % 2]; pw = psl[ict % 2]; bw = B_gsl[ict % 2]
                par = ct % 2
                bya, byb, bga, bgb = 4 * par, 4 * par + 1, 4 * par + 2, 4 * par + 3
                for kt in range(8):
                    P.op("pe", lambda e, kt=kt, bya=bya, pw=pw: e.matmul(
                        bankf(bya), lhsT=pw[:, kt, 0, :], rhs=AT[:, kt, gs], start=(kt == 0), stop=(kt == 7)),
                        r=B_AT + [bw], w=[banks[bya]])
                for kt in range(8):
                    P.op("pe", lambda e, kt=kt, byb=byb, pw=pw: e.matmul(
                        bankf(byb), lhsT=pw[:, kt, 1, :], rhs=ST[:, kt, gs], start=(kt == 0), stop=(kt == 7)),
                        r=[B_ST, bw], w=[banks[byb]])
                for kt in range(16):
                    P.op("pe", lambda e, kt=kt, bga=bga, gw=gw: e.matmul(
                        bankf(bga), lhsT=gw[:, kt, 0, :], rhs=hT[:, kt, gs], start=(kt == 0), stop=(kt == 15)),
                        r=[B_hT, bw], w=[banks[bga]])
                for kt in range(16):
                    P.op("pe", lambda e, kt=kt, bgb=bgb, gw=gw: e.matmul(
                        bankf(bgb), lhsT=gw[:, kt, 1, :], rhs=hT[:, kt, gs], start=(kt == 0), stop=(kt == 15)),
                        r=[B_hT, bw], w=[banks[bgb]])
                P.op("act", lambda e, bga=bga: e.activation(out=tha, in_=bankf(bga), func=AF.Tanh, scale=0.5),
                     r=[banks[bga]], w=[B_tha])
                P.op("act", lambda e, bgb=bgb: e.activation(out=thb, in_=bankf(bgb), func=AF.Tanh, scale=0.5),
                     r=[banks[bgb]], w=[B_thb])
                P.op("dve", lambda e, bya=bya: e.scalar_tensor_tensor(out=t1, in0=tha, scalar=1.0, in1=bankf(bya),
                                                                      op0=ALU.add, op1=ALU.mult),
                     r=[B_tha, banks[bya]], w=[B_t1])
                P.op("dve", lambda e, byb=byb: e.scalar_tensor_tensor(out=t2, in0=thb, scalar=1.0, in1=bankf(byb),
                                                                      op0=ALU.add, op1=ALU.mult),
                     r=[B_thb, banks[byb]], w=[B_t2])
                P.op("dve", lambda e, ct=ct: e.tensor_tensor(out=mT[:, ct, :], in0=t1, in1=t2, op=ALU.add),
                     r=[B_t1, B_t2], w=[B_mT])
                ict += 1
            for cb in range(8):
                if iwo + 1 < 32:
                    load_wo(iwo + 1)
                wo = wosl[iwo % 2]; bwo = B_wo[iwo % 2]
                for tt in range(4):
                    row0 = G * 512 + tt * 128
                    xi = ixr % 4
                    xb = xr[xi // 2][:, (xi % 2) * 256:(xi % 2) * 256 + 256]
                    bxb = B_xr[xi]
                    bk = (ixr % 8)
                    P.dma("act", xb, xo[row0:row0 + 128, cb * 256:(cb + 1) * 256], w=[bxb])
                    for kt in range(16):
                        P.op("pe", lambda e, kt=kt, tt=tt, bk=bk, wo=wo: e.matmul(
                            bankf(bk)[:, 0:256], lhsT=mT[:, kt, tt * 128:(tt + 1) * 128], rhs=wo[:, kt, :],
                            start=(kt == 0), stop=(kt == 15)), r=[B_mT, bwo], w=[banks[bk]])
                    P.op("dve", lambda e, bk=bk, xb=xb: e.scalar_tensor_tensor(
                        out=xb, in0=bankf(bk)[:, 0:256], scalar=0.5, in1=xb, op0=ALU.mult, op1=ALU.add),
                        r=[banks[bk], bxb], w=[bxb])
                    P.dma("sp", out_d[row0:row0 + 128, cb * 256:(cb + 1) * 256], xb, r=[bxb], dbuf=bxb)
                    ixr += 1
                iwo += 1
        fin = list(B_xr) + DBG_BUFS
        P.final_wait("sp", fin)

        block = st.enter_context(nc.Block())

        @block.tensor
        def _(e):
            P.emit(e, "pe")

        @block.scalar
        def _(e):
            P.emit(e, "act")

        @block.vector
        def _(e):
            P.emit(e, "dve")

        @block.gpsimd
        def _(e):
            P.emit(e, "pool")

        @block.sync
        def _(e):
            P.emit(e, "sp")
    return nc


def _host_layouts(x, norm_g, w_in, q_norm_g, k_norm_g, sgu_norm_g, w_spatial, b_spatial,
                  w_proj_a, w_proj_b, w_out):
    f32 = np.float32
    w_in = np.asarray(w_in, f32)[0]
    wi = w_in.reshape(16, 128, 11264)

    def fam(c0, n):
        blk = wi[:, :, c0:c0 + n].reshape(16, 128, n // 128, 128)
        return np.transpose(blk, (2, 1, 0, 3))
    q, k, v, za = fam(0, 1024), fam(1024, 1024), fam(2048, 1024), fam(3072, 1024)
    ub, vb, zb = fam(4096, 1024), fam(5120, 1024), fam(6144, 1024)
    ga, gb = fam(7168, 2048), fam(9216, 2048)
    wqkvz = np.ascontiguousarray(np.stack([q, k, v, za], axis=3)).reshape(8, 128, 8192)
    wkvpre = np.ascontiguousarray(np.transpose(np.stack([k, v], axis=3), (1, 0, 2, 3, 4))).reshape(128, 32768)
    wsgu = np.ascontiguousarray(np.stack([ub, vb, zb], axis=3)).reshape(8, 128, 6144)
    wgate = np.ascontiguousarray(np.stack([ga, gb], axis=3)).reshape(16, 128, 4096)
    pa = np.asarray(w_proj_a, f32)[0].reshape(8, 128, 16, 128)
    pb = np.asarray(w_proj_b, f32)[0].reshape(8, 128, 16, 128)
    wproj = np.ascontiguousarray(np.stack([np.transpose(pa, (2, 1, 0, 3)), np.transpose(pb, (2, 1, 0, 3))],
                                          axis=3)).reshape(16, 128, 2048)
    wph3 = np.ascontiguousarray(np.concatenate([wgate, wproj], axis=2))
    wo = np.asarray(w_out, f32)[0].reshape(16, 128, 8, 256)
    wout = np.ascontiguousarray(np.transpose(wo, (2, 1, 0, 3))).reshape(8, 128, 4096)
    vecs = np.zeros((128, 20), f32)
    vecs[:, 0:16] = np.asarray(norm_g, f32)[0].reshape(16, 128).T
    vecs[:, 16] = np.asarray(q_norm_g, f32)[0]
    vecs[:, 17] = np.asarray(k_norm_g, f32)[0]
    vecs[:, 18] = np.asarray(sgu_norm_g, f32)[0]
    bsp = np.ascontiguousarray(np.broadcast_to(np.asarray(b_spatial, f32)[0].reshape(1, 1024), (128, 1024)))
    wspT = np.ascontiguousarray(np.transpose(np.asarray(w_spatial, f32)[0], (2, 0, 1)))
    ident = np.eye(128, dtype=f32)
    tri = np.triu(np.ones((128, 128), f32))
    negtri = np.ascontiguousarray((1.0 - tri) * np.float32(-30000.0))
    shared = dict(wqkvz=wqkvz, wkvpre=wkvpre, wsgu=wsgu, wph3=wph3, wout=wout, vecs=vecs, bsp=bsp,
                  wspT=wspT, ident=ident, tri=tri, negtri=negtri)
    x = np.asarray(x, f32)
    in_maps = []
    gb_ = np.full((16, 16), -1e30, f32)
    for i in range(16):
        gb_[i, 0:i] = 0.0
    gbias = np.ascontiguousarray(np.broadcast_to(gb_.reshape(1, 256), (128, 256)))
    for c in range(8):
        b, j = c // 2, c % 2
        xz = x[b].reshape(16, 2, 128, 2048)
        m = dict(shared)
        m["xo"] = np.ascontiguousarray(xz[:, j]).reshape(2048, 2048)
        m["xp"] = np.ascontiguousarray(xz[:, 1 - j]).reshape(2048, 2048)
        v = vecs.copy()
        v[:, 19] = float(j)
        m["vecs"] = v
        m["gbias"] = gbias
        in_maps.append(m)
    return in_maps


_NC_CACHE = {}


def kernel(x, norm_g, w_in, q_norm_g, k_norm_g, sgu_norm_g, w_spatial, b_spatial, w_proj_a, w_proj_b, w_out,
           _debug=False):
    in_maps = _host_layouts(x, norm_g, w_in, q_norm_g, k_norm_g, sgu_norm_g, w_spatial, b_spatial,
                            w_proj_a, w_proj_b, w_out)
    nc = build_program(debug=_debug)
    res = run_bass_kernel_spmd(nc, in_maps, core_ids=list(range(8)))
    out = np.empty((4, 4096, 2048), np.float32)
    for c in range(8):
        b, j = c // 2, c % 2
        out[b].reshape(16, 2, 128, 2048)[:, j] = res.results[c]["out"].reshape(16, 128, 2048)
    if _debug:
        return out, res
    return out
```
